# Optimizing a Trainium2 kernel written in Bass

```python
import math
import jax, jax.numpy as jnp
from jax import lax
import numpy as np

D_MODEL = 2048
BATCH = 4
SEQ = 8192
DEPTH = 4

GRID_W = 64
CTX_LEN = 256
N_MLA_HEADS = 8
Q_RANK = 512
KV_RANK = 256
QK_NOPE = 128
QK_ROPE = 64
V_HEAD = 128
QK_HEAD = QK_NOPE + QK_ROPE
ATTN_SCALE = QK_HEAD ** -0.5
MLA_IN = Q_RANK + KV_RANK + QK_ROPE
MLA_OUT = N_MLA_HEADS * V_HEAD
Q_BLOCK = 128
ROPE_AXIS = QK_ROPE // 2
ROPE_FREQS = ROPE_AXIS // 2
ROPE_THETA = 10000.0
POOL_WIDTH = D_MODEL // 2
POOL_WINDOWS = (2, 4, 8, 16)
POOL_GROUPS = len(POOL_WINDOWS)
POOL_GROUP = POOL_WIDTH // POOL_GROUPS
IN_WIDTH = MLA_IN + POOL_WIDTH
EVEN_MIX_WIDTH = MLA_OUT + POOL_WIDTH
FOURIER_GROUPS = 4
FOURIER_GROUP = D_MODEL // FOURIER_GROUPS
D_FF = 4 * D_MODEL
EPS = 1e-6
N_EVEN = (DEPTH + 1) // 2
N_ODD = DEPTH // 2

kernel_name = 'hybrid_mla_pool_fourier_dit'


def rmsnorm(x, g):
    xf = x.astype(jnp.float32)
    y = xf * lax.rsqrt(jnp.mean(xf * xf, axis=-1, keepdims=True) + EPS)
    return (y * g.astype(jnp.float32)).astype(x.dtype)


def modulate(h, shift, scale):
    return h * (1 + scale) + shift


def axial_rope_tables(n):
    rows = n // GRID_W
    r = jnp.broadcast_to(jnp.arange(rows, dtype=jnp.float32)[:, None], (rows, GRID_W)).reshape(n)
    col = jnp.broadcast_to(jnp.arange(GRID_W, dtype=jnp.float32)[None, :], (rows, GRID_W)).reshape(n)
    inv = ROPE_THETA ** (-2.0 * jnp.arange(ROPE_FREQS, dtype=jnp.float32) / ROPE_AXIS)
    ang = jnp.stack([r[:, None] * inv, col[:, None] * inv], axis=1)
    ang = jnp.broadcast_to(ang[:, :, None, :], (n, 2, 2, ROPE_FREQS)).reshape(n, QK_ROPE)
    return jnp.cos(ang), jnp.sin(ang)


def rotate_half_axial(x):
    xr = x.reshape(x.shape[:-1] + (2, 2, ROPE_FREQS))
    return jnp.concatenate([-xr[..., 1:, :], xr[..., :1, :]], axis=-2).reshape(x.shape)


def apply_rope(x, cos, sin):
    return x * cos + rotate_half_axial(x) * sin


def mla_queries(p, q_norm, w_uq):
    b, n, _ = p.shape
    q = (rmsnorm(p[..., :Q_RANK], q_norm) @ w_uq).reshape(b, n, N_MLA_HEADS, QK_HEAD)
    return q[..., :QK_NOPE], q[..., QK_NOPE:]


def mla_keys(p, kv_norm, w_ukv):
    b, n, _ = p.shape
    kv_lat = p[..., Q_RANK:Q_RANK + KV_RANK]
    k_rope = p[..., Q_RANK + KV_RANK:MLA_IN]
    kv = (rmsnorm(kv_lat, kv_norm) @ w_ukv).reshape(b, n, N_MLA_HEADS, QK_NOPE + V_HEAD)
    return kv[..., :QK_NOPE], k_rope, kv[..., QK_NOPE:]


def attend(qn, qr, kn, kr, v):
    s = jnp.einsum('bqhd,bkhd->bhqk', qn, kn) + jnp.einsum('bqhr,bkr->bhqk', qr, kr)
    p = jax.nn.softmax(s.astype(jnp.float32) * ATTN_SCALE, axis=-1).astype(v.dtype)
    return jnp.einsum('bhqk,bkhd->bqhd', p, v)


def latent_attention(qn, qr, kn, kr, v):
    b, n, h, _ = qn.shape
    nb = n // Q_BLOCK

    def to_blocks(t):
        return jnp.moveaxis(t.reshape((b, nb, Q_BLOCK) + t.shape[2:]), 1, 0)

    out = lax.map(lambda qs: attend(qs[0], qs[1], kn, kr, v), (to_blocks(qn), to_blocks(qr)))
    return jnp.moveaxis(out, 0, 1).reshape(b, n, h * V_HEAD)


def multiscale_pool(u, w_pool, pool_scale):
    b, n, _ = u.shape
    ug = u.reshape(b, n, POOL_GROUPS, POOL_GROUP).astype(jnp.float32)
    cs = jnp.concatenate([jnp.zeros((b, 1, POOL_GROUPS, POOL_GROUP), jnp.float32),
                          jnp.cumsum(ug, axis=1)], axis=1)
    t = jnp.arange(n)
    means = []
    for gi, w in enumerate(POOL_WINDOWS):
        lo = jnp.clip(t - w // 2, 0, n)
        hi = jnp.clip(t + w // 2, 0, n)
        cg = cs[:, :, gi]
        means.append((cg[:, hi] - cg[:, lo]) / (hi - lo).astype(jnp.float32)[None, :, None])
    pooled = (jnp.stack(means, axis=2) - ug).astype(u.dtype)
    y = jnp.einsum('bngc,gcd->bngd', pooled, w_pool)
    return y.reshape(b, n, POOL_WIDTH) * pool_scale


def even_mixer(hx, hc, w_in, q_norm, w_uq, kv_norm, w_ukv, w_pool, pool_scale, w_out, cos, sin, ctx_out):
    px = hx @ w_in
    pc = hc @ w_in
    qn, qr = mla_queries(px, q_norm, w_uq)
    kn, kr, v = mla_keys(px, kv_norm, w_ukv)
    ckn, ckr, cv = mla_keys(pc, kv_norm, w_ukv)
    qr = apply_rope(qr, cos[:, None, :], sin[:, None, :])
    kr = apply_rope(kr, cos, sin)
    kn_all = jnp.concatenate([ckn, kn], axis=1)
    kr_all = jnp.concatenate([ckr, kr], axis=1)
    v_all = jnp.concatenate([cv, v], axis=1)
    attn_x = latent_attention(qn, qr, kn_all, kr_all, v_all)
    pool_x = multiscale_pool(px[..., MLA_IN:], w_pool, pool_scale)
    yx = jnp.concatenate([attn_x, pool_x], axis=-1) @ w_out
    if not ctx_out:
        return yx, None
    b, l, _ = hc.shape
    cqn, cqr = mla_queries(pc, q_norm, w_uq)
    attn_c = attend(cqn, cqr, ckn, ckr, cv).reshape(b, l, MLA_OUT)
    pool_c = multiscale_pool(pc[..., MLA_IN:], w_pool, pool_scale)
    yc = jnp.concatenate([attn_c, pool_c], axis=-1) @ w_out
    return yx, yc


def fourier_mixer(h, w_out):
    b, n, _ = h.shape
    hg = h.astype(jnp.float32).reshape(b, n, FOURIER_GROUPS, FOURIER_GROUP)
    f = jnp.fft.fftn(hg, axes=(1, 3), norm='ortho').real
    return f.reshape(b, n, D_MODEL).astype(h.dtype) @ w_out


def sq_relu_mlp(h, w1, w2):
    return jnp.square(jax.nn.relu(h @ w1)) @ w2


def setup_inputs(seed: int = 0) -> dict:
    key = jax.random.key(seed)
    ks = jax.random.split(key, 20)

    def nrm(k, shape, scale):
        return jax.random.normal(k, shape, jnp.float32) * scale

    return {
        'x': nrm(ks[0], (BATCH, SEQ, D_MODEL), 1.0),
        'c': nrm(ks[1], (BATCH, D_MODEL), 1.0),
        'ctx': nrm(ks[2], (BATCH, CTX_LEN, D_MODEL), 1.0),
        'c_ctx': nrm(ks[3], (D_MODEL,), 1.0),
        'w_mod': nrm(ks[4], (DEPTH, D_MODEL, 6 * D_MODEL), 0.5 * D_MODEL ** -0.5),
        'b_mod': nrm(ks[5], (DEPTH, 6 * D_MODEL), 0.02),
        'norm1': 1.0 + nrm(ks[6], (DEPTH, D_MODEL), 0.05),
        'norm2': 1.0 + nrm(ks[7], (DEPTH, D_MODEL), 0.05),
        'w_in': nrm(ks[8], (N_EVEN, D_MODEL, IN_WIDTH), D_MODEL ** -0.5),
        'q_norm': 1.0 + nrm(ks[9], (N_EVEN, Q_RANK), 0.05),
        'w_uq': nrm(ks[10], (N_EVEN, Q_RANK, N_MLA_HEADS * QK_HEAD), Q_RANK ** -0.5),
        'kv_norm': 1.0 + nrm(ks[11], (N_EVEN, KV_RANK), 0.05),
        'w_ukv': nrm(ks[12], (N_EVEN, KV_RANK, N_MLA_HEADS * (QK_NOPE + V_HEAD)), KV_RANK ** -0.5),
        'w_pool': nrm(ks[13], (N_EVEN, POOL_GROUPS, POOL_GROUP, POOL_GROUP), POOL_GROUP ** -0.5),
        'pool_scale': 1.0 + nrm(ks[14], (N_EVEN, POOL_WIDTH), 0.1),
        'w_out_even': nrm(ks[15], (N_EVEN, EVEN_MIX_WIDTH, D_MODEL), EVEN_MIX_WIDTH ** -0.5),
        'w_out_odd': nrm(ks[16], (N_ODD, D_MODEL, D_MODEL), D_MODEL ** -0.5),
        'w_mlp1': nrm(ks[17], (DEPTH, D_MODEL, D_FF), D_MODEL ** -0.5),
        'w_mlp2': nrm(ks[18], (DEPTH, D_FF, D_MODEL), D_FF ** -0.5),
        'final_norm': 1.0 + nrm(ks[19], (D_MODEL,), 0.05),
    }


def reference(x, c, ctx, c_ctx, w_mod, b_mod, norm1, norm2, w_in, q_norm, w_uq, kv_norm, w_ukv,
              w_pool, pool_scale, w_out_even, w_out_odd, w_mlp1, w_mlp2, final_norm):
    n = x.shape[1]
    cos, sin = axial_rope_tables(n)
    cos = cos.astype(x.dtype)
    sin = sin.astype(x.dtype)
    for l in range(DEPTH):
        last = l == DEPTH - 1
        even = l % 2 == 0
        i = l // 2
        mod_x = (jax.nn.silu(c) @ w_mod[l] + b_mod[l])[:, None, :]
        sh1, sc1, g1, sh2, sc2, g2 = jnp.split(mod_x, 6, axis=-1)
        hx = modulate(rmsnorm(x, norm1[l]), sh1, sc1)
        need_ctx = (not last) or even
        if need_ctx:
            mod_c = jax.nn.silu(c_ctx) @ w_mod[l] + b_mod[l]
            csh1, csc1, cg1, csh2, csc2, cg2 = jnp.split(mod_c, 6, axis=-1)
            hc = modulate(rmsnorm(ctx, norm1[l]), csh1, csc1)
        if even:
            yx, yc = even_mixer(hx, hc, w_in[i], q_norm[i], w_uq[i], kv_norm[i], w_ukv[i],
                                w_pool[i], pool_scale[i], w_out_even[i], cos, sin, not last)
        else:
            yx = fourier_mixer(hx, w_out_odd[i])
            yc = None if last else fourier_mixer(hc, w_out_odd[i])
        x = x + g1 * yx
        x = x + g2 * sq_relu_mlp(modulate(rmsnorm(x, norm2[l]), sh2, sc2), w_mlp1[l], w_mlp2[l])
        if not last:
            ctx = ctx + cg1 * yc
            ctx = ctx + cg2 * sq_relu_mlp(modulate(rmsnorm(ctx, norm2[l]), csh2, csc2),
                                          w_mlp1[l], w_mlp2[l])
    return rmsnorm(x, final_norm)
```

```python
import math
import numpy as np
import concourse.bass as bass
import concourse.mybir as mybir
from concourse.bass_utils import run_bass_kernel_spmd

F32 = mybir.dt.float32
BF16 = mybir.dt.bfloat16
AF = mybir.ActivationFunctionType
ALU = mybir.AluOpType
AX = mybir.AxisListType

H = 8; QR_ = 512; KVR = 256; NOPE = 128; ROPE = 64; VH = 128; QKH = 192
EPS = 1e-6
SCALE = QKH ** -0.5


class Cfg:
    def __init__(self, N=8192, L=256, D=2048, DFF=8192, depth=4):
        self.N = N; self.L = L; self.D = D; self.DFF = DFF; self.depth = depth
        self.T = N + L
        self.N2 = N // 128
        self.KC = D // 128
        self.FC = DFF // 128
        self.PW = D // 2
        self.IN = QR_ + KVR + ROPE + self.PW


class Res:
    __slots__ = ("last_w", "reads")

    def __init__(self):
        self.last_w = None
        self.reads = []


class Op:
    __slots__ = ("eng", "emit", "deps", "is_dma", "needed", "semval", "is_mm")

    def __init__(self, eng, emit, deps, is_dma, is_mm):
        self.eng = eng; self.emit = emit; self.deps = deps
        self.is_dma = is_dma; self.is_mm = is_mm
        self.needed = False; self.semval = None


class Prog:
    ENGS = ("tensor", "vector", "scalar", "gpsimd", "sync")

    def __init__(self, nc):
        self.nc = nc
        self.ops = []

    def op(self, eng, emit, reads=(), writes=(), is_dma=False, is_mm=False):
        idx = len(self.ops)
        deps = set()
        for r in reads:
            if r.last_w is not None:
                deps.add(r.last_w)
        for w in writes:
            if w.last_w is not None:
                deps.add(w.last_w)
            deps.update(w.reads)
        self.ops.append(Op(eng, emit, deps, is_dma, is_mm))
        for r in reads:
            r.reads.append(idx)
        for w in writes:
            w.last_w = idx
            w.reads = []
        return idx

    def finish(self, final_ops):
        nc = self.nc
        ops = self.ops
        for o in ops:
            if o.is_dma:
                o.needed = True
            if o.is_mm:
                o.deps = {d for d in o.deps if not ops[d].is_mm}
            for d in o.deps:
                ops[d].needed = True
        for f in final_ops:
            ops[f].needed = True
        self._cms = []

        def newsem(name):
            cm = nc.semaphore(name)
            s = cm.__enter__()
            self._cms.append(cm)
            return s
        NDMA = 20
        dma_sems = {e: [newsem(f"d{e}{i}") for i in range(NDMA)] for e in ("sync", "gpsimd", "scalar")}
        dma_rr = {e: 0 for e in dma_sems}
        dma_cnt = {}
        dma_last = {}
        sems = {e: newsem(f"c{e}") for e in self.ENGS}
        cnt = {e: 0 for e in self.ENGS}
        for i, o in enumerate(ops):
            if not o.needed:
                continue
            if o.is_dma:
                pool = dma_sems[o.eng]
                k = dma_rr[o.eng] % NDMA
                dma_rr[o.eng] += 1
                s = pool[k]
                key = (o.eng, k)
                if key in dma_last:
                    o.deps.add(dma_last[key])
                dma_last[key] = i
                dma_cnt[key] = dma_cnt.get(key, 0) + 16
                o.semval = (s, dma_cnt[key], 16, key)
            else:
                cnt[o.eng] += 1
                o.semval = (sems[o.eng], cnt[o.eng], 1, o.eng)
        known = {e: {} for e in self.ENGS}
        engobj = {"tensor": nc.tensor, "vector": nc.vector, "scalar": nc.scalar,
                  "gpsimd": nc.gpsimd, "sync": nc.sync}
        nwait = 0
        for o in ops:
            e = engobj[o.eng]
            need = {}
            for d in o.deps:
                s, v, _, key = ops[d].semval
                if key not in need or need[key][1] < v:
                    need[key] = (s, v)
            kn = known[o.eng]
            for key, (s, v) in need.items():
                if kn.get(key, 0) >= v:
                    continue
                e.wait_ge(s, v)
                kn[key] = v
                nwait += 1
            ins = o.emit(e)
            if o.needed:
                s, v, inc, key = o.semval
                ins.then_inc(s, inc)
        for f in final_ops:
            s, v, _, _ = ops[f].semval
            nc.sync.wait_ge(s, v)
        return len(ops), nwait


class Tl:
    def __init__(self, t, nres=1):
        self.t = t
        self.rs = [Res() for _ in range(nres)]

    @property
    def r(self):
        return self.rs[0]


def host_tables(cfg):
    N, L, N2 = cfg.N, cfg.L, cfg.N2
    tb = {}
    c = np.arange(512)
    ang = 2 * np.pi * np.outer(c, c) / 512
    tb["t_cch"] = (np.cos(ang) / math.sqrt(512)).astype(np.float32)
    tb["t_nsch"] = (-np.sin(ang) / math.sqrt(512)).astype(np.float32)
    a = np.arange(128)
    ang = 2 * np.pi * np.outer(a, a) / 128
    tb["t_c128"] = np.cos(ang).astype(np.float32)
    tb["t_s128"] = np.sin(ang).astype(np.float32)
    tb["t_ns128"] = (-np.sin(ang)).astype(np.float32)
    n2 = np.arange(N2)
    ang = 2 * np.pi * np.outer(a, n2) / N
    tb["t_twr"] = (np.cos(ang) / math.sqrt(N)).astype(np.float32)
    tb["t_twi"] = (-np.sin(ang) / math.sqrt(N)).astype(np.float32)
    ang = 2 * np.pi * np.outer(n2, n2) / N2
    tb["t_cs"] = np.concatenate([np.cos(ang), np.sin(ang)], 0).astype(np.float32)
    l = np.arange(L)
    ang = 2 * np.pi * np.outer(l, l) / L
    tb["t_cl"] = (np.cos(ang) / math.sqrt(L)).astype(np.float32)
    tb["t_sl"] = (np.sin(ang) / math.sqrt(L)).astype(np.float32)
    GRID_W = 64
    rows = N // GRID_W
    r = np.broadcast_to(np.arange(rows, dtype=np.float32)[:, None], (rows, GRID_W)).reshape(N)
    col = np.broadcast_to(np.arange(GRID_W, dtype=np.float32)[None, :], (rows, GRID_W)).reshape(N)
    inv = (10000.0 ** (-2.0 * np.arange(16, dtype=np.float32) / 32)).astype(np.float32)
    ang = np.stack([r[:, None] * inv, col[:, None] * inv], axis=1)
    ang = np.broadcast_to(ang[:, :, None, :], (N, 2, 2, 16)).reshape(N, 64)
    tb["t_cos"] = np.ascontiguousarray(np.cos(ang).T).astype(np.float32)
    tb["t_sin"] = np.ascontiguousarray(np.sin(ang).T).astype(np.float32)
    invc = np.zeros((4, cfg.T), np.float32)
    for gi, w in enumerate((2, 4, 8, 16)):
        for off, n in ((0, L), (L, N)):
            t = np.arange(n)
            lo = np.clip(t - w // 2, 0, n); hi = np.clip(t + w // 2, 0, n)
            invc[gi, off:off + n] = 1.0 / (hi - lo)
    tb["t_invc"] = invc
    tb["t_ident"] = np.eye(128, dtype=np.float32)
    return tb


def build(cfg, cfg_mixers=(True, True)):
    N, L, D, DFF, T, N2, KC, FC, PW = cfg.N, cfg.L, cfg.D, cfg.DFF, cfg.T, cfg.N2, cfg.KC, cfg.FC, cfg.PW
    depth = cfg.depth
    NE = (depth + 1) // 2; NO = depth // 2
    nc = bass.Bass("TRN2", target_bir_lowering=False)
    P = Prog(nc)

    def din(name, shape):
        return nc.dram_tensor(name, list(shape), F32, kind="ExternalInput")

    x_in = din("x", [N, D]); ctx_in = din("ctx", [L, D])
    c2T = din("c2T", [128, KC, 2])
    w_mod = din("w_mod", [depth, D, 6 * D]); b_modT = din("b_modT", [depth, 128, 6 * KC])
    b_modg = din("b_modg", [depth, 2, D])
    n1T = din("norm1T", [depth, 128, KC]); n2T = din("norm2T", [depth, 128, KC])
    w_in = din("w_in", [NE, D, cfg.IN]); qnT = din("q_normT", [NE, 128, 4]); w_uq = din("w_uq", [NE, QR_, H * QKH])
    kvnT = din("kv_normT", [NE, 128, 2]); w_ukv = din("w_ukv", [NE, KVR, H * 256])
    w_pool = din("w_pool", [NE, 4, 256, 256]); pscT = din("pool_scaleT", [NE, 128, 8])
    w_oute = din("w_out_even", [NE, D, D]); w_outo = din("w_out_odd", [max(NO, 1), D, D])
    w1 = din("w_mlp1", [depth, D, DFF]); w2 = din("w_mlp2", [depth, DFF, D])
    fin = din("final_norm", [D])
    tbl_shapes = {k: v.shape for k, v in host_tables(cfg).items()}
    tb = {k: din(k, s) for k, s in tbl_shapes.items()}
    out = nc.dram_tensor("out", [N, D], F32, kind="ExternalOutput")

    def dsc(name, shape, dt=BF16, nres=1):
        return Tl(nc.dram_tensor(name, list(shape), dt), nres)
    MT = [(0, L)] + [(L + i * 512, 512) for i in range(N // 512)]
    NMT = len(MT)
    XR = dsc("XR", [T, D], F32, NMT)
    w1b = [dsc(f"w1b{l}", [D, DFF]) for l in range(depth)]
    w2b = [dsc(f"w2b{l}", [DFF, D]) for l in range(depth)]
    winb = [dsc(f"winb{i}", [D, cfg.IN]) for i in range(NE)]
    woeb = [dsc(f"woeb{i}", [D, D]) for i in range(NE)]
    woob = [dsc(f"woob{i}", [D, D]) for i in range(NO)]
    GROW = dsc("GROW", [depth, 2, 2, D], F32)
    QN = dsc("QN", [H, 128, T], BF16, NMT); QRd = dsc("QRd", [H, 64, T], BF16, NMT)
    KN = dsc("KN", [H, 128, T], BF16, NMT); KRd = dsc("KRd", [64, T], BF16, NMT)
    VV = dsc("VV", [T, H * 128], BF16, NMT); UU = dsc("UU", [PW, T], F32, NMT)
    AT = dsc("AT", [H * 128, T], BF16, NMT)
    Z0 = dsc("Z0", [T, 2, D], BF16, NMT)
    Z1 = dsc("Z1", [2, N2, 128, D], BF16, N2)

    def sb(name, shape, dt=F32, nres=1):
        return Tl(nc.alloc_sbuf_tensor(name, list(shape), dt), nres)

    def ps(name, shape, dt=F32):
        return Tl(nc.alloc_psum_tensor(name, list(shape), dt))
    ident = sb("ident", [128, 128], BF16)
    ones = sb("ones", [128, 128], BF16)
    xt = [sb(f"xt{i}", [128, D]) for i in range(2)]
    xn = [sb(f"xn{i}", [128, D], BF16) for i in range(2)]
    xn2 = [Tl(nc.alloc_sbuf_tensor(f"xo{i}", [128, D], F32)) for i in range(1)] * 2
    hxT = sb("hxT", [128, KC, 512], BF16)
    BIG = sb("BIG", [128, 64, 512], BF16, 64)
    WR = [sb(f"WR{i}", [128, 16, 512], BF16) for i in range(3)]
    wctr = [0]

    def wnext():
        wctr[0] += 1
        return WR[wctr[0] % 3]
    Gt = sb("Gt", [128, D])
    tmp = [sb(f"tmp{i}", [128, 512]) for i in range(3)]
    stat = sb("stat", [128, 8])
    sc2 = sb("sc2", [128, KC, 2], BF16)
    PSF = [ps(f"psf{i}", [128, 512]) for i in range(6)]
    PST = ps("pst", [128, 2048], BF16)

    def dma(eng, out_ap, in_ap, reads, writes):
        return P.op(eng, lambda e: e.dma_start(out=out_ap, in_=in_ap), reads, writes, is_dma=True)

    def mm(o, oap, lap, rap, reads, start, stop):
        return P.op("tensor", lambda e: e.matmul(oap, lap, rap, start=start, stop=stop), reads, [o.r], is_mm=True)

    def V(fn, reads, writes):
        return P.op("vector", fn, reads, writes)

    def A(fn, reads, writes):
        return P.op("scalar", fn, reads, writes)

    def G(fn, reads, writes):
        return P.op("gpsimd", fn, reads, writes)

    ext = Res()

    dma("gpsimd", ident.t[:], tb["t_ident"][:, :], [ext], [ident.r])
    G(lambda e: e.memset(ones.t[:], 1.0), [], [ones.r])
    epst = sb("epst", [128, 1])
    G(lambda e: e.memset(epst.t[:], EPS), [], [epst.r])
    for mi, (r0, nt) in enumerate(MT):
        src = ctx_in[0:L, :] if mi == 0 else x_in[r0 - L:r0 - L + nt, :]
        dma("sync", XR.t[r0:r0 + nt, :], src, [ext], [XR.rs[mi]])

    def cast_w(dst, src_ap, rows, cols):
        step = max(1, min(rows, (1 << 22) // cols))
        for r0 in range(0, rows, step):
            r1 = min(rows, r0 + step)
            dma("gpsimd", dst.t[r0:r1, :], src_ap[r0:r1, :], [ext], [dst.r])

    c2s = sb("c2s", [128, KC, 2])
    dma("sync", c2s.t[:], c2T[:, :, :], [ext], [c2s.r])
    A(lambda e: e.activation(out=sc2.t[:], in_=c2s.t[:], func=AF.Silu), [c2s.r], [sc2.r])
    bmT = sb("bmT", [128, 6 * KC])
    bmg = sb("bmg", [2, 512])
    nrm = sb("nrm", [128, 2, KC])
    ABs = []
    for l in range(depth):
        ABl = sb(f"AB{l}", [128, 4, KC, 2])
        MODl = sb(f"MODT{l}", [128, 6 * KC, 2])
        ABs.append((ABl, MODl))
        dma("sync", bmT.t[:], b_modT[l, :, :], [ext], [bmT.r])
        dma("sync", nrm.t[:, 0, :], n1T[l, :, :], [ext], [nrm.r])
        dma("sync", nrm.t[:, 1, :], n2T[l, :, :], [ext], [nrm.r])
        for cb in range(6 * D // 512):
            wt = wnext()
            dma("gpsimd", wt.t[:], w_mod[l].rearrange("(kc p) c -> p kc c", p=128)[:, :, cb * 512:(cb + 1) * 512],
                [ext], [wt.r])
            pm = PSF[cb % 2]
            for j in range(4):
                ch = cb * 4 + j
                for kc in range(KC):
                    mm(pm, pm.t[:, j * 2:j * 2 + 2], wt.t[:, kc, j * 128:(j + 1) * 128], sc2.t[:, kc, :],
                       [wt.r, sc2.r], kc == 0, kc == KC - 1)
            V(lambda e, pm=pm, cb=cb, MODl=MODl: e.tensor_tensor(
                out=MODl.t[:, cb * 4:cb * 4 + 4, :], in0=pm.t[:, 0:8].rearrange("p (j r) -> p j r", r=2),
                in1=bmT.t[:, cb * 4:cb * 4 + 4].unsqueeze(2).to_broadcast([128, 4, 2]), op=ALU.add),
              [pm.r, bmT.r], [MODl.r])
            gi = None
            if 2 * KC <= cb * 4 < 3 * KC:
                gi = 0; c0 = cb * 512 - 2 * D
            elif 5 * KC <= cb * 4 < 6 * KC:
                gi = 1; c0 = cb * 512 - 5 * D
            if gi is not None:
                pg = PSF[2 + cb % 2]
                for kc in range(KC):
                    mm(pg, pg.t[0:2, :], sc2.t[:, kc, :], wt.t[:, kc, :], [wt.r, sc2.r], kc == 0, kc == KC - 1)
                gt = tmp[cb % 2]
                dma("sync", bmg.t[:], b_modg[l, gi, c0:c0 + 512].partition_broadcast(2), [ext], [bmg.r])
                V(lambda e, pg=pg, gt=gt: e.tensor_tensor(
                    out=gt.t[0:2, :], in0=pg.t[0:2, :], in1=bmg.t[0:2, :], op=ALU.add),
                  [pg.r, bmg.r], [gt.r])
                dma("sync", GROW.t[l, gi, :, c0:c0 + 512], gt.t[0:2, :], [gt.r], [GROW.r])
        for (ai, sci, shi, ni) in ((0, 1, 0, 0), (2, 4, 3, 1)):
            V(lambda e, ABl=ABl, MODl=MODl, ai=ai, sci=sci, ni=ni: e.scalar_tensor_tensor(
                out=ABl.t[:, ai, :, :], in0=MODl.t[:, sci * KC:(sci + 1) * KC, :], scalar=1.0,
                in1=nrm.t[:, ni, :].unsqueeze(2).to_broadcast([128, KC, 2]), op0=ALU.add, op1=ALU.mult),
              [MODl.r, nrm.r], [ABl.r])
            V(lambda e, ABl=ABl, MODl=MODl, ai=ai, shi=shi: e.tensor_copy(
                out=ABl.t[:, ai + 1, :, :], in_=MODl.t[:, shi * KC:(shi + 1) * KC, :]), [MODl.r], [ABl.r])

    for l in range(depth):
        cast_w(w1b[l], w1[l], D, DFF)
        cast_w(w2b[l], w2[l], DFF, D)
    for i in range(NE):
        cast_w(winb[i], w_in[i], D, cfg.IN)
        cast_w(woeb[i], w_oute[i], D, D)
    for i in range(NO):
        cast_w(woob[i], w_outo[i], D, D)

    def norm_to_hxT(l, mi, which, src=None):
        r0, nt = MT[mi]
        row = 1 if mi == 0 else 0
        ABl = ABs[l][0]
        ai = 0 if which == 1 else 2
        for s in range(nt // 128):
            x_ = xt[s % 2]; xn_ = xn[s % 2]
            dma("sync", x_.t[:], XR.t[r0 + s * 128:r0 + (s + 1) * 128, :], [XR.rs[mi]], [x_.r])
            st = stat
            A(lambda e, x_=x_, xn_=xn_: e.activation(out=xn_.t[:], in_=x_.t[:], func=AF.Square, accum_out=stat.t[:, 0:1]),
              [x_.r], [xn_.r, st.r])
            A(lambda e: e.activation(out=stat.t[:, 1:2], in_=stat.t[:, 0:1], func=AF.Sqrt, bias=epst.t[:, 0:1], scale=1.0 / D),
              [st.r, epst.r], [st.r])
            V(lambda e: e.reciprocal(out=stat.t[:, 2:3], in_=stat.t[:, 1:2]), [st.r], [st.r])
            V(lambda e, x_=x_, xn_=xn_: e.tensor_scalar(out=xn_.t[:], in0=x_.t[:], scalar1=stat.t[:, 2:3], scalar2=None,
                                                        op0=ALU.mult), [x_.r, st.r], [xn_.r])
            for kc in range(KC):
                P.op("tensor", lambda e, kc=kc, xn_=xn_: e.transpose(PST.t[:, kc * 128:(kc + 1) * 128],
                                                                    xn_.t[:, kc * 128:(kc + 1) * 128], ident.t[:]),
                     [xn_.r, ident.r], [PST.r], is_mm=True)
            for kc in range(KC):
                V(lambda e, kc=kc, s=s: e.tensor_scalar(
                    out=hxT.t[:, kc, s * 128:(s + 1) * 128], in0=PST.t[:, kc * 128:(kc + 1) * 128],
                    scalar1=ABl.t[:, ai, kc, row:row + 1], scalar2=ABl.t[:, ai + 1, kc, row:row + 1],
                    op0=ALU.mult, op1=ALU.add), [PST.r, ABl.r], [hxT.r])

    def G_or_A(fn, reads, writes):
        return V(fn, reads, writes)

    def load_G(l, gi, row):
        dma("sync", Gt.t[:], GROW.t[l, gi, row, :].partition_broadcast(128), [GROW.r], [Gt.r])

    def resid_update(mi, s, j, pt, rows_aps=None, xres=None):
        if xres is None:
            xres = [XR.rs[mi]]
        xb = tmp[(s * 4 + j) % 2]
        if rows_aps is None:
            r0, nt = MT[mi]
            rows_aps = [(0, 128, XR.t[r0 + s * 128:r0 + (s + 1) * 128, j * 512:(j + 1) * 512])]
        for (p0, pn, ap) in rows_aps:
            dma("gpsimd", xb.t[p0:p0 + pn, :], ap, xres, [xb.r])
        t2 = tmp[2]
        V(lambda e: e.tensor_tensor(out=t2.t[:], in0=pt.t[:], in1=Gt.t[:, j * 512:(j + 1) * 512], op=ALU.mult),
          [pt.r, Gt.r], [t2.r])
        V(lambda e: e.tensor_tensor(out=xb.t[:], in0=xb.t[:], in1=t2.t[:], op=ALU.add), [xb.r, t2.r], [xb.r])
        last = None
        for (p0, pn, ap) in rows_aps:
            last = dma("gpsimd", ap, xb.t[p0:p0 + pn, :], [xb.r], xres)
        return last

    def proj_out_resid(srcT, nt, wdram, mi=None, rows_fn=None, xres=None):
        for j in range(D // 512):
            wt = wnext()
            dma("sync", wt.t[:], wdram.t.ap().rearrange("(kc p) c -> p kc c", p=128)[:, :, j * 512:(j + 1) * 512],
                [wdram.r], [wt.r])
            for s in range(nt // 128):
                pt = PSF[(s + j) % 2]
                for kc in range(KC):
                    mm(pt, pt.t[:], srcT.t[:, kc, s * 128:(s + 1) * 128], wt.t[:, kc, :], [srcT.r, wt.r], kc == 0, kc == KC - 1)
                resid_update(mi, s, j, pt, None if rows_fn is None else rows_fn(s, j), xres)

    def mlp_phase(l, final=False):
        finals = []
        if final:
            fg = sb("fing", [128, D])
            dma("sync", fg.t[:], fin.ap().partition_broadcast(128), [ext], [fg.r])
        for mi, (r0, nt) in enumerate(MT):
            if final and mi == 0:
                continue
            if mi <= 1:
                load_G(l, 1, 1 if mi == 0 else 0)
            norm_to_hxT(l, mi, 2)
            for fb in range(DFF // 512):
                wt = wnext()
                dma("sync", wt.t[:], w1b[l].t.ap().rearrange("(kc p) f -> p kc f", p=128)[:, :, fb * 512:(fb + 1) * 512],
                    [w1b[l].r], [wt.r])
                for j in range(4):
                    fc = fb * 4 + j
                    pt = PSF[fc % 2]
                    for kc in range(KC):
                        mm(pt, pt.t[:, :nt], wt.t[:, kc, j * 128:(j + 1) * 128], hxT.t[:, kc, :nt], [wt.r, hxT.r], kc == 0, kc == KC - 1)
                    tq = tmp[fc % 2]
                    A(lambda e, pt=pt, tq=tq, nt=nt: e.activation(out=tq.t[:, :nt], in_=pt.t[:, :nt], func=AF.Relu), [pt.r], [tq.r])
                    V(lambda e, tq=tq, fc=fc, nt=nt: e.tensor_tensor(out=BIG.t[:, fc, :nt], in0=tq.t[:, :nt], in1=tq.t[:, :nt], op=ALU.mult),
                      [tq.r], [BIG.rs[fc]])
            nsub = nt // 128
            for j in range(D // 512):
                for fg_ in range(FC // 16):
                    wt = wnext()
                    dma("sync", wt.t[:], w2b[l].t.ap().rearrange("(fc p) d -> p fc d", p=128)[:, fg_ * 16:(fg_ + 1) * 16, j * 512:(j + 1) * 512],
                        [w2b[l].r], [wt.r])
                    for f in range(16):
                        fc = fg_ * 16 + f
                        for s in range(nsub):
                            pt = PSF[2 + s]
                            mm(pt, pt.t[:], BIG.t[:, fc, s * 128:(s + 1) * 128], wt.t[:, f, :], [BIG.rs[fc], wt.r], fc == 0, fc == FC - 1)
                for s in range(nsub):
                    pt = PSF[2 + s]
                    resid_update(mi, s, j, pt)
            if final:
                for s in range(nsub):
                    x_ = xt[s % 2]; xo = xn2[s % 2]
                    dma("sync", x_.t[:], XR.t[r0 + s * 128:r0 + (s + 1) * 128, :], [XR.rs[mi]], [x_.r])
                    A(lambda e, x_=x_, xo=xo: e.activation(out=xo.t[:], in_=x_.t[:], func=AF.Square, accum_out=stat.t[:, 4:5]),
                      [x_.r], [xo.r, stat.r])
                    A(lambda e: e.activation(out=stat.t[:, 5:6], in_=stat.t[:, 4:5], func=AF.Sqrt, bias=epst.t[:, 0:1], scale=1.0 / D),
                      [stat.r, epst.r], [stat.r])
                    V(lambda e: e.reciprocal(out=stat.t[:, 6:7], in_=stat.t[:, 5:6]), [stat.r], [stat.r])
                    V(lambda e, x_=x_, xo=xo: e.scalar_tensor_tensor(out=xo.t[:], in0=x_.t[:], scalar=stat.t[:, 6:7], in1=fg.t[:],
                                                            op0=ALU.mult, op1=ALU.mult), [x_.r, stat.r, fg.r], [xo.r])
                    finals.append(dma("gpsimd", out[r0 - L + s * 128:r0 - L + (s + 1) * 128, :], xo.t[:], [xo.r], []))
        return finals


    if NO > 0:
        cch = sb("cch", [128, 4, 512], BF16); nsch = sb("nsch", [128, 4, 512], BF16)
        dma("gpsimd", cch.t[:], tb["t_cch"].ap().rearrange("(kc p) m -> p kc m", p=128), [ext], [cch.r])
        dma("gpsimd", nsch.t[:], tb["t_nsch"].ap().rearrange("(kc p) m -> p kc m", p=128), [ext], [nsch.r])
        c128 = sb("c128", [128, 128], BF16); s128 = sb("s128", [128, 128], BF16); ns128 = sb("ns128", [128, 128], BF16)
        dma("gpsimd", c128.t[:], tb["t_c128"][:, :], [ext], [c128.r])
        dma("gpsimd", s128.t[:], tb["t_s128"][:, :], [ext], [s128.r])
        dma("gpsimd", ns128.t[:], tb["t_ns128"][:, :], [ext], [ns128.r])
        twr = sb("twr", [128, N2]); twi = sb("twi", [128, N2])
        dma("sync", twr.t[:], tb["t_twr"][:, :], [ext], [twr.r])
        dma("sync", twi.t[:], tb["t_twi"][:, :], [ext], [twi.r])
        cs = sb("cs", [2 * N2, N2], BF16)
        dma("gpsimd", cs.t[:], tb["t_cs"][:, :], [ext], [cs.r])
        cl = sb("cl", [128, L // 128, L], BF16); sl = sb("sl", [128, L // 128, L], BF16)
        dma("gpsimd", cl.t[:], tb["t_cl"].ap().rearrange("(a p) k -> p a k", p=128), [ext], [cl.r])
        dma("gpsimd", sl.t[:], tb["t_sl"].ap().rearrange("(a p) k -> p a k", p=128), [ext], [sl.r])

    def zreg(b):
        ap = BIG.t[:, 8 * b:8 * b + 8, :].rearrange("p (r c) f -> p r (c f)", r=2)
        return ap, BIG.rs[8 * b:8 * b + 8]

    def evac(i, out_ap, in_ap, reads, writes):
        if i % 2 == 0:
            V(lambda e: e.tensor_copy(out=out_ap, in_=in_ap), reads, writes)
        else:
            A(lambda e: e.activation(out=out_ap, in_=in_ap, func=AF.Copy), reads, writes)

    def odd_mixer(l, i, last):
        KS = 128 // N2
        xlat = XR.rs[1:]
        for mi, (r0, nt) in enumerate(MT):
            if last and mi == 0:
                continue
            norm_to_hxT(l, mi, 1)
            for s in range(nt // 128):
                zap, zrs = zreg(s % 2)
                for ri in range(2):
                    tabl = cch if ri == 0 else nsch
                    for g in range(4):
                        pt = PSF[(ri * 4 + g) % 2]
                        for kc in range(4):
                            mm(pt, pt.t[:], hxT.t[:, 4 * g + kc, s * 128:(s + 1) * 128], tabl.t[:, kc, :], [hxT.r, tabl.r], kc == 0, kc == 3)
                        evac(g, zap[:, ri, g * 512:(g + 1) * 512], pt.t[:], [pt.r], zrs)
                dma("gpsimd", Z0.t[r0 + s * 128:r0 + (s + 1) * 128, :, :], zap, zrs, [Z0.rs[mi]])
        z0lat = Z0.t[L:L + N, :, :].rearrange("(n1 n2) r d -> n1 n2 r d", n2=N2)
        for n2 in range(N2):
            zin, zirs = zreg(n2 % 2)
            zo, zors = zreg(2 + n2 % 2)
            dma("sync", zin, z0lat[:, n2, :, :], Z0.rs[1:], zirs)
            for cb in range(4):
                sl_ = slice(cb * 512, (cb + 1) * 512)
                yr = PSF[0 + 2 * (cb % 2)]; yi = PSF[1 + 2 * (cb % 2)]
                mm(yr, yr.t[:], c128.t[:], zin[:, 0, sl_], [c128.r] + zirs, True, False)
                mm(yr, yr.t[:], s128.t[:], zin[:, 1, sl_], [s128.r] + zirs, False, True)
                mm(yi, yi.t[:], c128.t[:], zin[:, 1, sl_], [c128.r] + zirs, True, False)
                mm(yi, yi.t[:], ns128.t[:], zin[:, 0, sl_], [ns128.r] + zirs, False, True)
                t1 = tmp[0]; t2 = tmp[1]
                A(lambda e, yi=yi, n2=n2: e.activation(out=t1.t[:], in_=yi.t[:], func=AF.Copy, scale=twi.t[:, n2:n2 + 1]),
                  [yi.r, twi.r], [t1.r])
                A(lambda e, yi=yi, n2=n2: e.activation(out=t2.t[:], in_=yi.t[:], func=AF.Copy, scale=twr.t[:, n2:n2 + 1]),
                  [yi.r, twr.r], [t2.r])
                V(lambda e, yr=yr, n2=n2, sl_=sl_, zo=zo: e.scalar_tensor_tensor(
                    out=zo[:, 0, sl_], in0=yr.t[:], scalar=twr.t[:, n2:n2 + 1], in1=t1.t[:], op0=ALU.mult, op1=ALU.subtract),
                  [yr.r, twr.r, t1.r], zors)
                V(lambda e, yr=yr, n2=n2, sl_=sl_, zo=zo: e.scalar_tensor_tensor(
                    out=zo[:, 1, sl_], in0=yr.t[:], scalar=twi.t[:, n2:n2 + 1], in1=t2.t[:], op0=ALU.mult, op1=ALU.add),
                  [yr.r, twi.r, t2.r], zors)
            for ri in range(2):
                dma("gpsimd", Z1.t[ri, n2, :, :], zo[:, ri, :], zors, [Z1.rs[n2]])
        load_G(l, 0, 0)
        z1v = Z1.t.ap().rearrange("r n k d -> (r n) k d")
        xrl = XR.t[L:L + N, :].rearrange("(k2 k1) d -> k1 k2 d", k1=128)
        for m in range(128 // (4 * KS)):
            for s in range(4):
                for k1l in range(KS):
                    k1 = (4 * m + s) * KS + k1l
                    z3 = xn[k1l % 2]
                    dma("sync", z3.t[0:2 * N2, :], z1v[:, k1, :], Z1.rs, [z3.r])
                    for kc in range(KC):
                        pt = PSF[2 + kc // 4]
                        c0 = (kc % 4) * 128 + k1l * N2
                        mm(pt, pt.t[:, c0:c0 + N2], z3.t[0:2 * N2, kc * 128:(kc + 1) * 128], cs.t[:, :], [z3.r, cs.r], True, True)
                for b in range(4):
                    pt = PSF[2 + b]
                    evac(b, hxT.t[:, 4 * b:4 * b + 4, s * 128:(s + 1) * 128], pt.t[:].rearrange("p (c t) -> p c t", c=4), [pt.r], [hxT.r])

            def rows_fn(s, j, m=m):
                return [(k1l * N2, N2, xrl[(4 * m + s) * KS + k1l, :, j * 512:(j + 1) * 512]) for k1l in range(KS)]
            proj_out_resid(hxT, 512, woob[i], None, rows_fn, xlat)
        if not last:
            load_G(l, 0, 1)
            zc = []
            for a in range(L // 128):
                zap, zrs = zreg(a)
                dma("sync", zap, Z0.t[a * 128:(a + 1) * 128, :, :], [Z0.rs[0]], zrs)
                zc.append((zap, zrs))
            for kc in range(KC):
                pt = PSF[kc % 2]
                n = 0
                for a in range(L // 128):
                    zap, zrs = zc[a]
                    for ri, tabl in ((0, cl), (1, sl)):
                        mm(pt, pt.t[:, :L], zap[:, ri, kc * 128:(kc + 1) * 128], tabl.t[:, a, :], zrs + [tabl.r], n == 0, n == 2 * (L // 128) - 1)
                        n += 1
                evac(kc, hxT.t[:, kc, 0:L], pt.t[:, :L], [pt.r], [hxT.r])
            proj_out_resid(hxT, L, woob[i], 0)

    wpool = sb("wpool", [128, 4, 2, 256], BF16)
    wrot = sb("wrot", [128, KC, 64], BF16)
    smallp = sb("smallp", [128, 16])
    qm = sb("qm", [128, 8]); km = sb("km", [128, 8]); nb = sb("nb", [128, 8])

    def bv(c0, n):
        return BIG.t[:, c0:c0 + n, :], BIG.rs[c0:c0 + n]

    def bvf(c0, n):
        return BIG.t[:, c0:c0 + n, :].rearrange("p a b -> p (a b)").bitcast(F32), BIG.rs[c0:c0 + n]

    def even_mixer(l, i, last):
        wuq, wuq_r = bv(0, 12); wuq = wuq.rearrange("p a b -> p (a b)").rearrange("p (k c) -> p k c", k=4)
        wuqrot, wuqrot_r = bv(12, 4)
        wukv, wukv_r = bv(16, 8); wukv = wukv.rearrange("p a b -> p (a b)").rearrange("p (k c) -> p k c", k=2)
        wv, wv_r = bv(24, 4); wv = wv.rearrange("p a b -> p (a b)").rearrange("p (k c) -> p k c", k=2)
        dma("gpsimd", wuq, w_uq[i].rearrange("(k p) c -> p k c", p=128), [ext], wuq_r)
        dma("gpsimd", wukv, w_ukv[i].rearrange("(k p) c -> p k c", p=128), [ext], wukv_r)
        dma("gpsimd", wpool.t[:], w_pool[i].rearrange("g (k p) d -> p g k d", p=128), [ext], [wpool.r])
        dma("sync", smallp.t[:, 0:4], qnT[i, :, :], [ext], [smallp.r])
        dma("sync", smallp.t[:, 4:6], kvnT[i, :, :], [ext], [smallp.r])
        dma("sync", smallp.t[:, 6:14], pscT[i, :, :], [ext], [smallp.r])
        for h in range(H):
            V(lambda e, h=h: e.tensor_copy(out=wv[:, :, h * 128:(h + 1) * 128], in_=wukv[:, :, h * 256 + 128:h * 256 + 256]),
              wukv_r, wv_r)
            for a in range(2):
                src0 = h * 192 + 128 + a * 32
                V(lambda e, h=h, a=a, src0=src0: e.tensor_scalar(out=wuqrot[:, :, h * 64 + a * 32:h * 64 + a * 32 + 16],
                                                               in0=wuq[:, :, src0 + 16:src0 + 32], scalar1=-1.0, scalar2=None, op0=ALU.mult),
                  wuq_r, wuqrot_r)
                V(lambda e, h=h, a=a, src0=src0: e.tensor_copy(out=wuqrot[:, :, h * 64 + a * 32 + 16:h * 64 + a * 32 + 32],
                                                             in_=wuq[:, :, src0:src0 + 16]), wuq_r, wuqrot_r)
        wb0 = wnext()
        winv = winb[i].t.ap().rearrange("(kc p) c -> p kc c", p=128)
        dma("sync", wb0.t[:, :, 0:320], winv[:, :, 512:832], [winb[i].r], [wb0.r])
        for a in range(2):
            V(lambda e, a=a: e.tensor_scalar(out=wrot.t[:, :, a * 32:a * 32 + 16], in0=wb0.t[:, :, 256 + a * 32 + 16:256 + a * 32 + 32],
                                            scalar1=-1.0, scalar2=None, op0=ALU.mult), [wb0.r], [wrot.r])
            V(lambda e, a=a: e.tensor_copy(out=wrot.t[:, :, a * 32 + 16:a * 32 + 32], in_=wb0.t[:, :, 256 + a * 32:256 + a * 32 + 16]),
              [wb0.r], [wrot.r])
        G(lambda e: e.memset(qm.t[:], 0.0), [], [qm.r])
        G(lambda e: e.memset(km.t[:], 0.0), [], [km.r])
        qlat, qlat_r = bvf(28, 8); qlat = qlat.rearrange("p (k t) -> p k t", k=4)
        kvlat, kvlat_r = bvf(36, 4); kvlat = kvlat.rearrange("p (k t) -> p k t", k=2)
        qn_, qn_r = bv(40, 4)
        kvn, kvn_r = bv(44, 2)
        sqb, sqb_r = bv(46, 2)
        cst, cst_r = bvf(48, 2); snt, snt_r = bvf(50, 2)
        stg = [bv(52 + k, 1) for k in range(8)]
        sctr = [0]

        def stage():
            sctr[0] += 1
            ap, rs = stg[sctr[0] % 8]
            return ap[:, 0, :], rs

        def rms_lat(lat, lat_r, nch, ncol0, out, out_r, nt, dim):
            pss = PSF[2]
            for c in range(nch):
                A(lambda e, c=c: e.activation(out=sqb[:, 0, :nt], in_=lat[:, c, :nt], func=AF.Square), lat_r, sqb_r)
                mm(pss, pss.t[:, :nt], ones.t[:], sqb[:, 0, :nt], [ones.r] + sqb_r, c == 0, c == nch - 1)
            A(lambda e: e.activation(out=tmp[0].t[:, :nt], in_=pss.t[:, :nt], func=AF.Sqrt, bias=epst.t[:, 0:1], scale=1.0 / dim),
              [pss.r, epst.r], [tmp[0].r])
            V(lambda e: e.reciprocal(out=tmp[1].t[:, :nt], in_=tmp[0].t[:, :nt]), [tmp[0].r], [tmp[1].r])
            for c in range(nch):
                V(lambda e, c=c: e.scalar_tensor_tensor(out=out[:, c, :nt], in0=lat[:, c, :nt], scalar=smallp.t[:, ncol0 + c:ncol0 + c + 1],
                                                        in1=tmp[1].t[:, :nt], op0=ALU.mult, op1=ALU.mult),
                  lat_r + [smallp.r, tmp[1].r], out_r)

        def rope_out(pr, prot, nt, is_lat, dst_ap, dst_res):
            st, st_r = stage()
            if is_lat:
                V(lambda e: e.tensor_tensor(out=tmp[0].t[0:64, :nt], in0=prot.t[0:64, :nt], in1=snt[0:64, :nt], op=ALU.mult),
                  [prot.r] + snt_r, [tmp[0].r])
                V(lambda e: e.tensor_tensor(out=tmp[1].t[0:64, :nt], in0=pr.t[0:64, :nt], in1=cst[0:64, :nt], op=ALU.mult),
                  [pr.r] + cst_r, [tmp[1].r])
                V(lambda e: e.tensor_tensor(out=st[0:64, :nt], in0=tmp[0].t[0:64, :nt], in1=tmp[1].t[0:64, :nt], op=ALU.add),
                  [tmp[0].r, tmp[1].r], st_r)
            else:
                V(lambda e: e.tensor_copy(out=st[0:64, :nt], in_=pr.t[0:64, :nt]), [pr.r], st_r)
            dma("gpsimd", dst_ap, st[0:64, :nt], st_r, dst_res)
            return st, st_r

        def upd_max(mx, h, nope, nope_r, rp, rp_r, nt):
            pn = PSF[3]
            A(lambda e: e.activation(out=sqb[:, 0, :nt], in_=nope[:, :nt], func=AF.Square), nope_r, sqb_r)
            A(lambda e: e.activation(out=sqb[0:64, 1, :nt], in_=rp[0:64, :nt], func=AF.Square), rp_r, sqb_r)
            mm(pn, pn.t[:, :nt], ones.t[:], sqb[:, 0, :nt], [ones.r] + sqb_r, True, False)
            mm(pn, pn.t[:, :nt], ones.t[0:64, :], sqb[0:64, 1, :nt], [ones.r] + sqb_r, False, True)
            V(lambda e: e.reduce_max(out=stat.t[:, 7:8], in_=pn.t[:, :nt], axis=AX.X), [pn.r], [stat.r])
            V(lambda e, h=h: e.tensor_max(out=mx.t[:, h:h + 1], in0=mx.t[:, h:h + 1], in1=stat.t[:, 7:8]), [mx.r, stat.r], [mx.r])

        for mi, (r0, nt) in enumerate(MT):
            is_lat = mi > 0
            norm_to_hxT(l, mi, 1)
            if is_lat:
                dma("sync", cst[0:64, :nt], tb["t_cos"][:, r0 - L:r0 - L + nt], [ext], cst_r)
                dma("sync", snt[0:64, :nt], tb["t_sin"][:, r0 - L:r0 - L + nt], [ext], snt_r)
            wA = wnext()
            dma("sync", wA.t[:], winv[:, :, 0:512], [winb[i].r], [wA.r])
            for qc in range(4):
                pt = PSF[qc % 2]
                for kc in range(KC):
                    mm(pt, pt.t[:, :nt], wA.t[:, kc, qc * 128:(qc + 1) * 128], hxT.t[:, kc, :nt], [wA.r, hxT.r], kc == 0, kc == KC - 1)
                A(lambda e, pt=pt, qc=qc, nt=nt: e.activation(out=qlat[:, qc, :nt], in_=pt.t[:, :nt], func=AF.Copy), [pt.r], qlat_r)
            rms_lat(qlat, qlat_r, 4, 0, qn_, qn_r, nt, 512)
            wB = wnext()
            dma("sync", wB.t[:, :, 0:320], winv[:, :, 512:832], [winb[i].r], [wB.r])
            for c in range(2):
                pt = PSF[c % 2]
                for kc in range(KC):
                    mm(pt, pt.t[:, :nt], wB.t[:, kc, c * 128:(c + 1) * 128], hxT.t[:, kc, :nt], [wB.r, hxT.r], kc == 0, kc == KC - 1)
                A(lambda e, pt=pt, c=c, nt=nt: e.activation(out=kvlat[:, c, :nt], in_=pt.t[:, :nt], func=AF.Copy), [pt.r], kvlat_r)
            rms_lat(kvlat, kvlat_r, 2, 4, kvn, kvn_r, nt, 256)
            pr = PSF[0]; prot = PSF[1]
            for kc in range(KC):
                mm(pr, pr.t[0:64, :nt], wB.t[:, kc, 256:320], hxT.t[:, kc, :nt], [wB.r, hxT.r], kc == 0, kc == KC - 1)
            for kc in range(KC):
                mm(prot, prot.t[0:64, :nt], wrot.t[:, kc, :], hxT.t[:, kc, :nt], [wrot.r, hxT.r], kc == 0, kc == KC - 1)
            krs, krs_r = rope_out(pr, prot, nt, is_lat, KRd.t[:, r0:r0 + nt], [KRd.rs[mi]])
            for h in range(H):
                pt = PSF[0]
                for kc in range(2):
                    mm(pt, pt.t[:, :nt], wukv[:, kc, h * 256:h * 256 + 128], kvn[:, kc, :nt], wukv_r + kvn_r, kc == 0, kc == 1)
                st, st_r = stage()
                evac(h, st[:, :nt], pt.t[:, :nt], [pt.r], st_r)
                dma("gpsimd", KN.t[h, :, r0:r0 + nt], st[:, :nt], st_r, [KN.rs[mi]])
                upd_max(km, h, st, st_r, krs, krs_r, nt)
                pt = PSF[1]
                for kc in range(4):
                    mm(pt, pt.t[:, :nt], wuq[:, kc, h * 192:h * 192 + 128], qn_[:, kc, :nt], wuq_r + qn_r, kc == 0, kc == 3)
                sq_, sq_r = stage()
                evac(h + 1, sq_[:, :nt], pt.t[:, :nt], [pt.r], sq_r)
                dma("gpsimd", QN.t[h, :, r0:r0 + nt], sq_[:, :nt], sq_r, [QN.rs[mi]])
                pr = PSF[4]; prot = PSF[5]
                for kc in range(4):
                    mm(pr, pr.t[0:64, :nt], wuq[:, kc, h * 192 + 128:h * 192 + 192], qn_[:, kc, :nt], wuq_r + qn_r, kc == 0, kc == 3)
                for kc in range(4):
                    mm(prot, prot.t[0:64, :nt], wuqrot[:, kc, h * 64:(h + 1) * 64], qn_[:, kc, :nt], wuqrot_r + qn_r, kc == 0, kc == 3)
                qrs, qrs_r = rope_out(pr, prot, nt, is_lat, QRd.t[h, :, r0:r0 + nt], [QRd.rs[mi]])
                upd_max(qm, h, sq_, sq_r, qrs, qrs_r, nt)
            for s in range(nt // 128):
                for vb in range(2):
                    pt = PSF[vb]
                    for kc in range(2):
                        mm(pt, pt.t[:], kvn[:, kc, s * 128:(s + 1) * 128], wv[:, kc, vb * 512:(vb + 1) * 512], kvn_r + wv_r, kc == 0, kc == 1)
                    st, st_r = stage()
                    evac(vb, st, pt.t[:], [pt.r], st_r)
                    dma("gpsimd", VV.t[r0 + s * 128:r0 + (s + 1) * 128, vb * 512:(vb + 1) * 512], st, st_r, [VV.rs[mi]])
            for blk in range(2):
                wC = wnext()
                dma("sync", wC.t[:], winv[:, :, 832 + blk * 512:832 + (blk + 1) * 512], [winb[i].r], [wC.r])
                for c in range(4):
                    pt = PSF[c % 2]
                    for kc in range(KC):
                        mm(pt, pt.t[:, :nt], wC.t[:, kc, c * 128:(c + 1) * 128], hxT.t[:, kc, :nt], [wC.r, hxT.r], kc == 0, kc == KC - 1)
                    tq = tmp[c % 2]
                    evac(c, tq.t[:, :nt], pt.t[:, :nt], [pt.r], [tq.r])
                    uc = blk * 4 + c
                    dma("gpsimd", UU.t[uc * 128:(uc + 1) * 128, r0:r0 + nt], tq.t[:, :nt], [tq.r], [UU.rs[mi]])
        V(lambda e: e.tensor_tensor(out=nb.t[:], in0=qm.t[:], in1=km.t[:], op=ALU.mult), [qm.r, km.r], [nb.r])
        A(lambda e: e.activation(out=nb.t[:], in_=nb.t[:], func=AF.Sqrt), [nb.r], [nb.r])
        V(lambda e: e.tensor_scalar(out=nb.t[:], in0=nb.t[:], scalar1=-SCALE, scalar2=None, op0=ALU.mult), [nb.r], [nb.r])

        NKC = T // 128
        TC = (T + 511) // 512
        krt, krt_r = bv(0, TC); krt = krt.rearrange("p a b -> p (a b)")
        knt, knt_r = bv(TC, TC); knt = knt.rearrange("p a b -> p (a b)")
        vt, vt_r = bv(2 * TC, TC); vt = vt.rearrange("p a b -> p (a b)")[:, 0:NKC * 128].rearrange("p (k d) -> p k d", d=128)
        qb_ = [bv(3 * TC + k, 1) for k in range(4)]
        pts = [bv(3 * TC + 4 + k, 1) for k in range(3)]
        ots = [bv(3 * TC + 7 + k, 1) for k in range(2)]
        dma("sync", krt[0:64, 0:T], KRd.t[:, :], KRd.rs, krt_r)
        qblocks = [(r0, nt, (L // 128 if mi == 0 else NKC)) for mi, (r0, nt) in enumerate(MT) if not (last and mi == 0)]
        pc = 0
        for h in range(H):
            dma("sync", knt[:, 0:T], KN.t[h, :, :], KN.rs, knt_r)
            dma("sync", vt, VV.t[:, h * 128:(h + 1) * 128].rearrange("(k p) d -> p k d", p=128), VV.rs, vt_r)
            for qi, (r0, nt, nk) in enumerate(qblocks):
                mi = qi if not last else qi + 1
                qa, qa_r = qb_[(qi % 2) * 2]; qr_, qr_r = qb_[(qi % 2) * 2 + 1]
                dma("sync", qa[:, 0, :nt], QN.t[h, :, r0:r0 + nt], [QN.rs[mi]], qa_r)
                dma("sync", qr_[0:64, 0, :nt], QRd.t[h, :, r0:r0 + nt], [QRd.rs[mi]], qr_r)
                po = PSF[2 + qi % 2]; pd = PSF[4 + qi % 2]
                for kc in range(nk):
                    st = PSF[kc % 2]
                    mm(st, st.t[:, :nt], knt[:, kc * 128:(kc + 1) * 128], qa[:, 0, :nt], knt_r + qa_r, True, False)
                    mm(st, st.t[:, :nt], krt[0:64, kc * 128:(kc + 1) * 128], qr_[0:64, 0, :nt], krt_r + qr_r, False, True)
                    pT, pT_r = pts[pc % 3]; pc += 1
                    A(lambda e, st=st, pT=pT, h=h, nt=nt: e.activation(out=pT[:, 0, :nt], in_=st.t[:, :nt], func=AF.Exp,
                                                                       bias=nb.t[:, h:h + 1], scale=SCALE), [st.r, nb.r], pT_r)
                    mm(po, po.t[:, :nt], vt[:, kc, :], pT[:, 0, :nt], vt_r + pT_r, kc == 0, kc == nk - 1)
                    mm(pd, pd.t[:, :nt], ones.t[:], pT[:, 0, :nt], [ones.r] + pT_r, kc == 0, kc == nk - 1)
                rd = tmp[qi % 2]
                V(lambda e, pd=pd, rd=rd, nt=nt: e.reciprocal(out=rd.t[:, :nt], in_=pd.t[:, :nt]), [pd.r], [rd.r])
                ot, ot_r = ots[qi % 2]
                V(lambda e, po=po, rd=rd, ot=ot, nt=nt: e.tensor_tensor(out=ot[:, 0, :nt], in0=po.t[:, :nt], in1=rd.t[:, :nt], op=ALU.mult),
                  [po.r, rd.r], ot_r)
                dma("gpsimd", AT.t[h * 128:(h + 1) * 128, r0:r0 + nt], ot[:, 0, :nt], ot_r, [AT.rs[mi]])

        ub = [bvf(0 + 5 * k, 5) for k in range(3)]
        icv, icv_r = bvf(16, 2)
        plT, plT_r = bv(20, 2)
        atv = AT.t.ap().rearrange("(c p) t -> p c t", p=128)
        for mi, (r0, nt) in enumerate(MT):
            if last and mi == 0:
                continue
            seg0, seg1 = (0, L) if mi == 0 else (L, T)
            if mi <= 1:
                load_G(l, 0, 1 if mi == 0 else 0)
            dma("sync", hxT.t[:, 0:8, :nt], atv[:, :, r0:r0 + nt], [AT.rs[mi]], [hxT.r])
            W_ = nt + 16
            for g in range(4):
                U_, U_r = ub[0]; U_ = U_[:, 0:2 * W_].rearrange("p (c w) -> p c w", c=2)
                S1, S1_r = ub[1]; S1 = S1[:, 0:2 * W_].rearrange("p (c w) -> p c w", c=2)
                S2, S2_r = ub[2]; S2 = S2[:, 0:2 * W_].rearrange("p (c w) -> p c w", c=2)
                V(lambda e, U_=U_: e.memset(U_, 0.0), [], U_r)
                lo = max(seg0, r0 - 8); hi = min(seg1, r0 + nt + 8)
                ures = [UU.rs[k] for k in range(NMT) if MT[k][0] < hi and MT[k][0] + MT[k][1] > lo]
                for c in range(2):
                    dma("sync", U_[:, c, lo - (r0 - 8):hi - (r0 - 8)], UU.t[(2 * g + c) * 128:(2 * g + c + 1) * 128, lo:hi], ures, U_r)
                dma("sync", icv[:, :nt], tb["t_invc"][g, r0:r0 + nt].partition_broadcast(128), [ext], icv_r)
                V(lambda e, U_=U_, S1=S1, W_=W_: e.tensor_tensor(out=S1[:, :, 1:W_], in0=U_[:, :, 0:W_ - 1], in1=U_[:, :, 1:W_], op=ALU.add), U_r, S1_r)
                cur, cur_r = S1, S1_r
                oth, oth_r = S2, S2_r
                sh = 1
                for lev in range(g):
                    a = 2 * sh
                    V(lambda e, cur=cur, oth=oth, a=a, sh=sh, W_=W_: e.tensor_tensor(out=oth[:, :, a:W_ - a], in0=cur[:, :, a - sh:W_ - a - sh],
                                                                             in1=cur[:, :, a + sh:W_ - a + sh], op=ALU.add), cur_r, oth_r)
                    cur, cur_r, oth, oth_r = oth, oth_r, cur, cur_r
                    sh *= 2
                for c in range(2):
                    V(lambda e, cur=cur, c=c, nt=nt: e.tensor_tensor(out=tmp[c].t[:, :nt], in0=cur[:, c, 8:8 + nt], in1=icv[:, :nt], op=ALU.mult),
                      cur_r + icv_r, [tmp[c].r])
                    V(lambda e, c=c, U_=U_, nt=nt: e.tensor_tensor(out=plT[:, c, :nt], in0=tmp[c].t[:, :nt], in1=U_[:, c, 8:8 + nt], op=ALU.subtract),
                      [tmp[c].r] + U_r, plT_r)
                for dc in range(2):
                    pt = PSF[dc]
                    for kc in range(2):
                        mm(pt, pt.t[:, :nt], wpool.t[:, g, kc, dc * 128:(dc + 1) * 128], plT[:, kc, :nt], [wpool.r] + plT_r, kc == 0, kc == 1)
                    V(lambda e, pt=pt, g=g, dc=dc, nt=nt: e.tensor_scalar(out=hxT.t[:, 8 + 2 * g + dc, :nt], in0=pt.t[:, :nt],
                                                                  scalar1=smallp.t[:, 6 + 2 * g + dc:7 + 2 * g + dc], scalar2=None, op0=ALU.mult),
                      [pt.r, smallp.r], [hxT.r])
            proj_out_resid(hxT, nt, woeb[i], mi)


    finals = []
    for l in range(depth):
        last = l == depth - 1
        even = l % 2 == 0
        i = l // 2
        if even and cfg_mixers[0]:
            even_mixer(l, i, last)
        if (not even) and cfg_mixers[1]:
            odd_mixer(l, i, last)
        finals = mlp_phase(l, final=last)
    nops, nwait = P.finish(finals)
    return nc, nops, nwait


def prep_core_inputs(cfg, inp, b, tables):
    KC = cfg.KC
    depth = cfg.depth

    def colT(v, n):
        v = np.asarray(v, np.float32)
        return np.ascontiguousarray(np.swapaxes(v.reshape(v.shape[:-1] + (n, 128)), -1, -2))
    m = {}
    m["x"] = np.ascontiguousarray(inp["x"][b], dtype=np.float32)
    m["ctx"] = np.ascontiguousarray(inp["ctx"][b], dtype=np.float32)
    c2 = np.stack([np.asarray(inp["c"][b], np.float32), np.asarray(inp["c_ctx"], np.float32)])
    m["c2T"] = np.ascontiguousarray(c2.reshape(2, KC, 128).transpose(2, 1, 0))
    m["w_mod"] = np.asarray(inp["w_mod"], np.float32)
    bm = np.asarray(inp["b_mod"], np.float32)
    D = cfg.D
    m["b_modT"] = colT(bm, 6 * KC)
    m["b_modg"] = np.ascontiguousarray(np.stack([bm[:, 2 * D:3 * D], bm[:, 5 * D:6 * D]], axis=1))
    m["norm1T"] = colT(inp["norm1"], KC)
    m["norm2T"] = colT(inp["norm2"], KC)
    m["w_in"] = np.asarray(inp["w_in"], np.float32)
    m["q_normT"] = colT(inp["q_norm"], 4)
    m["w_uq"] = np.asarray(inp["w_uq"], np.float32)
    m["kv_normT"] = colT(inp["kv_norm"], 2)
    m["w_ukv"] = np.asarray(inp["w_ukv"], np.float32)
    m["w_pool"] = np.asarray(inp["w_pool"], np.float32)
    m["pool_scaleT"] = colT(inp["pool_scale"], 8)
    m["w_out_even"] = np.asarray(inp["w_out_even"], np.float32)
    m["w_out_odd"] = np.asarray(inp["w_out_odd"], np.float32)
    m["w_mlp1"] = np.asarray(inp["w_mlp1"], np.float32)
    m["w_mlp2"] = np.asarray(inp["w_mlp2"], np.float32)
    m["final_norm"] = np.asarray(inp["final_norm"], np.float32)
    m.update(tables)
    return m


_CACHE = {}


def kernel(**inputs):
    cfg = Cfg()
    if "prog" not in _CACHE:
        _CACHE["prog"] = build(cfg)[0]
        _CACHE["tables"] = host_tables(cfg)
    nc = _CACHE["prog"]
    tables = _CACHE["tables"]
    inp = {k: np.asarray(v) for k, v in inputs.items()}
    B = inp["x"].shape[0]
    maps = [prep_core_inputs(cfg, inp, b, tables) for b in range(B)]
    in_maps = [maps[c % B] for c in range(8)]
    res = run_bass_kernel_spmd(nc, in_maps, core_ids=list(range(8)))
    out = np.stack([np.asarray(res.results[b]["out"], dtype=np.float32) for b in range(B)], axis=0)
    return out
```

```python
import math
import numpy as np
import concourse.bass as bass
import concourse.mybir as mybir
from concourse.bass_utils import run_bass_kernel_spmd

F32 = mybir.dt.float32
BF16 = mybir.dt.bfloat16
AF = mybir.ActivationFunctionType
ALU = mybir.AluOpType
AX = mybir.AxisListType

H = 8; QR_ = 512; KVR = 256; NOPE = 128; ROPE = 64; VH = 128; QKH = 192
EPS = 1e-6
SCALE = QKH ** -0.5


class Cfg:
    def __init__(self, N=8192, L=256, D=2048, DFF=8192, depth=4):
        self.N = N; self.L = L; self.D = D; self.DFF = DFF; self.depth = depth
        self.T = N + L
        self.N2 = N // 128
        self.KC = D // 128
        self.FC = DFF // 128
        self.PW = D // 2
        self.IN = QR_ + KVR + ROPE + self.PW


class Res:
    __slots__ = ("last_w", "reads")

    def __init__(self):
        self.last_w = None
        self.reads = []


class Op:
    __slots__ = ("eng", "emit", "deps", "is_dma", "needed", "semval", "is_mm")

    def __init__(self, eng, emit, deps, is_dma, is_mm):
        self.eng = eng; self.emit = emit; self.deps = deps
        self.is_dma = is_dma; self.is_mm = is_mm
        self.needed = False; self.semval = None


class Prog:
    ENGS = ("tensor", "vector", "scalar", "gpsimd", "sync")

    def __init__(self, nc):
        self.nc = nc
        self.ops = []

    def op(self, eng, emit, reads=(), writes=(), is_dma=False, is_mm=False):
        idx = len(self.ops)
        deps = set()
        for r in reads:
            if r.last_w is not None:
                deps.add(r.last_w)
        for w in writes:
            if w.last_w is not None:
                deps.add(w.last_w)
            deps.update(w.reads)
        self.ops.append(Op(eng, emit, deps, is_dma, is_mm))
        for r in reads:
            r.reads.append(idx)
        for w in writes:
            w.last_w = idx
            w.reads = []
        return idx

    def finish(self, final_ops):
        nc = self.nc
        ops = self.ops
        for o in ops:
            if o.is_dma:
                o.needed = True
            if o.is_mm:
                o.deps = {d for d in o.deps if not ops[d].is_mm}
            for d in o.deps:
                ops[d].needed = True
        for f in final_ops:
            ops[f].needed = True
        self._cms = []

        def newsem(name):
            cm = nc.semaphore(name)
            s = cm.__enter__()
            self._cms.append(cm)
            return s
        NDMA = 20
        dma_sems = {e: [newsem(f"d{e}{i}") for i in range(NDMA)] for e in ("sync", "gpsimd", "scalar")}
        dma_rr = {e: 0 for e in dma_sems}
        dma_cnt = {}
        dma_last = {}
        sems = {e: newsem(f"c{e}") for e in self.ENGS}
        cnt = {e: 0 for e in self.ENGS}
        for i, o in enumerate(ops):
            if not o.needed:
                continue
            if o.is_dma:
                pool = dma_sems[o.eng]
                k = dma_rr[o.eng] % NDMA
                dma_rr[o.eng] += 1
                s = pool[k]
                key = (o.eng, k)
                if key in dma_last:
                    o.deps.add(dma_last[key])
                dma_last[key] = i
                dma_cnt[key] = dma_cnt.get(key, 0) + 16
                o.semval = (s, dma_cnt[key], 16, key)
            else:
                cnt[o.eng] += 1
                o.semval = (sems[o.eng], cnt[o.eng], 1, o.eng)
        known = {e: {} for e in self.ENGS}
        engobj = {"tensor": nc.tensor, "vector": nc.vector, "scalar": nc.scalar,
                  "gpsimd": nc.gpsimd, "sync": nc.sync}
        nwait = 0
        for o in ops:
            e = engobj[o.eng]
            need = {}
            for d in o.deps:
                s, v, _, key = ops[d].semval
                if key not in need or need[key][1] < v:
                    need[key] = (s, v)
            kn = known[o.eng]
            for key, (s, v) in need.items():
                if kn.get(key, 0) >= v:
                    continue
                e.wait_ge(s, v)
                kn[key] = v
                nwait += 1
            ins = o.emit(e)
            if o.needed:
                s, v, inc, key = o.semval
                ins.then_inc(s, inc)
        for f in final_ops:
            s, v, _, _ = ops[f].semval
            nc.sync.wait_ge(s, v)
        return len(ops), nwait


class Tl:
    def __init__(self, t, nres=1):
        self.t = t
        self.rs = [Res() for _ in range(nres)]

    @property
    def r(self):
        return self.rs[0]


def host_tables(cfg):
    N, L, N2 = cfg.N, cfg.L, cfg.N2
    tb = {}
    c = np.arange(512)
    ang = 2 * np.pi * np.outer(c, c) / 512
    tb["t_cch"] = (np.cos(ang) / math.sqrt(512)).astype(np.float32)
    tb["t_nsch"] = (-np.sin(ang) / math.sqrt(512)).astype(np.float32)
    a = np.arange(128)
    ang = 2 * np.pi * np.outer(a, a) / 128
    tb["t_c128"] = np.cos(ang).astype(np.float32)
    tb["t_s128"] = np.sin(ang).astype(np.float32)
    tb["t_ns128"] = (-np.sin(ang)).astype(np.float32)
    n2 = np.arange(N2)
    ang = 2 * np.pi * np.outer(a, n2) / N
    tb["t_twr"] = (np.cos(ang) / math.sqrt(N)).astype(np.float32)
    tb["t_twi"] = (-np.sin(ang) / math.sqrt(N)).astype(np.float32)
    ang = 2 * np.pi * np.outer(n2, n2) / N2
    tb["t_cs"] = np.concatenate([np.cos(ang), np.sin(ang)], 0).astype(np.float32)
    l = np.arange(L)
    ang = 2 * np.pi * np.outer(l, l) / L
    tb["t_cl"] = (np.cos(ang) / math.sqrt(L)).astype(np.float32)
    tb["t_sl"] = (np.sin(ang) / math.sqrt(L)).astype(np.float32)
    GRID_W = 64
    rows = N // GRID_W
    r = np.broadcast_to(np.arange(rows, dtype=np.float32)[:, None], (rows, GRID_W)).reshape(N)
    col = np.broadcast_to(np.arange(GRID_W, dtype=np.float32)[None, :], (rows, GRID_W)).reshape(N)
    inv = (10000.0 ** (-2.0 * np.arange(16, dtype=np.float32) / 32)).astype(np.float32)
    ang = np.stack([r[:, None] * inv, col[:, None] * inv], axis=1)
    ang = np.broadcast_to(ang[:, :, None, :], (N, 2, 2, 16)).reshape(N, 64)
    tb["t_cos"] = np.ascontiguousarray(np.cos(ang).T).astype(np.float32)
    tb["t_sin"] = np.ascontiguousarray(np.sin(ang).T).astype(np.float32)
    invc = np.zeros((4, cfg.T), np.float32)
    for gi, w in enumerate((2, 4, 8, 16)):
        for off, n in ((0, L), (L, N)):
            t = np.arange(n)
            lo = np.clip(t - w // 2, 0, n); hi = np.clip(t + w // 2, 0, n)
            invc[gi, off:off + n] = 1.0 / (hi - lo)
    tb["t_invc"] = invc
    tb["t_ident"] = np.eye(128, dtype=np.float32)
    return tb


def build(cfg, cfg_mixers=(True, True)):
    N, L, D, DFF, T, N2, KC, FC, PW = cfg.N, cfg.L, cfg.D, cfg.DFF, cfg.T, cfg.N2, cfg.KC, cfg.FC, cfg.PW
    depth = cfg.depth
    NE = (depth + 1) // 2; NO = depth // 2
    nc = bass.Bass("TRN2", target_bir_lowering=False)
    P = Prog(nc)

    def din(name, shape):
        return nc.dram_tensor(name, list(shape), F32, kind="ExternalInput")

    x_in = din("x", [N, D]); ctx_in = din("ctx", [L, D])
    c2T = din("c2T", [128, KC, 2])
    w_mod = din("w_mod", [depth, D, 6 * D]); b_modT = din("b_modT", [depth, 128, 6 * KC])
    b_modg = din("b_modg", [depth, 2, D])
    n1T = din("norm1T", [depth, 128, KC]); n2T = din("norm2T", [depth, 128, KC])
    w_in = din("w_in", [NE, D, cfg.IN]); qnT = din("q_normT", [NE, 128, 4]); w_uq = din("w_uq", [NE, QR_, H * QKH])
    kvnT = din("kv_normT", [NE, 128, 2]); w_ukv = din("w_ukv", [NE, KVR, H * 256])
    w_pool = din("w_pool", [NE, 4, 256, 256]); pscT = din("pool_scaleT", [NE, 128, 8])
    w_oute = din("w_out_even", [NE, D, D]); w_outo = din("w_out_odd", [max(NO, 1), D, D])
    w1 = din("w_mlp1", [depth, D, DFF]); w2 = din("w_mlp2", [depth, DFF, D])
    fin = din("final_norm", [D])
    tbl_shapes = {k: v.shape for k, v in host_tables(cfg).items()}
    tb = {k: din(k, s) for k, s in tbl_shapes.items()}
    out = nc.dram_tensor("out", [N, D], F32, kind="ExternalOutput")

    def dsc(name, shape, dt=BF16, nres=1):
        return Tl(nc.dram_tensor(name, list(shape), dt), nres)
    MT = [(0, L)] + [(L + i * 512, 512) for i in range(N // 512)]
    NMT = len(MT)
    XR = dsc("XR", [T, D], F32, NMT)
    w1b = [dsc(f"w1b{l}", [D, DFF]) for l in range(depth)]
    w2b = [dsc(f"w2b{l}", [DFF, D]) for l in range(depth)]
    winb = [dsc(f"winb{i}", [D, cfg.IN]) for i in range(NE)]
    woeb = [dsc(f"woeb{i}", [D, D]) for i in range(NE)]
    woob = [dsc(f"woob{i}", [D, D]) for i in range(NO)]
    GROW = dsc("GROW", [depth, 2, 2, D], F32)
    QN = dsc("QN", [H, 128, T], BF16, NMT); QRd = dsc("QRd", [H, 64, T], BF16, NMT)
    KN = dsc("KN", [H, 128, T], BF16, NMT); KRd = dsc("KRd", [64, T], BF16, NMT)
    VV = dsc("VV", [T, H * 128], BF16, NMT); UU = dsc("UU", [PW, T], F32, NMT)
    AT = dsc("AT", [H * 128, T], BF16, NMT)
    Z0 = dsc("Z0", [T, 2, D], BF16, NMT)
    Z1 = dsc("Z1", [2, N2, 128, D], BF16, N2)

    def sb(name, shape, dt=F32, nres=1):
        return Tl(nc.alloc_sbuf_tensor(name, list(shape), dt), nres)

    def ps(name, shape, dt=F32):
        return Tl(nc.alloc_psum_tensor(name, list(shape), dt))
    ident = sb("ident", [128, 128], BF16)
    ones = sb("ones", [128, 128], BF16)
    xt = [sb(f"xt{i}", [128, D]) for i in range(2)]
    xn = [sb(f"xn{i}", [128, D], BF16) for i in range(2)]
    hxT = sb("hxT", [128, KC, 512], BF16)
    BIG = sb("BIG", [128, 64, 512], BF16, 64)
    WR = [sb(f"WR{i}", [128, 16, 512], BF16) for i in range(3)]
    wctr = [0]

    def wnext():
        wctr[0] += 1
        return WR[wctr[0] % 3]
    Gt = sb("Gt", [128, D])
    tmp = [sb(f"tmp{i}", [128, 512]) for i in range(3)]
    stat = sb("stat", [128, 8])
    sc2 = sb("sc2", [128, KC, 2], BF16)
    PSF = [ps(f"psf{i}", [128, 512]) for i in range(6)]
    PST = ps("pst", [128, 2048], BF16)

    def dma(eng, out_ap, in_ap, reads, writes):
        return P.op(eng, lambda e: e.dma_start(out=out_ap, in_=in_ap), reads, writes, is_dma=True)

    def mm(o, oap, lap, rap, reads, start, stop):
        return P.op("tensor", lambda e: e.matmul(oap, lap, rap, start=start, stop=stop), reads, [o.r], is_mm=True)

    def V(fn, reads, writes):
        return P.op("vector", fn, reads, writes)

    def A(fn, reads, writes):
        return P.op("scalar", fn, reads, writes)

    def G(fn, reads, writes):
        return P.op("gpsimd", fn, reads, writes)

    ext = Res()

    dma("gpsimd", ident.t[:], tb["t_ident"][:, :], [ext], [ident.r])
    G(lambda e: e.memset(ones.t[:], 1.0), [], [ones.r])
    epst = sb("epst", [128, 1])
    G(lambda e: e.memset(epst.t[:], EPS), [], [epst.r])
    for mi, (r0, nt) in enumerate(MT):
        src = ctx_in[0:L, :] if mi == 0 else x_in[r0 - L:r0 - L + nt, :]
        dma("sync", XR.t[r0:r0 + nt, :], src, [ext], [XR.rs[mi]])

    def cast_w(dst, src_ap, rows, cols):
        step = max(1, min(rows, (1 << 22) // cols))
        for r0 in range(0, rows, step):
            r1 = min(rows, r0 + step)
            dma("gpsimd", dst.t[r0:r1, :], src_ap[r0:r1, :], [ext], [dst.r])

    c2s = sb("c2s", [128, KC, 2])
    dma("sync", c2s.t[:], c2T[:, :, :], [ext], [c2s.r])
    A(lambda e: e.activation(out=sc2.t[:], in_=c2s.t[:], func=AF.Silu), [c2s.r], [sc2.r])
    bmT = sb("bmT", [128, 6 * KC])
    bmg = sb("bmg", [2, 512])
    nrm = sb("nrm", [128, 2, KC])
    ABs = []
    for l in range(depth):
        ABl = sb(f"AB{l}", [128, 4, KC, 2])
        MODl = sb(f"MODT{l}", [128, 6 * KC, 2])
        ABs.append((ABl, MODl))
        dma("sync", bmT.t[:], b_modT[l, :, :], [ext], [bmT.r])
        dma("sync", nrm.t[:, 0, :], n1T[l, :, :], [ext], [nrm.r])
        dma("sync", nrm.t[:, 1, :], n2T[l, :, :], [ext], [nrm.r])
        for cb in range(6 * D // 512):
            wt = wnext()
            dma("gpsimd", wt.t[:], w_mod[l].rearrange("(kc p) c -> p kc c", p=128)[:, :, cb * 512:(cb + 1) * 512],
                [ext], [wt.r])
            pm = PSF[cb % 2]
            for j in range(4):
                ch = cb * 4 + j
                for kc in range(KC):
                    mm(pm, pm.t[:, j * 2:j * 2 + 2], wt.t[:, kc, j * 128:(j + 1) * 128], sc2.t[:, kc, :],
                       [wt.r, sc2.r], kc == 0, kc == KC - 1)
            V(lambda e, pm=pm, cb=cb, MODl=MODl: e.tensor_tensor(
                out=MODl.t[:, cb * 4:cb * 4 + 4, :], in0=pm.t[:, 0:8].rearrange("p (j r) -> p j r", r=2),
                in1=bmT.t[:, cb * 4:cb * 4 + 4].unsqueeze(2).to_broadcast([128, 4, 2]), op=ALU.add),
              [pm.r, bmT.r], [MODl.r])
            gi = None
            if 2 * KC <= cb * 4 < 3 * KC:
                gi = 0; c0 = cb * 512 - 2 * D
            elif 5 * KC <= cb * 4 < 6 * KC:
                gi = 1; c0 = cb * 512 - 5 * D
            if gi is not None:
                pg = PSF[2 + cb % 2]
                for kc in range(KC):
                    mm(pg, pg.t[0:2, :], sc2.t[:, kc, :], wt.t[:, kc, :], [wt.r, sc2.r], kc == 0, kc == KC - 1)
                gt = tmp[cb % 2]
                dma("sync", bmg.t[:], b_modg[l, gi, c0:c0 + 512].partition_broadcast(2), [ext], [bmg.r])
                V(lambda e, pg=pg, gt=gt: e.tensor_tensor(
                    out=gt.t[0:2, :], in0=pg.t[0:2, :], in1=bmg.t[0:2, :], op=ALU.add),
                  [pg.r, bmg.r], [gt.r])
                dma("sync", GROW.t[l, gi, :, c0:c0 + 512], gt.t[0:2, :], [gt.r], [GROW.r])
        for (ai, sci, shi, ni) in ((0, 1, 0, 0), (2, 4, 3, 1)):
            V(lambda e, ABl=ABl, MODl=MODl, ai=ai, sci=sci, ni=ni: e.scalar_tensor_tensor(
                out=ABl.t[:, ai, :, :], in0=MODl.t[:, sci * KC:(sci + 1) * KC, :], scalar=1.0,
                in1=nrm.t[:, ni, :].unsqueeze(2).to_broadcast([128, KC, 2]), op0=ALU.add, op1=ALU.mult),
              [MODl.r, nrm.r], [ABl.r])
            V(lambda e, ABl=ABl, MODl=MODl, ai=ai, shi=shi: e.tensor_copy(
                out=ABl.t[:, ai + 1, :, :], in_=MODl.t[:, shi * KC:(shi + 1) * KC, :]), [MODl.r], [ABl.r])

    for l in range(depth):
        cast_w(w1b[l], w1[l], D, DFF)
        cast_w(w2b[l], w2[l], DFF, D)
    for i in range(NE):
        cast_w(winb[i], w_in[i], D, cfg.IN)
        cast_w(woeb[i], w_oute[i], D, D)
    for i in range(NO):
        cast_w(woob[i], w_outo[i], D, D)

    def norm_to_hxT(l, mi, which, src=None):
        r0, nt = MT[mi]
        row = 1 if mi == 0 else 0
        ABl = ABs[l][0]
        ai = 0 if which == 1 else 2
        for s in range(nt // 128):
            x_ = xt[s % 2]; xn_ = xn[s % 2]
            dma("sync", x_.t[:], XR.t[r0 + s * 128:r0 + (s + 1) * 128, :], [XR.rs[mi]], [x_.r])
            st = stat
            A(lambda e, x_=x_, xn_=xn_: e.activation(out=xn_.t[:], in_=x_.t[:], func=AF.Square, accum_out=stat.t[:, 0:1]),
              [x_.r], [xn_.r, st.r])
            A(lambda e: e.activation(out=stat.t[:, 1:2], in_=stat.t[:, 0:1], func=AF.Sqrt, bias=epst.t[:, 0:1], scale=1.0 / D),
              [st.r, epst.r], [st.r])
            V(lambda e: e.reciprocal(out=stat.t[:, 2:3], in_=stat.t[:, 1:2]), [st.r], [st.r])
            V(lambda e, x_=x_, xn_=xn_: e.tensor_scalar(out=xn_.t[:], in0=x_.t[:], scalar1=stat.t[:, 2:3], scalar2=None,
                                                        op0=ALU.mult), [x_.r, st.r], [xn_.r])
            for kc in range(KC):
                P.op("tensor", lambda e, kc=kc, xn_=xn_: e.transpose(PST.t[:, kc * 128:(kc + 1) * 128],
                                                                    xn_.t[:, kc * 128:(kc + 1) * 128], ident.t[:]),
                     [xn_.r, ident.r], [PST.r], is_mm=True)
            for kc in range(KC):
                V(lambda e, kc=kc, s=s: e.tensor_scalar(
                    out=hxT.t[:, kc, s * 128:(s + 1) * 128], in0=PST.t[:, kc * 128:(kc + 1) * 128],
                    scalar1=ABl.t[:, ai, kc, row:row + 1], scalar2=ABl.t[:, ai + 1, kc, row:row + 1],
                    op0=ALU.mult, op1=ALU.add), [PST.r, ABl.r], [hxT.r])

    def G_or_A(fn, reads, writes):
        return V(fn, reads, writes)

    def load_G(l, gi, row):
        dma("sync", Gt.t[:], GROW.t[l, gi, row, :].partition_broadcast(128), [GROW.r], [Gt.r])

    def resid_update(mi, s, j, pt, rows_aps=None, xres=None):
        if xres is None:
            xres = [XR.rs[mi]]
        xb = tmp[(s * 4 + j) % 2]
        if rows_aps is None:
            r0, nt = MT[mi]
            rows_aps = [(0, 128, XR.t[r0 + s * 128:r0 + (s + 1) * 128, j * 512:(j + 1) * 512])]
        for (p0, pn, ap) in rows_aps:
            dma("sync", xb.t[p0:p0 + pn, :], ap, xres, [xb.r])
        t2 = tmp[2]
        V(lambda e: e.tensor_tensor(out=t2.t[:], in0=pt.t[:], in1=Gt.t[:, j * 512:(j + 1) * 512], op=ALU.mult),
          [pt.r, Gt.r], [t2.r])
        V(lambda e: e.tensor_tensor(out=xb.t[:], in0=xb.t[:], in1=t2.t[:], op=ALU.add), [xb.r, t2.r], [xb.r])
        last = None
        for (p0, pn, ap) in rows_aps:
            last = dma("gpsimd", ap, xb.t[p0:p0 + pn, :], [xb.r], xres)
        return last

    def proj_out_resid(srcT, nt, wdram, mi=None, rows_fn=None, xres=None):
        for j in range(D // 512):
            wt = wnext()
            dma("sync", wt.t[:], wdram.t.ap().rearrange("(kc p) c -> p kc c", p=128)[:, :, j * 512:(j + 1) * 512],
                [wdram.r], [wt.r])
            for s in range(nt // 128):
                pt = PSF[(s + j) % 2]
                for kc in range(KC):
                    mm(pt, pt.t[:], srcT.t[:, kc, s * 128:(s + 1) * 128], wt.t[:, kc, :], [srcT.r, wt.r], kc == 0, kc == KC - 1)
                resid_update(mi, s, j, pt, None if rows_fn is None else rows_fn(s, j), xres)

    def mlp_phase(l, final=False):
        finals = []
        if final:
            fg = sb("fing", [128, D])
            dma("sync", fg.t[:], fin.ap().partition_broadcast(128), [ext], [fg.r])
        for mi, (r0, nt) in enumerate(MT):
            if final and mi == 0:
                continue
            if mi <= 1:
                load_G(l, 1, 1 if mi == 0 else 0)
            norm_to_hxT(l, mi, 2)
            for fb in range(DFF // 512):
                wt = wnext()
                dma("sync", wt.t[:], w1b[l].t.ap().rearrange("(kc p) f -> p kc f", p=128)[:, :, fb * 512:(fb + 1) * 512],
                    [w1b[l].r], [wt.r])
                for j in range(4):
                    fc = fb * 4 + j
                    pt = PSF[fc % 2]
                    for kc in range(KC):
                        mm(pt, pt.t[:, :nt], wt.t[:, kc, j * 128:(j + 1) * 128], hxT.t[:, kc, :nt], [wt.r, hxT.r], kc == 0, kc == KC - 1)
                    tq = tmp[fc % 2]
                    A(lambda e, pt=pt, tq=tq, nt=nt: e.activation(out=tq.t[:, :nt], in_=pt.t[:, :nt], func=AF.Relu), [pt.r], [tq.r])
                    V(lambda e, tq=tq, fc=fc, nt=nt: e.tensor_tensor(out=BIG.t[:, fc, :nt], in0=tq.t[:, :nt], in1=tq.t[:, :nt], op=ALU.mult),
                      [tq.r], [BIG.rs[fc]])
            nsub = nt // 128
            for j in range(D // 512):
                for fg_ in range(FC // 16):
                    wt = wnext()
                    dma("sync", wt.t[:], w2b[l].t.ap().rearrange("(fc p) d -> p fc d", p=128)[:, fg_ * 16:(fg_ + 1) * 16, j * 512:(j + 1) * 512],
                        [w2b[l].r], [wt.r])
                    for f in range(16):
                        fc = fg_ * 16 + f
                        for s in range(nsub):
                            pt = PSF[2 + s]
                            mm(pt, pt.t[:], BIG.t[:, fc, s * 128:(s + 1) * 128], wt.t[:, f, :], [BIG.rs[fc], wt.r], fc == 0, fc == FC - 1)
                for s in range(nsub):
                    pt = PSF[2 + s]
                    resid_update(mi, s, j, pt)
            if final:
                for s in range(nsub):
                    x_ = xt[s % 2]; xj = xn[s % 2]
                    dma("sync", x_.t[:], XR.t[r0 + s * 128:r0 + (s + 1) * 128, :], [XR.rs[mi]], [x_.r])
                    A(lambda e, x_=x_, xj=xj: e.activation(out=xj.t[:], in_=x_.t[:], func=AF.Square, accum_out=stat.t[:, 4:5]),
                      [x_.r], [xj.r, stat.r])
                    A(lambda e: e.activation(out=stat.t[:, 5:6], in_=stat.t[:, 4:5], func=AF.Sqrt, bias=epst.t[:, 0:1], scale=1.0 / D),
                      [stat.r, epst.r], [stat.r])
                    V(lambda e: e.reciprocal(out=stat.t[:, 6:7], in_=stat.t[:, 5:6]), [stat.r], [stat.r])
                    V(lambda e, x_=x_: e.scalar_tensor_tensor(out=x_.t[:], in0=x_.t[:], scalar=stat.t[:, 6:7], in1=fg.t[:],
                                                              op0=ALU.mult, op1=ALU.mult), [x_.r, stat.r, fg.r], [x_.r])
                    finals.append(dma("gpsimd", out[r0 - L + s * 128:r0 - L + (s + 1) * 128, :], x_.t[:], [x_.r], []))
        return finals


    if NO > 0:
        cch = sb("cch", [128, 4, 512], BF16); nsch = sb("nsch", [128, 4, 512], BF16)
        dma("gpsimd", cch.t[:], tb["t_cch"].ap().rearrange("(kc p) m -> p kc m", p=128), [ext], [cch.r])
        dma("gpsimd", nsch.t[:], tb["t_nsch"].ap().rearrange("(kc p) m -> p kc m", p=128), [ext], [nsch.r])
        c128 = sb("c128", [128, 128], BF16); s128 = sb("s128", [128, 128], BF16); ns128 = sb("ns128", [128, 128], BF16)
        dma("gpsimd", c128.t[:], tb["t_c128"][:, :], [ext], [c128.r])
        dma("gpsimd", s128.t[:], tb["t_s128"][:, :], [ext], [s128.r])
        dma("gpsimd", ns128.t[:], tb["t_ns128"][:, :], [ext], [ns128.r])
        twr = sb("twr", [128, N2]); twi = sb("twi", [128, N2])
        dma("sync", twr.t[:], tb["t_twr"][:, :], [ext], [twr.r])
        dma("sync", twi.t[:], tb["t_twi"][:, :], [ext], [twi.r])
        cs = sb("cs", [2 * N2, N2], BF16)
        dma("gpsimd", cs.t[:], tb["t_cs"][:, :], [ext], [cs.r])
        cl = sb("cl", [128, L // 128, L], BF16); sl = sb("sl", [128, L // 128, L], BF16)
        dma("gpsimd", cl.t[:], tb["t_cl"].ap().rearrange("(a p) k -> p a k", p=128), [ext], [cl.r])
        dma("gpsimd", sl.t[:], tb["t_sl"].ap().rearrange("(a p) k -> p a k", p=128), [ext], [sl.r])

    def zreg(b):
        ap = BIG.t[:, 8 * b:8 * b + 8, :].rearrange("p (r c) f -> p r (c f)", r=2)
        return ap, BIG.rs[8 * b:8 * b + 8]

    def evac(i, out_ap, in_ap, reads, writes):
        if i % 2 == 0:
            V(lambda e: e.tensor_copy(out=out_ap, in_=in_ap), reads, writes)
        else:
            A(lambda e: e.activation(out=out_ap, in_=in_ap, func=AF.Copy), reads, writes)

    def odd_mixer(l, i, last):
        KS = 128 // N2
        xlat = XR.rs[1:]
        for mi, (r0, nt) in enumerate(MT):
            if last and mi == 0:
                continue
            norm_to_hxT(l, mi, 1)
            for s in range(nt // 128):
                zap, zrs = zreg(s % 2)
                for ri in range(2):
                    tabl = cch if ri == 0 else nsch
                    for g in range(4):
                        pt = PSF[(ri * 4 + g) % 2]
                        for kc in range(4):
                            mm(pt, pt.t[:], hxT.t[:, 4 * g + kc, s * 128:(s + 1) * 128], tabl.t[:, kc, :], [hxT.r, tabl.r], kc == 0, kc == 3)
                        evac(g, zap[:, ri, g * 512:(g + 1) * 512], pt.t[:], [pt.r], zrs)
                dma("gpsimd", Z0.t[r0 + s * 128:r0 + (s + 1) * 128, :, :], zap, zrs, [Z0.rs[mi]])
        z0lat = Z0.t[L:L + N, :, :].rearrange("(n1 n2) r d -> n1 n2 r d", n2=N2)
        for n2 in range(N2):
            zin, zirs = zreg(n2 % 2)
            zo, zors = zreg(2 + n2 % 2)
            dma("sync", zin, z0lat[:, n2, :, :], Z0.rs[1:], zirs)
            for cb in range(4):
                sl_ = slice(cb * 512, (cb + 1) * 512)
                yr = PSF[0 + 2 * (cb % 2)]; yi = PSF[1 + 2 * (cb % 2)]
                mm(yr, yr.t[:], c128.t[:], zin[:, 0, sl_], [c128.r] + zirs, True, False)
                mm(yr, yr.t[:], s128.t[:], zin[:, 1, sl_], [s128.r] + zirs, False, True)
                mm(yi, yi.t[:], c128.t[:], zin[:, 1, sl_], [c128.r] + zirs, True, False)
                mm(yi, yi.t[:], ns128.t[:], zin[:, 0, sl_], [ns128.r] + zirs, False, True)
                t1 = tmp[0]; t2 = tmp[1]
                A(lambda e, yi=yi, n2=n2: e.activation(out=t1.t[:], in_=yi.t[:], func=AF.Copy, scale=twi.t[:, n2:n2 + 1]),
                  [yi.r, twi.r], [t1.r])
                A(lambda e, yi=yi, n2=n2: e.activation(out=t2.t[:], in_=yi.t[:], func=AF.Copy, scale=twr.t[:, n2:n2 + 1]),
                  [yi.r, twr.r], [t2.r])
                V(lambda e, yr=yr, n2=n2, sl_=sl_, zo=zo: e.scalar_tensor_tensor(
                    out=zo[:, 0, sl_], in0=yr.t[:], scalar=twr.t[:, n2:n2 + 1], in1=t1.t[:], op0=ALU.mult, op1=ALU.subtract),
                  [yr.r, twr.r, t1.r], zors)
                V(lambda e, yr=yr, n2=n2, sl_=sl_, zo=zo: e.scalar_tensor_tensor(
                    out=zo[:, 1, sl_], in0=yr.t[:], scalar=twi.t[:, n2:n2 + 1], in1=t2.t[:], op0=ALU.mult, op1=ALU.add),
                  [yr.r, twi.r, t2.r], zors)
            for ri in range(2):
                dma("gpsimd", Z1.t[ri, n2, :, :], zo[:, ri, :], zors, [Z1.rs[n2]])
        load_G(l, 0, 0)
        z1v = Z1.t.ap().rearrange("r n k d -> (r n) k d")
        xrl = XR.t[L:L + N, :].rearrange("(k2 k1) d -> k1 k2 d", k1=128)
        for m in range(128 // (4 * KS)):
            for s in range(4):
                for k1l in range(KS):
                    k1 = (4 * m + s) * KS + k1l
                    z3 = xn[k1l % 2]
                    dma("sync", z3.t[0:2 * N2, :], z1v[:, k1, :], Z1.rs, [z3.r])
                    for kc in range(KC):
                        pt = PSF[2 + kc // 4]
                        c0 = (kc % 4) * 128 + k1l * N2
                        mm(pt, pt.t[:, c0:c0 + N2], z3.t[0:2 * N2, kc * 128:(kc + 1) * 128], cs.t[:, :], [z3.r, cs.r], True, True)
                for b in range(4):
                    pt = PSF[2 + b]
                    evac(b, hxT.t[:, 4 * b:4 * b + 4, s * 128:(s + 1) * 128], pt.t[:].rearrange("p (c t) -> p c t", c=4), [pt.r], [hxT.r])

            def rows_fn(s, j, m=m):
                return [(k1l * N2, N2, xrl[(4 * m + s) * KS + k1l, :, j * 512:(j + 1) * 512]) for k1l in range(KS)]
            proj_out_resid(hxT, 512, woob[i], None, rows_fn, xlat)
        if not last:
            load_G(l, 0, 1)
            zc = []
            for a in range(L // 128):
                zap, zrs = zreg(a)
                dma("sync", zap, Z0.t[a * 128:(a + 1) * 128, :, :], [Z0.rs[0]], zrs)
                zc.append((zap, zrs))
            for kc in range(KC):
                pt = PSF[kc % 2]
                n = 0
                for a in range(L // 128):
                    zap, zrs = zc[a]
                    for ri, tabl in ((0, cl), (1, sl)):
                        mm(pt, pt.t[:, :L], zap[:, ri, kc * 128:(kc + 1) * 128], tabl.t[:, a, :], zrs + [tabl.r], n == 0, n == 2 * (L // 128) - 1)
                        n += 1
                evac(kc, hxT.t[:, kc, 0:L], pt.t[:, :L], [pt.r], [hxT.r])
            proj_out_resid(hxT, L, woob[i], 0)

    wpool = sb("wpool", [128, 4, 2, 256], BF16)
    wrot = sb("wrot", [128, KC, 64], BF16)
    smallp = sb("smallp", [128, 16])
    qm = sb("qm", [128, 8]); km = sb("km", [128, 8]); nb = sb("nb", [128, 8])
    accs = [sb("dacc0", [128, 512]), sb("dacc1", [128, 512])]
    onesf = sb("onesf", [128, 128])
    G(lambda e: e.memset(onesf.t[:], 1.0), [], [onesf.r])

    def bv(c0, n):
        return BIG.t[:, c0:c0 + n, :], BIG.rs[c0:c0 + n]

    def bvf(c0, n):
        return BIG.t[:, c0:c0 + n, :].rearrange("p a b -> p (a b)").bitcast(F32), BIG.rs[c0:c0 + n]

    def even_mixer(l, i, last):
        wuq, wuq_r = bv(0, 12); wuq = wuq.rearrange("p a b -> p (a b)").rearrange("p (k c) -> p k c", k=4)
        wuqrot, wuqrot_r = bv(12, 4)
        wukv, wukv_r = bv(16, 8); wukv = wukv.rearrange("p a b -> p (a b)").rearrange("p (k c) -> p k c", k=2)
        wv, wv_r = bv(24, 4); wv = wv.rearrange("p a b -> p (a b)").rearrange("p (k c) -> p k c", k=2)
        dma("gpsimd", wuq, w_uq[i].rearrange("(k p) c -> p k c", p=128), [ext], wuq_r)
        dma("gpsimd", wukv, w_ukv[i].rearrange("(k p) c -> p k c", p=128), [ext], wukv_r)
        dma("gpsimd", wpool.t[:], w_pool[i].rearrange("g (k p) d -> p g k d", p=128), [ext], [wpool.r])
        dma("sync", smallp.t[:, 0:4], qnT[i, :, :], [ext], [smallp.r])
        dma("sync", smallp.t[:, 4:6], kvnT[i, :, :], [ext], [smallp.r])
        dma("sync", smallp.t[:, 6:14], pscT[i, :, :], [ext], [smallp.r])
        for h in range(H):
            V(lambda e, h=h: e.tensor_copy(out=wv[:, :, h * 128:(h + 1) * 128], in_=wukv[:, :, h * 256 + 128:h * 256 + 256]),
              wukv_r, wv_r)
            for a in range(2):
                src0 = h * 192 + 128 + a * 32
                V(lambda e, h=h, a=a, src0=src0: e.tensor_scalar(out=wuqrot[:, :, h * 64 + a * 32:h * 64 + a * 32 + 16],
                                                               in0=wuq[:, :, src0 + 16:src0 + 32], scalar1=-1.0, scalar2=None, op0=ALU.mult),
                  wuq_r, wuqrot_r)
                V(lambda e, h=h, a=a, src0=src0: e.tensor_copy(out=wuqrot[:, :, h * 64 + a * 32 + 16:h * 64 + a * 32 + 32],
                                                             in_=wuq[:, :, src0:src0 + 16]), wuq_r, wuqrot_r)
        wb0 = wnext()
        winv = winb[i].t.ap().rearrange("(kc p) c -> p kc c", p=128)
        dma("sync", wb0.t[:, :, 0:320], winv[:, :, 512:832], [winb[i].r], [wb0.r])
        for a in range(2):
            V(lambda e, a=a: e.tensor_scalar(out=wrot.t[:, :, a * 32:a * 32 + 16], in0=wb0.t[:, :, 256 + a * 32 + 16:256 + a * 32 + 32],
                                            scalar1=-1.0, scalar2=None, op0=ALU.mult), [wb0.r], [wrot.r])
            V(lambda e, a=a: e.tensor_copy(out=wrot.t[:, :, a * 32 + 16:a * 32 + 32], in_=wb0.t[:, :, 256 + a * 32:256 + a * 32 + 16]),
              [wb0.r], [wrot.r])
        G(lambda e: e.memset(qm.t[:], 0.0), [], [qm.r])
        G(lambda e: e.memset(km.t[:], 0.0), [], [km.r])
        qlat, qlat_r = bvf(28, 8); qlat = qlat.rearrange("p (k t) -> p k t", k=4)
        kvlat, kvlat_r = bvf(36, 4); kvlat = kvlat.rearrange("p (k t) -> p k t", k=2)
        qn_, qn_r = bv(40, 4)
        kvn, kvn_r = bv(44, 2)
        sqb, sqb_r = bv(46, 2)
        cst, cst_r = bvf(48, 2); snt, snt_r = bvf(50, 2)
        stg = [bv(52 + k, 1) for k in range(8)]
        sctr = [0]

        def stage():
            sctr[0] += 1
            ap, rs = stg[sctr[0] % 8]
            return ap[:, 0, :], rs

        def rms_lat(lat, lat_r, nch, ncol0, out, out_r, nt, dim):
            pss = PSF[2]
            for c in range(nch):
                A(lambda e, c=c: e.activation(out=sqb[:, 0, :nt], in_=lat[:, c, :nt], func=AF.Square), lat_r, sqb_r)
                mm(pss, pss.t[:, :nt], ones.t[:], sqb[:, 0, :nt], [ones.r] + sqb_r, c == 0, c == nch - 1)
            A(lambda e: e.activation(out=tmp[0].t[:, :nt], in_=pss.t[:, :nt], func=AF.Sqrt, bias=epst.t[:, 0:1], scale=1.0 / dim),
              [pss.r, epst.r], [tmp[0].r])
            V(lambda e: e.reciprocal(out=tmp[1].t[:, :nt], in_=tmp[0].t[:, :nt]), [tmp[0].r], [tmp[1].r])
            for c in range(nch):
                V(lambda e, c=c: e.scalar_tensor_tensor(out=out[:, c, :nt], in0=lat[:, c, :nt], scalar=smallp.t[:, ncol0 + c:ncol0 + c + 1],
                                                        in1=tmp[1].t[:, :nt], op0=ALU.mult, op1=ALU.mult),
                  lat_r + [smallp.r, tmp[1].r], out_r)

        def rope_out(pr, prot, nt, is_lat, dst_ap, dst_res):
            st, st_r = stage()
            if is_lat:
                V(lambda e: e.tensor_tensor(out=tmp[0].t[0:64, :nt], in0=prot.t[0:64, :nt], in1=snt[0:64, :nt], op=ALU.mult),
                  [prot.r] + snt_r, [tmp[0].r])
                V(lambda e: e.tensor_tensor(out=tmp[1].t[0:64, :nt], in0=pr.t[0:64, :nt], in1=cst[0:64, :nt], op=ALU.mult),
                  [pr.r] + cst_r, [tmp[1].r])
                V(lambda e: e.tensor_tensor(out=st[0:64, :nt], in0=tmp[0].t[0:64, :nt], in1=tmp[1].t[0:64, :nt], op=ALU.add),
                  [tmp[0].r, tmp[1].r], st_r)
            else:
                V(lambda e: e.tensor_copy(out=st[0:64, :nt], in_=pr.t[0:64, :nt]), [pr.r], st_r)
            dma("gpsimd", dst_ap, st[0:64, :nt], st_r, dst_res)
            return st, st_r

        def upd_max(mx, h, nope, nope_r, rp, rp_r, nt):
            pn = PSF[3]
            A(lambda e: e.activation(out=sqb[:, 0, :nt], in_=nope[:, :nt], func=AF.Square), nope_r, sqb_r)
            A(lambda e: e.activation(out=sqb[0:64, 1, :nt], in_=rp[0:64, :nt], func=AF.Square), rp_r, sqb_r)
            mm(pn, pn.t[:, :nt], ones.t[:], sqb[:, 0, :nt], [ones.r] + sqb_r, True, False)
            mm(pn, pn.t[:, :nt], ones.t[0:64, :], sqb[0:64, 1, :nt], [ones.r] + sqb_r, False, True)
            V(lambda e: e.reduce_max(out=stat.t[:, 7:8], in_=pn.t[:, :nt], axis=AX.X), [pn.r], [stat.r])
            V(lambda e, h=h: e.tensor_max(out=mx.t[:, h:h + 1], in0=mx.t[:, h:h + 1], in1=stat.t[:, 7:8]), [mx.r, stat.r], [mx.r])

        for mi, (r0, nt) in enumerate(MT):
            is_lat = mi > 0
            norm_to_hxT(l, mi, 1)
            if is_lat:
                dma("sync", cst[0:64, :nt], tb["t_cos"][:, r0 - L:r0 - L + nt], [ext], cst_r)
                dma("sync", snt[0:64, :nt], tb["t_sin"][:, r0 - L:r0 - L + nt], [ext], snt_r)
            wA = wnext()
            dma("sync", wA.t[:], winv[:, :, 0:512], [winb[i].r], [wA.r])
            for qc in range(4):
                pt = PSF[qc % 2]
                for kc in range(KC):
                    mm(pt, pt.t[:, :nt], wA.t[:, kc, qc * 128:(qc + 1) * 128], hxT.t[:, kc, :nt], [wA.r, hxT.r], kc == 0, kc == KC - 1)
                A(lambda e, pt=pt, qc=qc, nt=nt: e.activation(out=qlat[:, qc, :nt], in_=pt.t[:, :nt], func=AF.Copy), [pt.r], qlat_r)
            rms_lat(qlat, qlat_r, 4, 0, qn_, qn_r, nt, 512)
            wB = wnext()
            dma("sync", wB.t[:, :, 0:320], winv[:, :, 512:832], [winb[i].r], [wB.r])
            for c in range(2):
                pt = PSF[c % 2]
                for kc in range(KC):
                    mm(pt, pt.t[:, :nt], wB.t[:, kc, c * 128:(c + 1) * 128], hxT.t[:, kc, :nt], [wB.r, hxT.r], kc == 0, kc == KC - 1)
                A(lambda e, pt=pt, c=c, nt=nt: e.activation(out=kvlat[:, c, :nt], in_=pt.t[:, :nt], func=AF.Copy), [pt.r], kvlat_r)
            rms_lat(kvlat, kvlat_r, 2, 4, kvn, kvn_r, nt, 256)
            pr = PSF[0]; prot = PSF[1]
            for kc in range(KC):
                mm(pr, pr.t[0:64, :nt], wB.t[:, kc, 256:320], hxT.t[:, kc, :nt], [wB.r, hxT.r], kc == 0, kc == KC - 1)
            for kc in range(KC):
                mm(prot, prot.t[0:64, :nt], wrot.t[:, kc, :], hxT.t[:, kc, :nt], [wrot.r, hxT.r], kc == 0, kc == KC - 1)
            krs, krs_r = rope_out(pr, prot, nt, is_lat, KRd.t[:, r0:r0 + nt], [KRd.rs[mi]])
            for h in range(H):
                pt = PSF[0]
                for kc in range(2):
                    mm(pt, pt.t[:, :nt], wukv[:, kc, h * 256:h * 256 + 128], kvn[:, kc, :nt], wukv_r + kvn_r, kc == 0, kc == 1)
                st, st_r = stage()
                evac(h, st[:, :nt], pt.t[:, :nt], [pt.r], st_r)
                dma("gpsimd", KN.t[h, :, r0:r0 + nt], st[:, :nt], st_r, [KN.rs[mi]])
                upd_max(km, h, st, st_r, krs, krs_r, nt)
                pt = PSF[1]
                for kc in range(4):
                    mm(pt, pt.t[:, :nt], wuq[:, kc, h * 192:h * 192 + 128], qn_[:, kc, :nt], wuq_r + qn_r, kc == 0, kc == 3)
                sq_, sq_r = stage()
                evac(h + 1, sq_[:, :nt], pt.t[:, :nt], [pt.r], sq_r)
                dma("gpsimd", QN.t[h, :, r0:r0 + nt], sq_[:, :nt], sq_r, [QN.rs[mi]])
                pr = PSF[4]; prot = PSF[5]
                for kc in range(4):
                    mm(pr, pr.t[0:64, :nt], wuq[:, kc, h * 192 + 128:h * 192 + 192], qn_[:, kc, :nt], wuq_r + qn_r, kc == 0, kc == 3)
                for kc in range(4):
                    mm(prot, prot.t[0:64, :nt], wuqrot[:, kc, h * 64:(h + 1) * 64], qn_[:, kc, :nt], wuqrot_r + qn_r, kc == 0, kc == 3)
                qrs, qrs_r = rope_out(pr, prot, nt, is_lat, QRd.t[h, :, r0:r0 + nt], [QRd.rs[mi]])
                upd_max(qm, h, sq_, sq_r, qrs, qrs_r, nt)
            for s in range(nt // 128):
                for vb in range(2):
                    pt = PSF[vb]
                    for kc in range(2):
                        mm(pt, pt.t[:], kvn[:, kc, s * 128:(s + 1) * 128], wv[:, kc, vb * 512:(vb + 1) * 512], kvn_r + wv_r, kc == 0, kc == 1)
                    st, st_r = stage()
                    evac(vb, st, pt.t[:], [pt.r], st_r)
                    dma("gpsimd", VV.t[r0 + s * 128:r0 + (s + 1) * 128, vb * 512:(vb + 1) * 512], st, st_r, [VV.rs[mi]])
            for blk in range(2):
                wC = wnext()
                dma("sync", wC.t[:], winv[:, :, 832 + blk * 512:832 + (blk + 1) * 512], [winb[i].r], [wC.r])
                for c in range(4):
                    pt = PSF[c % 2]
                    for kc in range(KC):
                        mm(pt, pt.t[:, :nt], wC.t[:, kc, c * 128:(c + 1) * 128], hxT.t[:, kc, :nt], [wC.r, hxT.r], kc == 0, kc == KC - 1)
                    tq = tmp[c % 2]
                    evac(c, tq.t[:, :nt], pt.t[:, :nt], [pt.r], [tq.r])
                    uc = blk * 4 + c
                    dma("gpsimd", UU.t[uc * 128:(uc + 1) * 128, r0:r0 + nt], tq.t[:, :nt], [tq.r], [UU.rs[mi]])
        V(lambda e: e.tensor_tensor(out=nb.t[:], in0=qm.t[:], in1=km.t[:], op=ALU.mult), [qm.r, km.r], [nb.r])
        A(lambda e: e.activation(out=nb.t[:], in_=nb.t[:], func=AF.Sqrt), [nb.r], [nb.r])
        V(lambda e: e.tensor_scalar(out=nb.t[:], in0=nb.t[:], scalar1=-SCALE, scalar2=None, op0=ALU.mult), [nb.r], [nb.r])

        NKC = T // 128
        TC = (T + 511) // 512
        krt, krt_r = bv(0, TC); krt = krt.rearrange("p a b -> p (a b)")
        knt, knt_r = bv(TC, TC); knt = knt.rearrange("p a b -> p (a b)")
        vt, vt_r = bv(2 * TC, TC); vt = vt.rearrange("p a b -> p (a b)")[:, 0:NKC * 128].rearrange("p (k d) -> p k d", d=128)
        qb_ = [bv(3 * TC + k, 1) for k in range(4)]
        pts = [bv(3 * TC + 4 + k, 1) for k in range(3)]
        ots = [bv(3 * TC + 7 + k, 1) for k in range(2)]
        dma("sync", krt[0:64, 0:T], KRd.t[:, :], KRd.rs, krt_r)
        qblocks = [(r0, nt, (L // 128 if mi == 0 else NKC)) for mi, (r0, nt) in enumerate(MT) if not (last and mi == 0)]
        pc = [0]
        for h in range(H):
            dma("sync", knt[:, 0:T], KN.t[h, :, :], KN.rs, knt_r)
            dma("sync", vt, VV.t[:, h * 128:(h + 1) * 128].rearrange("(k p) d -> p k d", p=128), VV.rs, vt_r)
            steps = [(qi, kc) for qi, (r0, nt, nk) in enumerate(qblocks) for kc in range(nk)]

            def qbufs(qi):
                return qb_[(qi % 2) * 2], qb_[(qi % 2) * 2 + 1]

            def emit_S(idx, h=h):
                qi, kc = steps[idx]
                r0, nt, nk = qblocks[qi]
                mi = qi if not last else qi + 1
                (qa, qa_r), (qr_, qr_r) = qbufs(qi)
                if kc == 0:
                    dma("sync", qa[:, 0, :nt], QN.t[h, :, r0:r0 + nt], [QN.rs[mi]], qa_r)
                    dma("sync", qr_[0:64, 0, :nt], QRd.t[h, :, r0:r0 + nt], [QRd.rs[mi]], qr_r)
                st = PSF[idx % 2]
                mm(st, st.t[:, :nt], knt[:, kc * 128:(kc + 1) * 128], qa[:, 0, :nt], knt_r + qa_r, True, False)
                mm(st, st.t[:, :nt], krt[0:64, kc * 128:(kc + 1) * 128], qr_[0:64, 0, :nt], krt_r + qr_r, False, True)

            emit_S(0)
            for idx, (qi, kc) in enumerate(steps):
                r0, nt, nk = qblocks[qi]
                mi = qi if not last else qi + 1
                if idx + 1 < len(steps):
                    emit_S(idx + 1)
                st = PSF[idx % 2]
                po = PSF[2 + qi % 2]; pd = PSF[4 + qi % 2]
                acc = accs[qi % 2]
                pT, pT_r = pts[pc[0] % 3]; pc[0] += 1
                A(lambda e, st=st, pT=pT, h=h, nt=nt: e.activation(out=pT[:, 0, :nt], in_=st.t[:, :nt], func=AF.Exp,
                                                                   bias=nb.t[:, h:h + 1], scale=SCALE), [st.r, nb.r], pT_r)
                mm(po, po.t[:, :nt], vt[:, kc, :], pT[:, 0, :nt], vt_r + pT_r, kc == 0, kc == nk - 1)
                if kc == 0:
                    V(lambda e, acc=acc, pT=pT, nt=nt: e.tensor_copy(out=acc.t[:, :nt], in_=pT[:, 0, :nt]), pT_r, [acc.r])
                else:
                    V(lambda e, acc=acc, pT=pT, nt=nt: e.tensor_tensor(out=acc.t[:, :nt], in0=acc.t[:, :nt], in1=pT[:, 0, :nt], op=ALU.add),
                      pT_r + [acc.r], [acc.r])
                if kc == nk - 1:
                    mm(pd, pd.t[:, :nt], onesf.t[:], acc.t[:, :nt], [onesf.r, acc.r], True, True)
                    rd = tmp[qi % 2]
                    V(lambda e, pd=pd, rd=rd, nt=nt: e.reciprocal(out=rd.t[:, :nt], in_=pd.t[:, :nt]), [pd.r], [rd.r])
                    ot, ot_r = ots[qi % 2]
                    V(lambda e, po=po, rd=rd, ot=ot, nt=nt: e.tensor_tensor(out=ot[:, 0, :nt], in0=po.t[:, :nt], in1=rd.t[:, :nt], op=ALU.mult),
                      [po.r, rd.r], ot_r)
                    dma("gpsimd", AT.t[h * 128:(h + 1) * 128, r0:r0 + nt], ot[:, 0, :nt], ot_r, [AT.rs[mi]])

        ub = [bvf(0 + 5 * k, 5) for k in range(3)]
        icv, icv_r = bvf(16, 2)
        plT, plT_r = bv(20, 2)
        atv = AT.t.ap().rearrange("(c p) t -> p c t", p=128)
        for mi, (r0, nt) in enumerate(MT):
            if last and mi == 0:
                continue
            seg0, seg1 = (0, L) if mi == 0 else (L, T)
            if mi <= 1:
                load_G(l, 0, 1 if mi == 0 else 0)
            dma("sync", hxT.t[:, 0:8, :nt], atv[:, :, r0:r0 + nt], [AT.rs[mi]], [hxT.r])
            W_ = nt + 16
            for g in range(4):
                U_, U_r = ub[0]; U_ = U_[:, 0:2 * W_].rearrange("p (c w) -> p c w", c=2)
                S1, S1_r = ub[1]; S1 = S1[:, 0:2 * W_].rearrange("p (c w) -> p c w", c=2)
                S2, S2_r = ub[2]; S2 = S2[:, 0:2 * W_].rearrange("p (c w) -> p c w", c=2)
                V(lambda e, U_=U_: e.memset(U_, 0.0), [], U_r)
                lo = max(seg0, r0 - 8); hi = min(seg1, r0 + nt + 8)
                ures = [UU.rs[k] for k in range(NMT) if MT[k][0] < hi and MT[k][0] + MT[k][1] > lo]
                for c in range(2):
                    dma("sync", U_[:, c, lo - (r0 - 8):hi - (r0 - 8)], UU.t[(2 * g + c) * 128:(2 * g + c + 1) * 128, lo:hi], ures, U_r)
                dma("sync", icv[:, :nt], tb["t_invc"][g, r0:r0 + nt].partition_broadcast(128), [ext], icv_r)
                V(lambda e, U_=U_, S1=S1, W_=W_: e.tensor_tensor(out=S1[:, :, 1:W_], in0=U_[:, :, 0:W_ - 1], in1=U_[:, :, 1:W_], op=ALU.add), U_r, S1_r)
                cur, cur_r = S1, S1_r
                oth, oth_r = S2, S2_r
                sh = 1
                for lev in range(g):
                    a = 2 * sh
                    V(lambda e, cur=cur, oth=oth, a=a, sh=sh, W_=W_: e.tensor_tensor(out=oth[:, :, a:W_ - a], in0=cur[:, :, a - sh:W_ - a - sh],
                                                                             in1=cur[:, :, a + sh:W_ - a + sh], op=ALU.add), cur_r, oth_r)
                    cur, cur_r, oth, oth_r = oth, oth_r, cur, cur_r
                    sh *= 2
                for c in range(2):
                    V(lambda e, cur=cur, c=c, nt=nt: e.tensor_tensor(out=tmp[c].t[:, :nt], in0=cur[:, c, 8:8 + nt], in1=icv[:, :nt], op=ALU.mult),
                      cur_r + icv_r, [tmp[c].r])
                    V(lambda e, c=c, U_=U_, nt=nt: e.tensor_tensor(out=plT[:, c, :nt], in0=tmp[c].t[:, :nt], in1=U_[:, c, 8:8 + nt], op=ALU.subtract),
                      [tmp[c].r] + U_r, plT_r)
                for dc in range(2):
                    pt = PSF[dc]
                    for kc in range(2):
                        mm(pt, pt.t[:, :nt], wpool.t[:, g, kc, dc * 128:(dc + 1) * 128], plT[:, kc, :nt], [wpool.r] + plT_r, kc == 0, kc == 1)
                    V(lambda e, pt=pt, g=g, dc=dc, nt=nt: e.tensor_scalar(out=hxT.t[:, 8 + 2 * g + dc, :nt], in0=pt.t[:, :nt],
                                                                  scalar1=smallp.t[:, 6 + 2 * g + dc:7 + 2 * g + dc], scalar2=None, op0=ALU.mult),
                      [pt.r, smallp.r], [hxT.r])
            proj_out_resid(hxT, nt, woeb[i], mi)


    finals = []
    for l in range(depth):
        last = l == depth - 1
        even = l % 2 == 0
        i = l // 2
        if even and cfg_mixers[0]:
            even_mixer(l, i, last)
        if (not even) and cfg_mixers[1]:
            odd_mixer(l, i, last)
        finals = mlp_phase(l, final=last)
    nops, nwait = P.finish(finals)
    print('sbuf bytes remaining', nc.sbuf_bytes_remaining)
    return nc, nops, nwait


def prep_core_inputs(cfg, inp, b, tables):
    KC = cfg.KC
    depth = cfg.depth

    def colT(v, n):
        v = np.asarray(v, np.float32)
        return np.ascontiguousarray(np.swapaxes(v.reshape(v.shape[:-1] + (n, 128)), -1, -2))
    m = {}
    m["x"] = np.ascontiguousarray(inp["x"][b], dtype=np.float32)
    m["ctx"] = np.ascontiguousarray(inp["ctx"][b], dtype=np.float32)
    c2 = np.stack([np.asarray(inp["c"][b], np.float32), np.asarray(inp["c_ctx"], np.float32)])
    m["c2T"] = np.ascontiguousarray(c2.reshape(2, KC, 128).transpose(2, 1, 0))
    m["w_mod"] = np.asarray(inp["w_mod"], np.float32)
    bm = np.asarray(inp["b_mod"], np.float32)
    D = cfg.D
    m["b_modT"] = colT(bm, 6 * KC)
    m["b_modg"] = np.ascontiguousarray(np.stack([bm[:, 2 * D:3 * D], bm[:, 5 * D:6 * D]], axis=1))
    m["norm1T"] = colT(inp["norm1"], KC)
    m["norm2T"] = colT(inp["norm2"], KC)
    m["w_in"] = np.asarray(inp["w_in"], np.float32)
    m["q_normT"] = colT(inp["q_norm"], 4)
    m["w_uq"] = np.asarray(inp["w_uq"], np.float32)
    m["kv_normT"] = colT(inp["kv_norm"], 2)
    m["w_ukv"] = np.asarray(inp["w_ukv"], np.float32)
    m["w_pool"] = np.asarray(inp["w_pool"], np.float32)
    m["pool_scaleT"] = colT(inp["pool_scale"], 8)
    m["w_out_even"] = np.asarray(inp["w_out_even"], np.float32)
    m["w_out_odd"] = np.asarray(inp["w_out_odd"], np.float32)
    m["w_mlp1"] = np.asarray(inp["w_mlp1"], np.float32)
    m["w_mlp2"] = np.asarray(inp["w_mlp2"], np.float32)
    m["final_norm"] = np.asarray(inp["final_norm"], np.float32)
    m.update(tables)
    return m


_CACHE = {}


def kernel(**inputs):
    cfg = Cfg()
    if "prog" not in _CACHE:
        _CACHE["prog"] = build(cfg)[0]
        _CACHE["tables"] = host_tables(cfg)
    nc = _CACHE["prog"]
    tables = _CACHE["tables"]
    inp = {k: np.asarray(v) for k, v in inputs.items()}
    B = inp["x"].shape[0]
    maps = [prep_core_inputs(cfg, inp, b, tables) for b in range(B)]
    in_maps = [maps[c % B] for c in range(8)]
    res = run_bass_kernel_spmd(nc, in_maps, core_ids=list(range(8)))
    out = np.stack([np.asarray(res.results[b]["out"], dtype=np.float32) for b in range(B)], axis=0)
    return out
```

```python
import math
import numpy as np
import concourse.bass as bass
import concourse.mybir as mybir
from concourse.bass_utils import run_bass_kernel_spmd

F32 = mybir.dt.float32
BF16 = mybir.dt.bfloat16
AF = mybir.ActivationFunctionType
ALU = mybir.AluOpType
AX = mybir.AxisListType

H = 8; QR_ = 512; KVR = 256; NOPE = 128; ROPE = 64; VH = 128; QKH = 192
EPS = 1e-6
SCALE = QKH ** -0.5


class Cfg:
    def __init__(self, N=8192, L=256, D=2048, DFF=8192, depth=4):
        self.N = N; self.L = L; self.D = D; self.DFF = DFF; self.depth = depth
        self.T = N + L
        self.N2 = N // 128
        self.KC = D // 128
        self.FC = DFF // 128
        self.PW = D // 2
        self.IN = QR_ + KVR + ROPE + self.PW


class Res:
    __slots__ = ("last_w", "reads")

    def __init__(self):
        self.last_w = None
        self.reads = []


class Op:
    __slots__ = ("eng", "emit", "deps", "is_dma", "needed", "semval", "is_mm")

    def __init__(self, eng, emit, deps, is_dma, is_mm):
        self.eng = eng; self.emit = emit; self.deps = deps
        self.is_dma = is_dma; self.is_mm = is_mm
        self.needed = False; self.semval = None


class Prog:
    ENGS = ("tensor", "vector", "scalar", "gpsimd", "sync")

    def __init__(self, nc):
        self.nc = nc
        self.ops = []

    def op(self, eng, emit, reads=(), writes=(), is_dma=False, is_mm=False):
        idx = len(self.ops)
        deps = set()
        for r in reads:
            if r.last_w is not None:
                deps.add(r.last_w)
        for w in writes:
            if w.last_w is not None:
                deps.add(w.last_w)
            deps.update(w.reads)
        self.ops.append(Op(eng, emit, deps, is_dma, is_mm))
        for r in reads:
            r.reads.append(idx)
        for w in writes:
            w.last_w = idx
            w.reads = []
        return idx

    def finish(self, final_ops):
        nc = self.nc
        ops = self.ops
        for o in ops:
            if o.is_dma:
                o.needed = True
            if o.is_mm:
                o.deps = {d for d in o.deps if not ops[d].is_mm}
            for d in o.deps:
                ops[d].needed = True
        for f in final_ops:
            ops[f].needed = True
        self._cms = []

        def newsem(name):
            cm = nc.semaphore(name)
            s = cm.__enter__()
            self._cms.append(cm)
            return s
        NDMA = 20
        dma_sems = {e: [newsem(f"d{e}{i}") for i in range(NDMA)] for e in ("sync", "gpsimd", "scalar")}
        dma_rr = {e: 0 for e in dma_sems}
        dma_cnt = {}
        dma_last = {}
        sems = {e: newsem(f"c{e}") for e in self.ENGS}
        cnt = {e: 0 for e in self.ENGS}
        for i, o in enumerate(ops):
            if not o.needed:
                continue
            if o.is_dma:
                pool = dma_sems[o.eng]
                k = dma_rr[o.eng] % NDMA
                dma_rr[o.eng] += 1
                s = pool[k]
                key = (o.eng, k)
                if key in dma_last:
                    o.deps.add(dma_last[key])
                dma_last[key] = i
                dma_cnt[key] = dma_cnt.get(key, 0) + 16
                o.semval = (s, dma_cnt[key], 16, key)
            else:
                cnt[o.eng] += 1
                o.semval = (sems[o.eng], cnt[o.eng], 1, o.eng)
        known = {e: {} for e in self.ENGS}
        engobj = {"tensor": nc.tensor, "vector": nc.vector, "scalar": nc.scalar,
                  "gpsimd": nc.gpsimd, "sync": nc.sync}
        nwait = 0
        for o in ops:
            e = engobj[o.eng]
            need = {}
            for d in o.deps:
                s, v, _, key = ops[d].semval
                if key not in need or need[key][1] < v:
                    need[key] = (s, v)
            kn = known[o.eng]
            for key, (s, v) in need.items():
                if kn.get(key, 0) >= v:
                    continue
                e.wait_ge(s, v)
                kn[key] = v
                nwait += 1
            ins = o.emit(e)
            if o.needed:
                s, v, inc, key = o.semval
                ins.then_inc(s, inc)
        for f in final_ops:
            s, v, _, _ = ops[f].semval
            nc.sync.wait_ge(s, v)
        return len(ops), nwait


class Tl:
    def __init__(self, t, nres=1):
        self.t = t
        self.rs = [Res() for _ in range(nres)]

    @property
    def r(self):
        return self.rs[0]


def host_tables(cfg):
    N, L, N2 = cfg.N, cfg.L, cfg.N2
    tb = {}
    c = np.arange(512)
    ang = 2 * np.pi * np.outer(c, c) / 512
    tb["t_cch"] = (np.cos(ang) / math.sqrt(512)).astype(np.float32)
    tb["t_nsch"] = (-np.sin(ang) / math.sqrt(512)).astype(np.float32)
    a = np.arange(128)
    ang = 2 * np.pi * np.outer(a, a) / 128
    tb["t_c128"] = np.cos(ang).astype(np.float32)
    tb["t_s128"] = np.sin(ang).astype(np.float32)
    tb["t_ns128"] = (-np.sin(ang)).astype(np.float32)
    n2 = np.arange(N2)
    ang = 2 * np.pi * np.outer(a, n2) / N
    tb["t_twr"] = (np.cos(ang) / math.sqrt(N)).astype(np.float32)
    tb["t_twi"] = (-np.sin(ang) / math.sqrt(N)).astype(np.float32)
    ang = 2 * np.pi * np.outer(n2, n2) / N2
    tb["t_cs"] = np.concatenate([np.cos(ang), np.sin(ang)], 0).astype(np.float32)
    l = np.arange(L)
    ang = 2 * np.pi * np.outer(l, l) / L
    tb["t_cl"] = (np.cos(ang) / math.sqrt(L)).astype(np.float32)
    tb["t_sl"] = (np.sin(ang) / math.sqrt(L)).astype(np.float32)
    GRID_W = 64
    rows = N // GRID_W
    r = np.broadcast_to(np.arange(rows, dtype=np.float32)[:, None], (rows, GRID_W)).reshape(N)
    col = np.broadcast_to(np.arange(GRID_W, dtype=np.float32)[None, :], (rows, GRID_W)).reshape(N)
    inv = (10000.0 ** (-2.0 * np.arange(16, dtype=np.float32) / 32)).astype(np.float32)
    ang = np.stack([r[:, None] * inv, col[:, None] * inv], axis=1)
    ang = np.broadcast_to(ang[:, :, None, :], (N, 2, 2, 16)).reshape(N, 64)
    tb["t_cos"] = np.ascontiguousarray(np.cos(ang).T).astype(np.float32)
    tb["t_sin"] = np.ascontiguousarray(np.sin(ang).T).astype(np.float32)
    invc = np.zeros((4, cfg.T), np.float32)
    for gi, w in enumerate((2, 4, 8, 16)):
        for off, n in ((0, L), (L, N)):
            t = np.arange(n)
            lo = np.clip(t - w // 2, 0, n); hi = np.clip(t + w // 2, 0, n)
            invc[gi, off:off + n] = 1.0 / (hi - lo)
    tb["t_invc"] = invc
    tb["t_ident"] = np.eye(128, dtype=np.float32)
    return tb


def build(cfg, cfg_mixers=(True, True)):
    N, L, D, DFF, T, N2, KC, FC, PW = cfg.N, cfg.L, cfg.D, cfg.DFF, cfg.T, cfg.N2, cfg.KC, cfg.FC, cfg.PW
    depth = cfg.depth
    NE = (depth + 1) // 2; NO = depth // 2
    nc = bass.Bass("TRN2", target_bir_lowering=False)
    P = Prog(nc)

    def din(name, shape):
        return nc.dram_tensor(name, list(shape), F32, kind="ExternalInput")

    x_in = din("x", [N, D]); ctx_in = din("ctx", [L, D])
    c2T = din("c2T", [128, KC, 2])
    w_mod = din("w_mod", [depth, D, 6 * D]); b_modT = din("b_modT", [depth, 128, 6 * KC])
    b_modg = din("b_modg", [depth, 2, D])
    n1T = din("norm1T", [depth, 128, KC]); n2T = din("norm2T", [depth, 128, KC])
    w_in = din("w_in", [NE, D, cfg.IN]); qnT = din("q_normT", [NE, 128, 4]); w_uq = din("w_uq", [NE, QR_, H * QKH])
    kvnT = din("kv_normT", [NE, 128, 2]); w_ukv = din("w_ukv", [NE, KVR, H * 256])
    w_pool = din("w_pool", [NE, 4, 256, 256]); pscT = din("pool_scaleT", [NE, 128, 8])
    w_oute = din("w_out_even", [NE, D, D]); w_outo = din("w_out_odd", [max(NO, 1), D, D])
    w1 = din("w_mlp1", [depth, D, DFF]); w2 = din("w_mlp2", [depth, DFF, D])
    fin = din("final_norm", [D])
    tbl_shapes = {k: v.shape for k, v in host_tables(cfg).items()}
    tb = {k: din(k, s) for k, s in tbl_shapes.items()}
    out = nc.dram_tensor("out", [N, D], F32, kind="ExternalOutput")

    def dsc(name, shape, dt=BF16, nres=1):
        return Tl(nc.dram_tensor(name, list(shape), dt), nres)
    MT = [(0, L)] + [(L + i * 512, 512) for i in range(N // 512)]
    NMT = len(MT)
    XR = dsc("XR", [T, D], F32, NMT)
    w1b = [dsc(f"w1b{l}", [D, DFF]) for l in range(depth)]
    w2b = [dsc(f"w2b{l}", [DFF, D]) for l in range(depth)]
    winb = [dsc(f"winb{i}", [D, cfg.IN]) for i in range(NE)]
    woeb = [dsc(f"woeb{i}", [D, D]) for i in range(NE)]
    woob = [dsc(f"woob{i}", [D, D]) for i in range(NO)]
    GROW = dsc("GROW", [depth, 2, 2, D], F32)
    QN = dsc("QN", [H, 128, T], BF16, NMT); QRd = dsc("QRd", [H, 64, T], BF16, NMT)
    KN = dsc("KN", [H, 128, T], BF16, NMT); KRd = dsc("KRd", [64, T], BF16, NMT)
    VV = dsc("VV", [T, H * 128], BF16, NMT); UU = dsc("UU", [PW, T], F32, NMT)
    AT = dsc("AT", [H * 128, T], BF16, NMT)
    Z0 = dsc("Z0", [T, 2, D], BF16, NMT)
    Z1 = dsc("Z1", [2, N2, 128, D], BF16, N2)

    def sb(name, shape, dt=F32, nres=1):
        return Tl(nc.alloc_sbuf_tensor(name, list(shape), dt), nres)

    def ps(name, shape, dt=F32):
        return Tl(nc.alloc_psum_tensor(name, list(shape), dt))
    ident = sb("ident", [128, 128], BF16)
    ones = sb("ones", [128, 128], BF16)
    xt = [sb(f"xt{i}", [128, D]) for i in range(2)]
    xn = [sb(f"xn{i}", [128, D], BF16) for i in range(2)]
    hxT = sb("hxT", [128, KC, 512], BF16)
    BIG = sb("BIG", [128, 64, 512], BF16, 64)
    WR = [sb(f"WR{i}", [128, 16, 512], BF16) for i in range(3)]
    wctr = [0]

    def wnext():
        wctr[0] += 1
        return WR[wctr[0] % 3]
    Gt = sb("Gt", [128, D])
    tmp = [sb(f"tmp{i}", [128, 512]) for i in range(3)]
    stat = sb("stat", [128, 8])
    accs = [sb("dacc0", [128, 512]), sb("dacc1", [128, 512])]
    t2ring = [tmp[2], accs[0], accs[1], sb("t2x", [128, 512])]
    sc2 = sb("sc2", [128, KC, 2], BF16)
    PSF = [ps(f"psf{i}", [128, 512]) for i in range(6)]
    PST = ps("pst", [128, 2048], BF16)

    def dma(eng, out_ap, in_ap, reads, writes):
        return P.op(eng, lambda e: e.dma_start(out=out_ap, in_=in_ap), reads, writes, is_dma=True)

    def mm(o, oap, lap, rap, reads, start, stop):
        return P.op("tensor", lambda e: e.matmul(oap, lap, rap, start=start, stop=stop), reads, [o.r], is_mm=True)

    def V(fn, reads, writes):
        return P.op("vector", fn, reads, writes)

    def A(fn, reads, writes):
        return P.op("scalar", fn, reads, writes)

    def G(fn, reads, writes):
        return P.op("gpsimd", fn, reads, writes)

    ext = Res()

    dma("gpsimd", ident.t[:], tb["t_ident"][:, :], [ext], [ident.r])
    G(lambda e: e.memset(ones.t[:], 1.0), [], [ones.r])
    epst = sb("epst", [128, 1])
    G(lambda e: e.memset(epst.t[:], EPS), [], [epst.r])
    for mi, (r0, nt) in enumerate(MT):
        src = ctx_in[0:L, :] if mi == 0 else x_in[r0 - L:r0 - L + nt, :]
        dma("sync", XR.t[r0:r0 + nt, :], src, [ext], [XR.rs[mi]])

    def cast_w(dst, src_ap, rows, cols):
        step = max(1, min(rows, (1 << 22) // cols))
        for r0 in range(0, rows, step):
            r1 = min(rows, r0 + step)
            dma("gpsimd", dst.t[r0:r1, :], src_ap[r0:r1, :], [ext], [dst.r])

    c2s = sb("c2s", [128, KC, 2])
    dma("sync", c2s.t[:], c2T[:, :, :], [ext], [c2s.r])
    A(lambda e: e.activation(out=sc2.t[:], in_=c2s.t[:], func=AF.Silu), [c2s.r], [sc2.r])
    bmT = sb("bmT", [128, 6 * KC])
    bmg = sb("bmg", [2, 512])
    nrm = sb("nrm", [128, 2, KC])
    ABs = []
    for l in range(depth):
        ABl = sb(f"AB{l}", [128, 4, KC, 2])
        MODl = sb(f"MODT{l}", [128, 6 * KC, 2])
        ABs.append((ABl, MODl))
        dma("sync", bmT.t[:], b_modT[l, :, :], [ext], [bmT.r])
        dma("sync", nrm.t[:, 0, :], n1T[l, :, :], [ext], [nrm.r])
        dma("sync", nrm.t[:, 1, :], n2T[l, :, :], [ext], [nrm.r])
        for cb in range(6 * D // 512):
            wt = wnext()
            dma("gpsimd", wt.t[:], w_mod[l].rearrange("(kc p) c -> p kc c", p=128)[:, :, cb * 512:(cb + 1) * 512],
                [ext], [wt.r])
            pm = PSF[cb % 2]
            for j in range(4):
                ch = cb * 4 + j
                for kc in range(KC):
                    mm(pm, pm.t[:, j * 2:j * 2 + 2], wt.t[:, kc, j * 128:(j + 1) * 128], sc2.t[:, kc, :],
                       [wt.r, sc2.r], kc == 0, kc == KC - 1)
            V(lambda e, pm=pm, cb=cb, MODl=MODl: e.tensor_tensor(
                out=MODl.t[:, cb * 4:cb * 4 + 4, :], in0=pm.t[:, 0:8].rearrange("p (j r) -> p j r", r=2),
                in1=bmT.t[:, cb * 4:cb * 4 + 4].unsqueeze(2).to_broadcast([128, 4, 2]), op=ALU.add),
              [pm.r, bmT.r], [MODl.r])
            gi = None
            if 2 * KC <= cb * 4 < 3 * KC:
                gi = 0; c0 = cb * 512 - 2 * D
            elif 5 * KC <= cb * 4 < 6 * KC:
                gi = 1; c0 = cb * 512 - 5 * D
            if gi is not None:
                pg = PSF[2 + cb % 2]
                for kc in range(KC):
                    mm(pg, pg.t[0:2, :], sc2.t[:, kc, :], wt.t[:, kc, :], [wt.r, sc2.r], kc == 0, kc == KC - 1)
                gt = tmp[cb % 2]
                dma("sync", bmg.t[:], b_modg[l, gi, c0:c0 + 512].partition_broadcast(2), [ext], [bmg.r])
                V(lambda e, pg=pg, gt=gt: e.tensor_tensor(
                    out=gt.t[0:2, :], in0=pg.t[0:2, :], in1=bmg.t[0:2, :], op=ALU.add),
                  [pg.r, bmg.r], [gt.r])
                dma("sync", GROW.t[l, gi, :, c0:c0 + 512], gt.t[0:2, :], [gt.r], [GROW.r])
        for (ai, sci, shi, ni) in ((0, 1, 0, 0), (2, 4, 3, 1)):
            V(lambda e, ABl=ABl, MODl=MODl, ai=ai, sci=sci, ni=ni: e.scalar_tensor_tensor(
                out=ABl.t[:, ai, :, :], in0=MODl.t[:, sci * KC:(sci + 1) * KC, :], scalar=1.0,
                in1=nrm.t[:, ni, :].unsqueeze(2).to_broadcast([128, KC, 2]), op0=ALU.add, op1=ALU.mult),
              [MODl.r, nrm.r], [ABl.r])
            V(lambda e, ABl=ABl, MODl=MODl, ai=ai, shi=shi: e.tensor_copy(
                out=ABl.t[:, ai + 1, :, :], in_=MODl.t[:, shi * KC:(shi + 1) * KC, :]), [MODl.r], [ABl.r])

    for l in range(depth):
        cast_w(w1b[l], w1[l], D, DFF)
        cast_w(w2b[l], w2[l], DFF, D)
    for i in range(NE):
        cast_w(winb[i], w_in[i], D, cfg.IN)
        cast_w(woeb[i], w_oute[i], D, D)
    for i in range(NO):
        cast_w(woob[i], w_outo[i], D, D)

    def normA(mi, s):
        r0, nt = MT[mi]
        x_ = xt[s % 2]; xn_ = xn[s % 2]
        st = stat
        dma("sync", x_.t[:], XR.t[r0 + s * 128:r0 + (s + 1) * 128, :], [XR.rs[mi]], [x_.r])
        A(lambda e, x_=x_, xn_=xn_: e.activation(out=xn_.t[:], in_=x_.t[:], func=AF.Square, accum_out=stat.t[:, 0:1]),
          [x_.r], [xn_.r, st.r])
        A(lambda e: e.activation(out=stat.t[:, 1:2], in_=stat.t[:, 0:1], func=AF.Sqrt, bias=epst.t[:, 0:1], scale=1.0 / D),
          [st.r, epst.r], [st.r])
        V(lambda e: e.reciprocal(out=stat.t[:, 2:3], in_=stat.t[:, 1:2]), [st.r], [st.r])
        V(lambda e, x_=x_, xn_=xn_: e.tensor_scalar(out=xn_.t[:], in0=x_.t[:], scalar1=stat.t[:, 2:3], scalar2=None,
                                                    op0=ALU.mult), [x_.r, st.r], [xn_.r])

    def normB(l, mi, s, which):
        row = 1 if mi == 0 else 0
        ABl = ABs[l][0]
        ai = 0 if which == 1 else 2
        xn_ = xn[s % 2]
        for kc in range(KC):
            P.op("tensor", lambda e, kc=kc, xn_=xn_: e.transpose(PST.t[:, kc * 128:(kc + 1) * 128],
                                                                xn_.t[:, kc * 128:(kc + 1) * 128], ident.t[:]),
                 [xn_.r, ident.r], [PST.r], is_mm=True)
        for kc in range(KC):
            V(lambda e, kc=kc, s=s: e.tensor_scalar(
                out=hxT.t[:, kc, s * 128:(s + 1) * 128], in0=PST.t[:, kc * 128:(kc + 1) * 128],
                scalar1=ABl.t[:, ai, kc, row:row + 1], scalar2=ABl.t[:, ai + 1, kc, row:row + 1],
                op0=ALU.mult, op1=ALU.add), [PST.r, ABl.r], [hxT.r])

    def norm_to_hxT(l, mi, which, src=None):
        for s in range(MT[mi][1] // 128):
            normA(mi, s)
            normB(l, mi, s, which)

    def G_or_A(fn, reads, writes):
        return V(fn, reads, writes)

    def load_G(l, gi, row):
        dma("sync", Gt.t[:], GROW.t[l, gi, row, :].partition_broadcast(128), [GROW.r], [Gt.r])

    rctr = [0]

    def resid_update(mi, s, j, pt, rows_aps=None, xres=None):
        if xres is None:
            xres = [XR.rs[mi]]
        if rows_aps is None:
            r0, nt = MT[mi]
            rows_aps = [(0, 128, XR.t[r0 + s * 128:r0 + (s + 1) * 128, j * 512:(j + 1) * 512])]
        t2 = t2ring[rctr[0] % len(t2ring)]; rctr[0] += 1
        V(lambda e, t2=t2: e.tensor_tensor(out=t2.t[:], in0=pt.t[:], in1=Gt.t[:, j * 512:(j + 1) * 512], op=ALU.mult),
          [pt.r, Gt.r], [t2.r])
        last = None
        for (p0, pn, ap) in rows_aps:
            last = P.op("gpsimd", lambda e, ap=ap, p0=p0, pn=pn, t2=t2: e.dma_start(out=ap, in_=t2.t[p0:p0 + pn, :], accum_op=ALU.add),
                        [t2.r], xres, is_dma=True)
        return last

    def proj_out_resid(srcT, nt, wdram, mi=None, rows_fn=None, xres=None):
        for j in range(D // 512):
            wt = wnext()
            dma("sync", wt.t[:], wdram.t.ap().rearrange("(kc p) c -> p kc c", p=128)[:, :, j * 512:(j + 1) * 512],
                [wdram.r], [wt.r])
            for s in range(nt // 128):
                pt = PSF[(s + j) % 2]
                for kc in range(KC):
                    mm(pt, pt.t[:], srcT.t[:, kc, s * 128:(s + 1) * 128], wt.t[:, kc, :], [srcT.r, wt.r], kc == 0, kc == KC - 1)
                resid_update(mi, s, j, pt, None if rows_fn is None else rows_fn(s, j), xres)

    def mlp_phase(l, final=False):
        finals = []
        if final:
            fg = sb("fing", [128, D])
            dma("sync", fg.t[:], fin.ap().partition_broadcast(128), [ext], [fg.r])
        tiles = [mi for mi in range(NMT) if not (final and mi == 0)]
        for tidx, mi in enumerate(tiles):
            r0, nt = MT[mi]
            if mi <= 1:
                load_G(l, 1, 1 if mi == 0 else 0)
            if tidx == 0:
                norm_to_hxT(l, mi, 2)
            nxt = tiles[tidx + 1] if tidx + 1 < len(tiles) else None
            for fb in range(DFF // 512):
                wt = wnext()
                dma("sync", wt.t[:], w1b[l].t.ap().rearrange("(kc p) f -> p kc f", p=128)[:, :, fb * 512:(fb + 1) * 512],
                    [w1b[l].r], [wt.r])
                for j in range(4):
                    fc = fb * 4 + j
                    pt = PSF[fc % 2]
                    for kc in range(KC):
                        mm(pt, pt.t[:, :nt], wt.t[:, kc, j * 128:(j + 1) * 128], hxT.t[:, kc, :nt], [wt.r, hxT.r], kc == 0, kc == KC - 1)
                    tq = tmp[fc % 2]
                    A(lambda e, pt=pt, tq=tq, nt=nt: e.activation(out=tq.t[:, :nt], in_=pt.t[:, :nt], func=AF.Relu), [pt.r], [tq.r])
                    V(lambda e, tq=tq, fc=fc, nt=nt: e.tensor_tensor(out=BIG.t[:, fc, :nt], in0=tq.t[:, :nt], in1=tq.t[:, :nt], op=ALU.mult),
                      [tq.r], [BIG.rs[fc]])
            nsub = nt // 128
            slot = 0
            hook_slots = {1: 0, 2: 1, 3: 2, 5: 3}
            bdone = []
            if nxt is not None:
                nsn = MT[nxt][1] // 128
                normA(nxt, 0)
                if nsn > 1:
                    normA(nxt, 1)
            for j in range(D // 512):
                for fg_ in range(FC // 16):
                    if nxt is not None and slot in hook_slots and hook_slots[slot] < nsn:
                        k = hook_slots[slot]
                        normB(l, nxt, k, 2)
                        bdone.append(k)
                        if k + 2 < nsn:
                            normA(nxt, k + 2)
                    slot += 1
                    wt = wnext()
                    dma("sync", wt.t[:], w2b[l].t.ap().rearrange("(fc p) d -> p fc d", p=128)[:, fg_ * 16:(fg_ + 1) * 16, j * 512:(j + 1) * 512],
                        [w2b[l].r], [wt.r])
                    for f in range(16):
                        fc = fg_ * 16 + f
                        for s in range(nsub):
                            pt = PSF[2 + s]
                            mm(pt, pt.t[:], BIG.t[:, fc, s * 128:(s + 1) * 128], wt.t[:, f, :], [BIG.rs[fc], wt.r], fc == 0, fc == FC - 1)
                for s in range(nsub):
                    pt = PSF[2 + s]
                    resid_update(mi, s, j, pt)
            if nxt is not None:
                for k in range(nsn):
                    if k not in bdone:
                        normB(l, nxt, k, 2)
                        if k + 2 < nsn:
                            normA(nxt, k + 2)
            if final:
                for s in range(nsub):
                    x_ = xt[s % 2]; xj = xn[s % 2]
                    dma("sync", x_.t[:], XR.t[r0 + s * 128:r0 + (s + 1) * 128, :], [XR.rs[mi]], [x_.r])
                    A(lambda e, x_=x_, xj=xj: e.activation(out=xj.t[:], in_=x_.t[:], func=AF.Square, accum_out=stat.t[:, 4:5]),
                      [x_.r], [xj.r, stat.r])
                    A(lambda e: e.activation(out=stat.t[:, 5:6], in_=stat.t[:, 4:5], func=AF.Sqrt, bias=epst.t[:, 0:1], scale=1.0 / D),
                      [stat.r, epst.r], [stat.r])
                    V(lambda e: e.reciprocal(out=stat.t[:, 6:7], in_=stat.t[:, 5:6]), [stat.r], [stat.r])
                    V(lambda e, x_=x_: e.scalar_tensor_tensor(out=x_.t[:], in0=x_.t[:], scalar=stat.t[:, 6:7], in1=fg.t[:],
                                                              op0=ALU.mult, op1=ALU.mult), [x_.r, stat.r, fg.r], [x_.r])
                    finals.append(dma("gpsimd", out[r0 - L + s * 128:r0 - L + (s + 1) * 128, :], x_.t[:], [x_.r], []))
        return finals


    if NO > 0:
        cch = sb("cch", [128, 4, 512], BF16); nsch = sb("nsch", [128, 4, 512], BF16)
        dma("gpsimd", cch.t[:], tb["t_cch"].ap().rearrange("(kc p) m -> p kc m", p=128), [ext], [cch.r])
        dma("gpsimd", nsch.t[:], tb["t_nsch"].ap().rearrange("(kc p) m -> p kc m", p=128), [ext], [nsch.r])
        c128 = sb("c128", [128, 128], BF16); s128 = sb("s128", [128, 128], BF16); ns128 = sb("ns128", [128, 128], BF16)
        dma("gpsimd", c128.t[:], tb["t_c128"][:, :], [ext], [c128.r])
        dma("gpsimd", s128.t[:], tb["t_s128"][:, :], [ext], [s128.r])
        dma("gpsimd", ns128.t[:], tb["t_ns128"][:, :], [ext], [ns128.r])
        twr = sb("twr", [128, N2]); twi = sb("twi", [128, N2])
        dma("sync", twr.t[:], tb["t_twr"][:, :], [ext], [twr.r])
        dma("sync", twi.t[:], tb["t_twi"][:, :], [ext], [twi.r])
        cs = sb("cs", [2 * N2, N2], BF16)
        dma("gpsimd", cs.t[:], tb["t_cs"][:, :], [ext], [cs.r])
        cl = sb("cl", [128, L // 128, L], BF16); sl = sb("sl", [128, L // 128, L], BF16)
        dma("gpsimd", cl.t[:], tb["t_cl"].ap().rearrange("(a p) k -> p a k", p=128), [ext], [cl.r])
        dma("gpsimd", sl.t[:], tb["t_sl"].ap().rearrange("(a p) k -> p a k", p=128), [ext], [sl.r])

    def zreg(b):
        ap = BIG.t[:, 8 * b:8 * b + 8, :].rearrange("p (r c) f -> p r (c f)", r=2)
        return ap, BIG.rs[8 * b:8 * b + 8]

    def evac(i, out_ap, in_ap, reads, writes):
        if i % 2 == 0:
            V(lambda e: e.tensor_copy(out=out_ap, in_=in_ap), reads, writes)
        else:
            A(lambda e: e.activation(out=out_ap, in_=in_ap, func=AF.Copy), reads, writes)

    def odd_mixer(l, i, last):
        KS = 128 // N2
        xlat = XR.rs[1:]
        for mi, (r0, nt) in enumerate(MT):
            if last and mi == 0:
                continue
            norm_to_hxT(l, mi, 1)
            for s in range(nt // 128):
                zap, zrs = zreg(s % 2)
                for ri in range(2):
                    tabl = cch if ri == 0 else nsch
                    for g in range(4):
                        pt = PSF[(ri * 4 + g) % 2]
                        for kc in range(4):
                            mm(pt, pt.t[:], hxT.t[:, 4 * g + kc, s * 128:(s + 1) * 128], tabl.t[:, kc, :], [hxT.r, tabl.r], kc == 0, kc == 3)
                        evac(g, zap[:, ri, g * 512:(g + 1) * 512], pt.t[:], [pt.r], zrs)
                dma("gpsimd", Z0.t[r0 + s * 128:r0 + (s + 1) * 128, :, :], zap, zrs, [Z0.rs[mi]])
        z0lat = Z0.t[L:L + N, :, :].rearrange("(n1 n2) r d -> n1 n2 r d", n2=N2)
        for n2 in range(N2):
            zin, zirs = zreg(n2 % 2)
            zo, zors = zreg(2 + n2 % 2)
            dma("sync", zin, z0lat[:, n2, :, :], Z0.rs[1:], zirs)
            for cb in range(4):
                sl_ = slice(cb * 512, (cb + 1) * 512)
                yr = PSF[0 + 2 * (cb % 2)]; yi = PSF[1 + 2 * (cb % 2)]
                mm(yr, yr.t[:], c128.t[:], zin[:, 0, sl_], [c128.r] + zirs, True, False)
                mm(yr, yr.t[:], s128.t[:], zin[:, 1, sl_], [s128.r] + zirs, False, True)
                mm(yi, yi.t[:], c128.t[:], zin[:, 1, sl_], [c128.r] + zirs, True, False)
                mm(yi, yi.t[:], ns128.t[:], zin[:, 0, sl_], [ns128.r] + zirs, False, True)
                t1 = tmp[0]; t2 = tmp[1]
                A(lambda e, yi=yi, n2=n2: e.activation(out=t1.t[:], in_=yi.t[:], func=AF.Copy, scale=twi.t[:, n2:n2 + 1]),
                  [yi.r, twi.r], [t1.r])
                A(lambda e, yi=yi, n2=n2: e.activation(out=t2.t[:], in_=yi.t[:], func=AF.Copy, scale=twr.t[:, n2:n2 + 1]),
                  [yi.r, twr.r], [t2.r])
                V(lambda e, yr=yr, n2=n2, sl_=sl_, zo=zo: e.scalar_tensor_tensor(
                    out=zo[:, 0, sl_], in0=yr.t[:], scalar=twr.t[:, n2:n2 + 1], in1=t1.t[:], op0=ALU.mult, op1=ALU.subtract),
                  [yr.r, twr.r, t1.r], zors)
                V(lambda e, yr=yr, n2=n2, sl_=sl_, zo=zo: e.scalar_tensor_tensor(
                    out=zo[:, 1, sl_], in0=yr.t[:], scalar=twi.t[:, n2:n2 + 1], in1=t2.t[:], op0=ALU.mult, op1=ALU.add),
                  [yr.r, twi.r, t2.r], zors)
            for ri in range(2):
                dma("gpsimd", Z1.t[ri, n2, :, :], zo[:, ri, :], zors, [Z1.rs[n2]])
        load_G(l, 0, 0)
        z1v = Z1.t.ap().rearrange("r n k d -> (r n) k d")
        xrl = XR.t[L:L + N, :].rearrange("(k2 k1) d -> k1 k2 d", k1=128)
        for m in range(128 // (4 * KS)):
            for s in range(4):
                for k1l in range(KS):
                    k1 = (4 * m + s) * KS + k1l
                    z3 = xn[k1l % 2]
                    dma("sync", z3.t[0:2 * N2, :], z1v[:, k1, :], Z1.rs, [z3.r])
                    for kc in range(KC):
                        pt = PSF[2 + kc // 4]
                        c0 = (kc % 4) * 128 + k1l * N2
                        mm(pt, pt.t[:, c0:c0 + N2], z3.t[0:2 * N2, kc * 128:(kc + 1) * 128], cs.t[:, :], [z3.r, cs.r], True, True)
                for b in range(4):
                    pt = PSF[2 + b]
                    evac(b, hxT.t[:, 4 * b:4 * b + 4, s * 128:(s + 1) * 128], pt.t[:].rearrange("p (c t) -> p c t", c=4), [pt.r], [hxT.r])

            def rows_fn(s, j, m=m):
                return [(k1l * N2, N2, xrl[(4 * m + s) * KS + k1l, :, j * 512:(j + 1) * 512]) for k1l in range(KS)]
            proj_out_resid(hxT, 512, woob[i], None, rows_fn, xlat)
        if not last:
            load_G(l, 0, 1)
            zc = []
            for a in range(L // 128):
                zap, zrs = zreg(a)
                dma("sync", zap, Z0.t[a * 128:(a + 1) * 128, :, :], [Z0.rs[0]], zrs)
                zc.append((zap, zrs))
            for kc in range(KC):
                pt = PSF[kc % 2]
                n = 0
                for a in range(L // 128):
                    zap, zrs = zc[a]
                    for ri, tabl in ((0, cl), (1, sl)):
                        mm(pt, pt.t[:, :L], zap[:, ri, kc * 128:(kc + 1) * 128], tabl.t[:, a, :], zrs + [tabl.r], n == 0, n == 2 * (L // 128) - 1)
                        n += 1
                evac(kc, hxT.t[:, kc, 0:L], pt.t[:, :L], [pt.r], [hxT.r])
            proj_out_resid(hxT, L, woob[i], 0)

    wpool = sb("wpool", [128, 4, 2, 256], BF16)
    wrot = sb("wrot", [128, KC, 64], BF16)
    smallp = sb("smallp", [128, 16])
    qm = sb("qm", [128, 8]); km = sb("km", [128, 8]); nb = sb("nb", [128, 8])
    onesf = sb("onesf", [128, 128])
    G(lambda e: e.memset(onesf.t[:], 1.0), [], [onesf.r])

    def bv(c0, n):
        return BIG.t[:, c0:c0 + n, :], BIG.rs[c0:c0 + n]

    def bvf(c0, n):
        return BIG.t[:, c0:c0 + n, :].rearrange("p a b -> p (a b)").bitcast(F32), BIG.rs[c0:c0 + n]

    def even_mixer(l, i, last):
        wuq, wuq_r = bv(0, 12); wuq = wuq.rearrange("p a b -> p (a b)").rearrange("p (k c) -> p k c", k=4)
        wuqrot, wuqrot_r = bv(12, 4)
        wukv, wukv_r = bv(16, 8); wukv = wukv.rearrange("p a b -> p (a b)").rearrange("p (k c) -> p k c", k=2)
        wv, wv_r = bv(24, 4); wv = wv.rearrange("p a b -> p (a b)").rearrange("p (k c) -> p k c", k=2)
        dma("gpsimd", wuq, w_uq[i].rearrange("(k p) c -> p k c", p=128), [ext], wuq_r)
        dma("gpsimd", wukv, w_ukv[i].rearrange("(k p) c -> p k c", p=128), [ext], wukv_r)
        dma("gpsimd", wpool.t[:], w_pool[i].rearrange("g (k p) d -> p g k d", p=128), [ext], [wpool.r])
        dma("sync", smallp.t[:, 0:4], qnT[i, :, :], [ext], [smallp.r])
        dma("sync", smallp.t[:, 4:6], kvnT[i, :, :], [ext], [smallp.r])
        dma("sync", smallp.t[:, 6:14], pscT[i, :, :], [ext], [smallp.r])
        for h in range(H):
            V(lambda e, h=h: e.tensor_copy(out=wv[:, :, h * 128:(h + 1) * 128], in_=wukv[:, :, h * 256 + 128:h * 256 + 256]),
              wukv_r, wv_r)
            for a in range(2):
                src0 = h * 192 + 128 + a * 32
                V(lambda e, h=h, a=a, src0=src0: e.tensor_scalar(out=wuqrot[:, :, h * 64 + a * 32:h * 64 + a * 32 + 16],
                                                               in0=wuq[:, :, src0 + 16:src0 + 32], scalar1=-1.0, scalar2=None, op0=ALU.mult),
                  wuq_r, wuqrot_r)
                V(lambda e, h=h, a=a, src0=src0: e.tensor_copy(out=wuqrot[:, :, h * 64 + a * 32 + 16:h * 64 + a * 32 + 32],
                                                             in_=wuq[:, :, src0:src0 + 16]), wuq_r, wuqrot_r)
        wb0 = wnext()
        winv = winb[i].t.ap().rearrange("(kc p) c -> p kc c", p=128)
        dma("sync", wb0.t[:, :, 0:320], winv[:, :, 512:832], [winb[i].r], [wb0.r])
        for a in range(2):
            V(lambda e, a=a: e.tensor_scalar(out=wrot.t[:, :, a * 32:a * 32 + 16], in0=wb0.t[:, :, 256 + a * 32 + 16:256 + a * 32 + 32],
                                            scalar1=-1.0, scalar2=None, op0=ALU.mult), [wb0.r], [wrot.r])
            V(lambda e, a=a: e.tensor_copy(out=wrot.t[:, :, a * 32 + 16:a * 32 + 32], in_=wb0.t[:, :, 256 + a * 32:256 + a * 32 + 16]),
              [wb0.r], [wrot.r])
        G(lambda e: e.memset(qm.t[:], 0.0), [], [qm.r])
        G(lambda e: e.memset(km.t[:], 0.0), [], [km.r])
        qlat, qlat_r = bvf(28, 8); qlat = qlat.rearrange("p (k t) -> p k t", k=4)
        kvlat, kvlat_r = bvf(36, 4); kvlat = kvlat.rearrange("p (k t) -> p k t", k=2)
        qn_, qn_r = bv(40, 4)
        kvn, kvn_r = bv(44, 2)
        sqb, sqb_r = bv(46, 2)
        cst, cst_r = bvf(48, 2); snt, snt_r = bvf(50, 2)
        stg = [bv(52 + k, 1) for k in range(8)]
        sctr = [0]

        def stage():
            sctr[0] += 1
            ap, rs = stg[sctr[0] % 8]
            return ap[:, 0, :], rs

        def rms_lat(lat, lat_r, nch, ncol0, out, out_r, nt, dim):
            pss = PSF[2]
            for c in range(nch):
                A(lambda e, c=c: e.activation(out=sqb[:, 0, :nt], in_=lat[:, c, :nt], func=AF.Square), lat_r, sqb_r)
                mm(pss, pss.t[:, :nt], ones.t[:], sqb[:, 0, :nt], [ones.r] + sqb_r, c == 0, c == nch - 1)
            A(lambda e: e.activation(out=tmp[0].t[:, :nt], in_=pss.t[:, :nt], func=AF.Sqrt, bias=epst.t[:, 0:1], scale=1.0 / dim),
              [pss.r, epst.r], [tmp[0].r])
            V(lambda e: e.reciprocal(out=tmp[1].t[:, :nt], in_=tmp[0].t[:, :nt]), [tmp[0].r], [tmp[1].r])
            for c in range(nch):
                V(lambda e, c=c: e.scalar_tensor_tensor(out=out[:, c, :nt], in0=lat[:, c, :nt], scalar=smallp.t[:, ncol0 + c:ncol0 + c + 1],
                                                        in1=tmp[1].t[:, :nt], op0=ALU.mult, op1=ALU.mult),
                  lat_r + [smallp.r, tmp[1].r], out_r)

        def rope_out(pr, prot, nt, is_lat, dst_ap, dst_res):
            st, st_r = stage()
            if is_lat:
                V(lambda e: e.tensor_tensor(out=tmp[0].t[0:64, :nt], in0=prot.t[0:64, :nt], in1=snt[0:64, :nt], op=ALU.mult),
                  [prot.r] + snt_r, [tmp[0].r])
                V(lambda e: e.tensor_tensor(out=tmp[1].t[0:64, :nt], in0=pr.t[0:64, :nt], in1=cst[0:64, :nt], op=ALU.mult),
                  [pr.r] + cst_r, [tmp[1].r])
                V(lambda e: e.tensor_tensor(out=st[0:64, :nt], in0=tmp[0].t[0:64, :nt], in1=tmp[1].t[0:64, :nt], op=ALU.add),
                  [tmp[0].r, tmp[1].r], st_r)
            else:
                V(lambda e: e.tensor_copy(out=st[0:64, :nt], in_=pr.t[0:64, :nt]), [pr.r], st_r)
            dma("gpsimd", dst_ap, st[0:64, :nt], st_r, dst_res)
            return st, st_r

        def upd_max(mx, h, nope, nope_r, rp, rp_r, nt):
            pn = PSF[3]
            A(lambda e: e.activation(out=sqb[:, 0, :nt], in_=nope[:, :nt], func=AF.Square), nope_r, sqb_r)
            A(lambda e: e.activation(out=sqb[0:64, 1, :nt], in_=rp[0:64, :nt], func=AF.Square), rp_r, sqb_r)
            mm(pn, pn.t[:, :nt], ones.t[:], sqb[:, 0, :nt], [ones.r] + sqb_r, True, False)
            mm(pn, pn.t[:, :nt], ones.t[0:64, :], sqb[0:64, 1, :nt], [ones.r] + sqb_r, False, True)
            V(lambda e: e.reduce_max(out=stat.t[:, 7:8], in_=pn.t[:, :nt], axis=AX.X), [pn.r], [stat.r])
            V(lambda e, h=h: e.tensor_max(out=mx.t[:, h:h + 1], in0=mx.t[:, h:h + 1], in1=stat.t[:, 7:8]), [mx.r, stat.r], [mx.r])

        for mi, (r0, nt) in enumerate(MT):
            is_lat = mi > 0
            norm_to_hxT(l, mi, 1)
            if is_lat:
                dma("sync", cst[0:64, :nt], tb["t_cos"][:, r0 - L:r0 - L + nt], [ext], cst_r)
                dma("sync", snt[0:64, :nt], tb["t_sin"][:, r0 - L:r0 - L + nt], [ext], snt_r)
            wA = wnext()
            dma("sync", wA.t[:], winv[:, :, 0:512], [winb[i].r], [wA.r])
            for qc in range(4):
                pt = PSF[qc % 2]
                for kc in range(KC):
                    mm(pt, pt.t[:, :nt], wA.t[:, kc, qc * 128:(qc + 1) * 128], hxT.t[:, kc, :nt], [wA.r, hxT.r], kc == 0, kc == KC - 1)
                A(lambda e, pt=pt, qc=qc, nt=nt: e.activation(out=qlat[:, qc, :nt], in_=pt.t[:, :nt], func=AF.Copy), [pt.r], qlat_r)
            rms_lat(qlat, qlat_r, 4, 0, qn_, qn_r, nt, 512)
            wB = wnext()
            dma("sync", wB.t[:, :, 0:320], winv[:, :, 512:832], [winb[i].r], [wB.r])
            for c in range(2):
                pt = PSF[c % 2]
                for kc in range(KC):
                    mm(pt, pt.t[:, :nt], wB.t[:, kc, c * 128:(c + 1) * 128], hxT.t[:, kc, :nt], [wB.r, hxT.r], kc == 0, kc == KC - 1)
                A(lambda e, pt=pt, c=c, nt=nt: e.activation(out=kvlat[:, c, :nt], in_=pt.t[:, :nt], func=AF.Copy), [pt.r], kvlat_r)
            rms_lat(kvlat, kvlat_r, 2, 4, kvn, kvn_r, nt, 256)
            pr = PSF[0]; prot = PSF[1]
            for kc in range(KC):
                mm(pr, pr.t[0:64, :nt], wB.t[:, kc, 256:320], hxT.t[:, kc, :nt], [wB.r, hxT.r], kc == 0, kc == KC - 1)
            for kc in range(KC):
                mm(prot, prot.t[0:64, :nt], wrot.t[:, kc, :], hxT.t[:, kc, :nt], [wrot.r, hxT.r], kc == 0, kc == KC - 1)
            krs, krs_r = rope_out(pr, prot, nt, is_lat, KRd.t[:, r0:r0 + nt], [KRd.rs[mi]])
            for h in range(H):
                pt = PSF[0]
                for kc in range(2):
                    mm(pt, pt.t[:, :nt], wukv[:, kc, h * 256:h * 256 + 128], kvn[:, kc, :nt], wukv_r + kvn_r, kc == 0, kc == 1)
                st, st_r = stage()
                evac(h, st[:, :nt], pt.t[:, :nt], [pt.r], st_r)
                dma("gpsimd", KN.t[h, :, r0:r0 + nt], st[:, :nt], st_r, [KN.rs[mi]])
                upd_max(km, h, st, st_r, krs, krs_r, nt)
                pt = PSF[1]
                for kc in range(4):
                    mm(pt, pt.t[:, :nt], wuq[:, kc, h * 192:h * 192 + 128], qn_[:, kc, :nt], wuq_r + qn_r, kc == 0, kc == 3)
                sq_, sq_r = stage()
                evac(h + 1, sq_[:, :nt], pt.t[:, :nt], [pt.r], sq_r)
                dma("gpsimd", QN.t[h, :, r0:r0 + nt], sq_[:, :nt], sq_r, [QN.rs[mi]])
                pr = PSF[4]; prot = PSF[5]
                for kc in range(4):
                    mm(pr, pr.t[0:64, :nt], wuq[:, kc, h * 192 + 128:h * 192 + 192], qn_[:, kc, :nt], wuq_r + qn_r, kc == 0, kc == 3)
                for kc in range(4):
                    mm(prot, prot.t[0:64, :nt], wuqrot[:, kc, h * 64:(h + 1) * 64], qn_[:, kc, :nt], wuqrot_r + qn_r, kc == 0, kc == 3)
                qrs, qrs_r = rope_out(pr, prot, nt, is_lat, QRd.t[h, :, r0:r0 + nt], [QRd.rs[mi]])
                upd_max(qm, h, sq_, sq_r, qrs, qrs_r, nt)
            for s in range(nt // 128):
                for vb in range(2):
                    pt = PSF[vb]
                    for kc in range(2):
                        mm(pt, pt.t[:], kvn[:, kc, s * 128:(s + 1) * 128], wv[:, kc, vb * 512:(vb + 1) * 512], kvn_r + wv_r, kc == 0, kc == 1)
                    st, st_r = stage()
                    evac(vb, st, pt.t[:], [pt.r], st_r)
                    dma("gpsimd", VV.t[r0 + s * 128:r0 + (s + 1) * 128, vb * 512:(vb + 1) * 512], st, st_r, [VV.rs[mi]])
            for blk in range(2):
                wC = wnext()
                dma("sync", wC.t[:], winv[:, :, 832 + blk * 512:832 + (blk + 1) * 512], [winb[i].r], [wC.r])
                for c in range(4):
                    pt = PSF[c % 2]
                    for kc in range(KC):
                        mm(pt, pt.t[:, :nt], wC.t[:, kc, c * 128:(c + 1) * 128], hxT.t[:, kc, :nt], [wC.r, hxT.r], kc == 0, kc == KC - 1)
                    tq = tmp[c % 2]
                    evac(c, tq.t[:, :nt], pt.t[:, :nt], [pt.r], [tq.r])
                    uc = blk * 4 + c
                    dma("gpsimd", UU.t[uc * 128:(uc + 1) * 128, r0:r0 + nt], tq.t[:, :nt], [tq.r], [UU.rs[mi]])
        V(lambda e: e.tensor_tensor(out=nb.t[:], in0=qm.t[:], in1=km.t[:], op=ALU.mult), [qm.r, km.r], [nb.r])
        A(lambda e: e.activation(out=nb.t[:], in_=nb.t[:], func=AF.Sqrt), [nb.r], [nb.r])
        V(lambda e: e.tensor_scalar(out=nb.t[:], in0=nb.t[:], scalar1=-SCALE, scalar2=None, op0=ALU.mult), [nb.r], [nb.r])

        NKC = T // 128
        TC = (T + 511) // 512
        krt, krt_r = bv(0, TC); krt = krt.rearrange("p a b -> p (a b)")
        knt, knt_r = bv(TC, TC); knt = knt.rearrange("p a b -> p (a b)")
        vt, vt_r = bv(2 * TC, TC); vt = vt.rearrange("p a b -> p (a b)")[:, 0:NKC * 128].rearrange("p (k d) -> p k d", d=128)
        qb_ = [bv(3 * TC + k, 1) for k in range(4)]
        pts = [bv(3 * TC + 4 + k, 1) for k in range(3)]
        ots = [bv(3 * TC + 7 + k, 1) for k in range(2)]
        dma("sync", krt[0:64, 0:T], KRd.t[:, :], KRd.rs, krt_r)
        V(lambda e: e.memset(krt[64:128, :], 0.0), [], krt_r)
        for k in (1, 3):
            V(lambda e, k=k: e.memset(qb_[k][0][64:128, 0, :], 0.0), [], qb_[k][1])
        qblocks = [(r0, nt, (L // 128 if mi == 0 else NKC)) for mi, (r0, nt) in enumerate(MT) if not (last and mi == 0)]
        pc = [0]
        for h in range(H):
            dma("sync", knt[:, 0:T], KN.t[h, :, :], KN.rs, knt_r)
            dma("sync", vt, VV.t[:, h * 128:(h + 1) * 128].rearrange("(k p) d -> p k d", p=128), VV.rs, vt_r)
            steps = [(qi, kc) for qi, (r0, nt, nk) in enumerate(qblocks) for kc in range(nk)]

            def qbufs(qi):
                return qb_[(qi % 2) * 2], qb_[(qi % 2) * 2 + 1]

            def emit_S(idx, h=h):
                qi, kc = steps[idx]
                r0, nt, nk = qblocks[qi]
                mi = qi if not last else qi + 1
                (qa, qa_r), (qr_, qr_r) = qbufs(qi)
                if kc == 0:
                    dma("sync", qa[:, 0, :nt], QN.t[h, :, r0:r0 + nt], [QN.rs[mi]], qa_r)
                    dma("sync", qr_[0:64, 0, :nt], QRd.t[h, :, r0:r0 + nt], [QRd.rs[mi]], qr_r)
                st = PSF[idx % 2]
                mm(st, st.t[:, :nt], knt[:, kc * 128:(kc + 1) * 128], qa[:, 0, :nt], knt_r + qa_r, True, False)
                mm(st, st.t[:, :nt], krt[:, kc * 128:(kc + 1) * 128], qr_[:, 0, :nt], krt_r + qr_r, False, True)

            emit_S(0)
            for idx, (qi, kc) in enumerate(steps):
                r0, nt, nk = qblocks[qi]
                mi = qi if not last else qi + 1
                if idx + 1 < len(steps):
                    emit_S(idx + 1)
                st = PSF[idx % 2]
                po = PSF[2 + qi % 2]; pd = PSF[4 + qi % 2]
                acc = accs[qi % 2]
                pT, pT_r = pts[pc[0] % 3]; pc[0] += 1
                A(lambda e, st=st, pT=pT, h=h, nt=nt: e.activation(out=pT[:, 0, :nt], in_=st.t[:, :nt], func=AF.Exp,
                                                                   bias=nb.t[:, h:h + 1], scale=SCALE), [st.r, nb.r], pT_r)
                mm(po, po.t[:, :nt], vt[:, kc, :], pT[:, 0, :nt], vt_r + pT_r, kc == 0, kc == nk - 1)
                if kc == 0:
                    V(lambda e, acc=acc, pT=pT, nt=nt: e.tensor_copy(out=acc.t[:, :nt], in_=pT[:, 0, :nt]), pT_r, [acc.r])
                else:
                    V(lambda e, acc=acc, pT=pT, nt=nt: e.tensor_tensor(out=acc.t[:, :nt], in0=acc.t[:, :nt], in1=pT[:, 0, :nt], op=ALU.add),
                      pT_r + [acc.r], [acc.r])
                if kc == nk - 1:
                    mm(pd, pd.t[:, :nt], onesf.t[:], acc.t[:, :nt], [onesf.r, acc.r], True, True)
                    rd = tmp[qi % 2]
                    V(lambda e, pd=pd, rd=rd, nt=nt: e.reciprocal(out=rd.t[:, :nt], in_=pd.t[:, :nt]), [pd.r], [rd.r])
                    ot, ot_r = ots[qi % 2]
                    V(lambda e, po=po, rd=rd, ot=ot, nt=nt: e.tensor_tensor(out=ot[:, 0, :nt], in0=po.t[:, :nt], in1=rd.t[:, :nt], op=ALU.mult),
                      [po.r, rd.r], ot_r)
                    dma("gpsimd", AT.t[h * 128:(h + 1) * 128, r0:r0 + nt], ot[:, 0, :nt], ot_r, [AT.rs[mi]])

        ub = [bvf(0 + 5 * k, 5) for k in range(3)]
        icv, icv_r = bvf(16, 2)
        plT, plT_r = bv(20, 2)
        atv = AT.t.ap().rearrange("(c p) t -> p c t", p=128)
        for mi, (r0, nt) in enumerate(MT):
            if last and mi == 0:
                continue
            seg0, seg1 = (0, L) if mi == 0 else (L, T)
            if mi <= 1:
                load_G(l, 0, 1 if mi == 0 else 0)
            dma("sync", hxT.t[:, 0:8, :nt], atv[:, :, r0:r0 + nt], [AT.rs[mi]], [hxT.r])
            W_ = nt + 16
            for g in range(4):
                U_, U_r = ub[0]; U_ = U_[:, 0:2 * W_].rearrange("p (c w) -> p c w", c=2)
                S1, S1_r = ub[1]; S1 = S1[:, 0:2 * W_].rearrange("p (c w) -> p c w", c=2)
                S2, S2_r = ub[2]; S2 = S2[:, 0:2 * W_].rearrange("p (c w) -> p c w", c=2)
                V(lambda e, U_=U_: e.memset(U_, 0.0), [], U_r)
                lo = max(seg0, r0 - 8); hi = min(seg1, r0 + nt + 8)
                ures = [UU.rs[k] for k in range(NMT) if MT[k][0] < hi and MT[k][0] + MT[k][1] > lo]
                for c in range(2):
                    dma("sync", U_[:, c, lo - (r0 - 8):hi - (r0 - 8)], UU.t[(2 * g + c) * 128:(2 * g + c + 1) * 128, lo:hi], ures, U_r)
                dma("sync", icv[:, :nt], tb["t_invc"][g, r0:r0 + nt].partition_broadcast(128), [ext], icv_r)
                V(lambda e, U_=U_, S1=S1, W_=W_: e.tensor_tensor(out=S1[:, :, 1:W_], in0=U_[:, :, 0:W_ - 1], in1=U_[:, :, 1:W_], op=ALU.add), U_r, S1_r)
                cur, cur_r = S1, S1_r
                oth, oth_r = S2, S2_r
                sh = 1
                for lev in range(g):
                    a = 2 * sh
                    V(lambda e, cur=cur, oth=oth, a=a, sh=sh, W_=W_: e.tensor_tensor(out=oth[:, :, a:W_ - a], in0=cur[:, :, a - sh:W_ - a - sh],
                                                                             in1=cur[:, :, a + sh:W_ - a + sh], op=ALU.add), cur_r, oth_r)
                    cur, cur_r, oth, oth_r = oth, oth_r, cur, cur_r
                    sh *= 2
                for c in range(2):
                    V(lambda e, cur=cur, c=c, nt=nt: e.tensor_tensor(out=tmp[c].t[:, :nt], in0=cur[:, c, 8:8 + nt], in1=icv[:, :nt], op=ALU.mult),
                      cur_r + icv_r, [tmp[c].r])
                    V(lambda e, c=c, U_=U_, nt=nt: e.tensor_tensor(out=plT[:, c, :nt], in0=tmp[c].t[:, :nt], in1=U_[:, c, 8:8 + nt], op=ALU.subtract),
                      [tmp[c].r] + U_r, plT_r)
                for dc in range(2):
                    pt = PSF[dc]
                    for kc in range(2):
                        mm(pt, pt.t[:, :nt], wpool.t[:, g, kc, dc * 128:(dc + 1) * 128], plT[:, kc, :nt], [wpool.r] + plT_r, kc == 0, kc == 1)
                    V(lambda e, pt=pt, g=g, dc=dc, nt=nt: e.tensor_scalar(out=hxT.t[:, 8 + 2 * g + dc, :nt], in0=pt.t[:, :nt],
                                                                  scalar1=smallp.t[:, 6 + 2 * g + dc:7 + 2 * g + dc], scalar2=None, op0=ALU.mult),
                      [pt.r, smallp.r], [hxT.r])
            proj_out_resid(hxT, nt, woeb[i], mi)


    finals = []
    for l in range(depth):
        last = l == depth - 1
        even = l % 2 == 0
        i = l // 2
        if even and cfg_mixers[0]:
            even_mixer(l, i, last)
        if (not even) and cfg_mixers[1]:
            odd_mixer(l, i, last)
        finals = mlp_phase(l, final=last)
    nops, nwait = P.finish(finals)
    print('sbuf bytes remaining', nc.sbuf_bytes_remaining)
    return nc, nops, nwait


def prep_core_inputs(cfg, inp, b, tables):
    KC = cfg.KC
    depth = cfg.depth

    def colT(v, n):
        v = np.asarray(v, np.float32)
        return np.ascontiguousarray(np.swapaxes(v.reshape(v.shape[:-1] + (n, 128)), -1, -2))
    m = {}
    m["x"] = np.ascontiguousarray(inp["x"][b], dtype=np.float32)
    m["ctx"] = np.ascontiguousarray(inp["ctx"][b], dtype=np.float32)
    c2 = np.stack([np.asarray(inp["c"][b], np.float32), np.asarray(inp["c_ctx"], np.float32)])
    m["c2T"] = np.ascontiguousarray(c2.reshape(2, KC, 128).transpose(2, 1, 0))
    m["w_mod"] = np.asarray(inp["w_mod"], np.float32)
    bm = np.asarray(inp["b_mod"], np.float32)
    D = cfg.D
    m["b_modT"] = colT(bm, 6 * KC)
    m["b_modg"] = np.ascontiguousarray(np.stack([bm[:, 2 * D:3 * D], bm[:, 5 * D:6 * D]], axis=1))
    m["norm1T"] = colT(inp["norm1"], KC)
    m["norm2T"] = colT(inp["norm2"], KC)
    m["w_in"] = np.asarray(inp["w_in"], np.float32)
    m["q_normT"] = colT(inp["q_norm"], 4)
    m["w_uq"] = np.asarray(inp["w_uq"], np.float32)
    m["kv_normT"] = colT(inp["kv_norm"], 2)
    m["w_ukv"] = np.asarray(inp["w_ukv"], np.float32)
    m["w_pool"] = np.asarray(inp["w_pool"], np.float32)
    m["pool_scaleT"] = colT(inp["pool_scale"], 8)
    m["w_out_even"] = np.asarray(inp["w_out_even"], np.float32)
    m["w_out_odd"] = np.asarray(inp["w_out_odd"], np.float32)
    m["w_mlp1"] = np.asarray(inp["w_mlp1"], np.float32)
    m["w_mlp2"] = np.asarray(inp["w_mlp2"], np.float32)
    m["final_norm"] = np.asarray(inp["final_norm"], np.float32)
    m.update(tables)
    return m


_CACHE = {}


def kernel(**inputs):
    cfg = Cfg()
    if "prog" not in _CACHE:
        _CACHE["prog"] = build(cfg)[0]
        _CACHE["tables"] = host_tables(cfg)
    nc = _CACHE["prog"]
    tables = _CACHE["tables"]
    inp = {k: np.asarray(v) for k, v in inputs.items()}
    B = inp["x"].shape[0]
    maps = [prep_core_inputs(cfg, inp, b, tables) for b in range(B)]
    in_maps = [maps[c % B] for c in range(8)]
    res = run_bass_kernel_spmd(nc, in_maps, core_ids=list(range(8)))
    out = np.stack([np.asarray(res.results[b]["out"], dtype=np.float32) for b in range(B)], axis=0)
    return out
```

```python
import math
import numpy as np
import concourse.bass as bass
import concourse.mybir as mybir
from concourse.bass_utils import run_bass_kernel_spmd

F32 = mybir.dt.float32
BF16 = mybir.dt.bfloat16
AF = mybir.ActivationFunctionType
ALU = mybir.AluOpType
AX = mybir.AxisListType

H = 8; QR_ = 512; KVR = 256; NOPE = 128; ROPE = 64; VH = 128; QKH = 192
EPS = 1e-6
SCALE = QKH ** -0.5


class Cfg:
    def __init__(self, N=8192, L=256, D=2048, DFF=8192, depth=4):
        self.N = N; self.L = L; self.D = D; self.DFF = DFF; self.depth = depth
        self.T = N + L
        self.N2 = N // 128
        self.KC = D // 128
        self.FC = DFF // 128
        self.PW = D // 2
        self.IN = QR_ + KVR + ROPE + self.PW


class Res:
    __slots__ = ("last_w", "reads")

    def __init__(self):
        self.last_w = None
        self.reads = []


class Op:
    __slots__ = ("eng", "emit", "deps", "is_dma", "needed", "semval", "is_mm")

    def __init__(self, eng, emit, deps, is_dma, is_mm):
        self.eng = eng; self.emit = emit; self.deps = deps
        self.is_dma = is_dma; self.is_mm = is_mm
        self.needed = False; self.semval = None


class Prog:
    ENGS = ("tensor", "vector", "scalar", "gpsimd", "sync")

    def __init__(self, nc):
        self.nc = nc
        self.ops = []

    def op(self, eng, emit, reads=(), writes=(), is_dma=False, is_mm=False):
        idx = len(self.ops)
        deps = set()
        for r in reads:
            if r.last_w is not None:
                deps.add(r.last_w)
        for w in writes:
            if w.last_w is not None:
                deps.add(w.last_w)
            deps.update(w.reads)
        self.ops.append(Op(eng, emit, deps, is_dma, is_mm))
        for r in reads:
            r.reads.append(idx)
        for w in writes:
            w.last_w = idx
            w.reads = []
        return idx

    def finish(self, final_ops):
        nc = self.nc
        ops = self.ops
        for o in ops:
            if o.is_dma:
                o.needed = True
            if o.is_mm:
                o.deps = {d for d in o.deps if not ops[d].is_mm}
            for d in o.deps:
                ops[d].needed = True
        for f in final_ops:
            ops[f].needed = True
        self._cms = []

        def newsem(name):
            cm = nc.semaphore(name)
            s = cm.__enter__()
            self._cms.append(cm)
            return s
        NDMA = 20
        dma_sems = {e: [newsem(f"d{e}{i}") for i in range(NDMA)] for e in ("sync", "gpsimd", "scalar")}
        dma_rr = {e: 0 for e in dma_sems}
        dma_cnt = {}
        dma_last = {}
        sems = {e: newsem(f"c{e}") for e in self.ENGS}
        cnt = {e: 0 for e in self.ENGS}
        for i, o in enumerate(ops):
            if not o.needed:
                continue
            if o.is_dma:
                pool = dma_sems[o.eng]
                k = dma_rr[o.eng] % NDMA
                dma_rr[o.eng] += 1
                s = pool[k]
                key = (o.eng, k)
                if key in dma_last:
                    o.deps.add(dma_last[key])
                dma_last[key] = i
                dma_cnt[key] = dma_cnt.get(key, 0) + 16
                o.semval = (s, dma_cnt[key], 16, key)
            else:
                cnt[o.eng] += 1
                o.semval = (sems[o.eng], cnt[o.eng], 1, o.eng)
        known = {e: {} for e in self.ENGS}
        engobj = {"tensor": nc.tensor, "vector": nc.vector, "scalar": nc.scalar,
                  "gpsimd": nc.gpsimd, "sync": nc.sync}
        nwait = 0
        for o in ops:
            e = engobj[o.eng]
            need = {}
            for d in o.deps:
                s, v, _, key = ops[d].semval
                if key not in need or need[key][1] < v:
                    need[key] = (s, v)
            kn = known[o.eng]
            for key, (s, v) in need.items():
                if kn.get(key, 0) >= v:
                    continue
                e.wait_ge(s, v)
                kn[key] = v
                nwait += 1
            ins = o.emit(e)
            if o.needed:
                s, v, inc, key = o.semval
                ins.then_inc(s, inc)
        for f in final_ops:
            s, v, _, _ = ops[f].semval
            nc.sync.wait_ge(s, v)
        return len(ops), nwait


class Tl:
    def __init__(self, t, nres=1):
        self.t = t
        self.rs = [Res() for _ in range(nres)]

    @property
    def r(self):
        return self.rs[0]


def host_tables(cfg):
    N, L, N2 = cfg.N, cfg.L, cfg.N2
    tb = {}
    c = np.arange(512)
    ang = 2 * np.pi * np.outer(c, c) / 512
    tb["t_cch"] = (np.cos(ang) / math.sqrt(512)).astype(np.float32)
    tb["t_nsch"] = (-np.sin(ang) / math.sqrt(512)).astype(np.float32)
    a = np.arange(128)
    ang = 2 * np.pi * np.outer(a, a) / 128
    tb["t_c128"] = np.cos(ang).astype(np.float32)
    tb["t_s128"] = np.sin(ang).astype(np.float32)
    tb["t_ns128"] = (-np.sin(ang)).astype(np.float32)
    n2 = np.arange(N2)
    ang = 2 * np.pi * np.outer(a, n2) / N
    tb["t_twr"] = (np.cos(ang) / math.sqrt(N)).astype(np.float32)
    tb["t_twi"] = (-np.sin(ang) / math.sqrt(N)).astype(np.float32)
    ang = 2 * np.pi * np.outer(n2, n2) / N2
    tb["t_cs"] = np.concatenate([np.cos(ang), np.sin(ang)], 0).astype(np.float32)
    l = np.arange(L)
    ang = 2 * np.pi * np.outer(l, l) / L
    tb["t_cl"] = (np.cos(ang) / math.sqrt(L)).astype(np.float32)
    tb["t_sl"] = (np.sin(ang) / math.sqrt(L)).astype(np.float32)
    GRID_W = 64
    rows = N // GRID_W
    r = np.broadcast_to(np.arange(rows, dtype=np.float32)[:, None], (rows, GRID_W)).reshape(N)
    col = np.broadcast_to(np.arange(GRID_W, dtype=np.float32)[None, :], (rows, GRID_W)).reshape(N)
    inv = (10000.0 ** (-2.0 * np.arange(16, dtype=np.float32) / 32)).astype(np.float32)
    ang = np.stack([r[:, None] * inv, col[:, None] * inv], axis=1)
    ang = np.broadcast_to(ang[:, :, None, :], (N, 2, 2, 16)).reshape(N, 64)
    tb["t_cos"] = np.ascontiguousarray(np.cos(ang).T).astype(np.float32)
    tb["t_sin"] = np.ascontiguousarray(np.sin(ang).T).astype(np.float32)
    invc = np.zeros((4, cfg.T), np.float32)
    for gi, w in enumerate((2, 4, 8, 16)):
        for off, n in ((0, L), (L, N)):
            t = np.arange(n)
            lo = np.clip(t - w // 2, 0, n); hi = np.clip(t + w // 2, 0, n)
            invc[gi, off:off + n] = 1.0 / (hi - lo)
    tb["t_invc"] = invc
    tb["t_ident"] = np.eye(128, dtype=np.float32)
    return tb


def build(cfg, cfg_mixers=(True, True)):
    N, L, D, DFF, T, N2, KC, FC, PW = cfg.N, cfg.L, cfg.D, cfg.DFF, cfg.T, cfg.N2, cfg.KC, cfg.FC, cfg.PW
    depth = cfg.depth
    NE = (depth + 1) // 2; NO = depth // 2
    nc = bass.Bass("TRN2", target_bir_lowering=False)
    P = Prog(nc)

    def din(name, shape):
        return nc.dram_tensor(name, list(shape), F32, kind="ExternalInput")

    x_in = din("x", [N, D]); ctx_in = din("ctx", [L, D])
    c2T = din("c2T", [128, KC, 2])
    w_mod = din("w_mod", [depth, D, 6 * D]); b_modT = din("b_modT", [depth, 128, 6 * KC])
    b_modg = din("b_modg", [depth, 2, D])
    n1T = din("norm1T", [depth, 128, KC]); n2T = din("norm2T", [depth, 128, KC])
    w_in = din("w_in", [NE, D, cfg.IN]); qnT = din("q_normT", [NE, 128, 4]); w_uq = din("w_uq", [NE, QR_, H * QKH])
    kvnT = din("kv_normT", [NE, 128, 2]); w_ukv = din("w_ukv", [NE, KVR, H * 256])
    w_pool = din("w_pool", [NE, 4, 256, 256]); pscT = din("pool_scaleT", [NE, 128, 8])
    w_oute = din("w_out_even", [NE, D, D]); w_outo = din("w_out_odd", [max(NO, 1), D, D])
    w1 = din("w_mlp1", [depth, D, DFF]); w2 = din("w_mlp2", [depth, DFF, D])
    fin = din("final_norm", [D])
    tbl_shapes = {k: v.shape for k, v in host_tables(cfg).items()}
    tb = {k: din(k, s) for k, s in tbl_shapes.items()}
    out = nc.dram_tensor("out", [N, D], F32, kind="ExternalOutput")

    def dsc(name, shape, dt=BF16, nres=1):
        return Tl(nc.dram_tensor(name, list(shape), dt), nres)
    MT = [(0, L)] + [(L + i * 512, 512) for i in range(N // 512)]
    NMT = len(MT)
    XR = dsc("XR", [T, D], F32, NMT)
    w1b = [dsc(f"w1b{l}", [D, DFF]) for l in range(depth)]
    w2b = [dsc(f"w2b{l}", [DFF, D]) for l in range(depth)]
    winb = [dsc(f"winb{i}", [D, cfg.IN]) for i in range(NE)]
    woeb = [dsc(f"woeb{i}", [D, D]) for i in range(NE)]
    woob = [dsc(f"woob{i}", [D, D]) for i in range(NO)]
    GROW = dsc("GROW", [depth, 2, 2, D], F32)
    QN = dsc("QN", [H, 128, T], BF16, NMT); QRd = dsc("QRd", [H, 64, T], BF16, NMT)
    KN = dsc("KN", [H, 128, T], BF16, NMT); KRd = dsc("KRd", [64, T], BF16, NMT)
    VV = dsc("VV", [T, H * 128], BF16, NMT); UU = dsc("UU", [PW, T], F32, NMT)
    AT = dsc("AT", [H * 128, T], BF16, NMT)
    Z0 = dsc("Z0", [T, 2, D], BF16, NMT)
    Z1 = dsc("Z1", [2, N2, 128, D], BF16, N2)

    def sb(name, shape, dt=F32, nres=1):
        return Tl(nc.alloc_sbuf_tensor(name, list(shape), dt), nres)

    def ps(name, shape, dt=F32):
        return Tl(nc.alloc_psum_tensor(name, list(shape), dt))
    ident = sb("ident", [128, 128], BF16)
    ones = sb("ones", [128, 128], BF16)
    xt = [sb(f"xt{i}", [128, D]) for i in range(2)]
    xn = [sb(f"xn{i}", [128, D], BF16) for i in range(2)]
    hxT = sb("hxT", [128, KC, 512], BF16)
    BIG = sb("BIG", [128, 64, 512], BF16, 64)
    WR = [sb(f"WR{i}", [128, 16, 512], BF16) for i in range(3)]
    wctr = [0]

    def wnext():
        wctr[0] += 1
        return WR[wctr[0] % 3]
    Gt = sb("Gt", [128, D])
    tmp = [sb(f"tmp{i}", [128, 512]) for i in range(3)]
    stat = sb("stat", [128, 8])
    accs = [sb("dacc0", [128, 512]), sb("dacc1", [128, 512])]
    t2ring = [tmp[2], accs[0], accs[1], sb("t2x", [128, 512])]
    sc2 = sb("sc2", [128, KC, 2], BF16)
    PSF = [ps(f"psf{i}", [128, 512]) for i in range(6)]
    PST = ps("pst", [128, 2048], BF16)

    def dma(eng, out_ap, in_ap, reads, writes):
        return P.op(eng, lambda e: e.dma_start(out=out_ap, in_=in_ap), reads, writes, is_dma=True)

    def mm(o, oap, lap, rap, reads, start, stop):
        return P.op("tensor", lambda e: e.matmul(oap, lap, rap, start=start, stop=stop), reads, [o.r], is_mm=True)

    def V(fn, reads, writes):
        return P.op("vector", fn, reads, writes)

    def A(fn, reads, writes):
        return P.op("scalar", fn, reads, writes)

    def G(fn, reads, writes):
        return P.op("gpsimd", fn, reads, writes)

    ext = Res()

    dma("gpsimd", ident.t[:], tb["t_ident"][:, :], [ext], [ident.r])
    G(lambda e: e.memset(ones.t[:], 1.0), [], [ones.r])
    epst = sb("epst", [128, 1])
    G(lambda e: e.memset(epst.t[:], EPS), [], [epst.r])
    for mi, (r0, nt) in enumerate(MT):
        src = ctx_in[0:L, :] if mi == 0 else x_in[r0 - L:r0 - L + nt, :]
        dma("sync", XR.t[r0:r0 + nt, :], src, [ext], [XR.rs[mi]])

    def cast_w(dst, src_ap, rows, cols):
        step = max(1, min(rows, (1 << 22) // cols))
        for r0 in range(0, rows, step):
            r1 = min(rows, r0 + step)
            dma("gpsimd", dst.t[r0:r1, :], src_ap[r0:r1, :], [ext], [dst.r])

    c2s = sb("c2s", [128, KC, 2])
    dma("sync", c2s.t[:], c2T[:, :, :], [ext], [c2s.r])
    A(lambda e: e.activation(out=sc2.t[:], in_=c2s.t[:], func=AF.Silu), [c2s.r], [sc2.r])
    bmT = sb("bmT", [128, 6 * KC])
    bmg = sb("bmg", [2, 512])
    nrm = sb("nrm", [128, 2, KC])
    ABs = []
    for l in range(depth):
        ABl = sb(f"AB{l}", [128, 4, KC, 2])
        MODl = sb(f"MODT{l}", [128, 6 * KC, 2])
        ABs.append((ABl, MODl))
        dma("sync", bmT.t[:], b_modT[l, :, :], [ext], [bmT.r])
        dma("sync", nrm.t[:, 0, :], n1T[l, :, :], [ext], [nrm.r])
        dma("sync", nrm.t[:, 1, :], n2T[l, :, :], [ext], [nrm.r])
        for cb in range(6 * D // 512):
            wt = wnext()
            dma("gpsimd", wt.t[:], w_mod[l].rearrange("(kc p) c -> p kc c", p=128)[:, :, cb * 512:(cb + 1) * 512],
                [ext], [wt.r])
            pm = PSF[cb % 2]
            for j in range(4):
                ch = cb * 4 + j
                for kc in range(KC):
                    mm(pm, pm.t[:, j * 2:j * 2 + 2], wt.t[:, kc, j * 128:(j + 1) * 128], sc2.t[:, kc, :],
                       [wt.r, sc2.r], kc == 0, kc == KC - 1)
            V(lambda e, pm=pm, cb=cb, MODl=MODl: e.tensor_tensor(
                out=MODl.t[:, cb * 4:cb * 4 + 4, :], in0=pm.t[:, 0:8].rearrange("p (j r) -> p j r", r=2),
                in1=bmT.t[:, cb * 4:cb * 4 + 4].unsqueeze(2).to_broadcast([128, 4, 2]), op=ALU.add),
              [pm.r, bmT.r], [MODl.r])
            gi = None
            if 2 * KC <= cb * 4 < 3 * KC:
                gi = 0; c0 = cb * 512 - 2 * D
            elif 5 * KC <= cb * 4 < 6 * KC:
                gi = 1; c0 = cb * 512 - 5 * D
            if gi is not None:
                pg = PSF[2 + cb % 2]
                for kc in range(KC):
                    mm(pg, pg.t[0:2, :], sc2.t[:, kc, :], wt.t[:, kc, :], [wt.r, sc2.r], kc == 0, kc == KC - 1)
                gt = tmp[cb % 2]
                dma("sync", bmg.t[:], b_modg[l, gi, c0:c0 + 512].partition_broadcast(2), [ext], [bmg.r])
                V(lambda e, pg=pg, gt=gt: e.tensor_tensor(
                    out=gt.t[0:2, :], in0=pg.t[0:2, :], in1=bmg.t[0:2, :], op=ALU.add),
                  [pg.r, bmg.r], [gt.r])
                dma("sync", GROW.t[l, gi, :, c0:c0 + 512], gt.t[0:2, :], [gt.r], [GROW.r])
        for (ai, sci, shi, ni) in ((0, 1, 0, 0), (2, 4, 3, 1)):
            V(lambda e, ABl=ABl, MODl=MODl, ai=ai, sci=sci, ni=ni: e.scalar_tensor_tensor(
                out=ABl.t[:, ai, :, :], in0=MODl.t[:, sci * KC:(sci + 1) * KC, :], scalar=1.0,
                in1=nrm.t[:, ni, :].unsqueeze(2).to_broadcast([128, KC, 2]), op0=ALU.add, op1=ALU.mult),
              [MODl.r, nrm.r], [ABl.r])
            V(lambda e, ABl=ABl, MODl=MODl, ai=ai, shi=shi: e.tensor_copy(
                out=ABl.t[:, ai + 1, :, :], in_=MODl.t[:, shi * KC:(shi + 1) * KC, :]), [MODl.r], [ABl.r])

    for l in range(depth):
        i = l // 2
        if l % 2 == 0:
            cast_w(winb[i], w_in[i], D, cfg.IN)
            cast_w(woeb[i], w_oute[i], D, D)
        else:
            cast_w(woob[i], w_outo[i], D, D)
        cast_w(w1b[l], w1[l], D, DFF)
        cast_w(w2b[l], w2[l], DFF, D)

    def normA(mi, s):
        r0, nt = MT[mi]
        x_ = xt[s % 2]; xn_ = xn[s % 2]
        st = stat
        dma("sync", x_.t[:], XR.t[r0 + s * 128:r0 + (s + 1) * 128, :], [XR.rs[mi]], [x_.r])
        A(lambda e, x_=x_, xn_=xn_: e.activation(out=xn_.t[:], in_=x_.t[:], func=AF.Square, accum_out=stat.t[:, 0:1]),
          [x_.r], [xn_.r, st.r])
        A(lambda e: e.activation(out=stat.t[:, 1:2], in_=stat.t[:, 0:1], func=AF.Sqrt, bias=epst.t[:, 0:1], scale=1.0 / D),
          [st.r, epst.r], [st.r])
        V(lambda e: e.reciprocal(out=stat.t[:, 2:3], in_=stat.t[:, 1:2]), [st.r], [st.r])
        V(lambda e, x_=x_, xn_=xn_: e.tensor_scalar(out=xn_.t[:], in0=x_.t[:], scalar1=stat.t[:, 2:3], scalar2=None,
                                                    op0=ALU.mult), [x_.r, st.r], [xn_.r])

    def normB(l, mi, s, which):
        row = 1 if mi == 0 else 0
        ABl = ABs[l][0]
        ai = 0 if which == 1 else 2
        xn_ = xn[s % 2]
        for kc in range(KC):
            P.op("tensor", lambda e, kc=kc, xn_=xn_: e.transpose(PST.t[:, kc * 128:(kc + 1) * 128],
                                                                xn_.t[:, kc * 128:(kc + 1) * 128], ident.t[:]),
                 [xn_.r, ident.r], [PST.r], is_mm=True)
        for kc in range(KC):
            V(lambda e, kc=kc, s=s: e.tensor_scalar(
                out=hxT.t[:, kc, s * 128:(s + 1) * 128], in0=PST.t[:, kc * 128:(kc + 1) * 128],
                scalar1=ABl.t[:, ai, kc, row:row + 1], scalar2=ABl.t[:, ai + 1, kc, row:row + 1],
                op0=ALU.mult, op1=ALU.add), [PST.r, ABl.r], [hxT.r])

    def norm_to_hxT(l, mi, which, src=None):
        for s in range(MT[mi][1] // 128):
            normA(mi, s)
            normB(l, mi, s, which)

    def G_or_A(fn, reads, writes):
        return V(fn, reads, writes)

    def load_G(l, gi, row):
        dma("sync", Gt.t[:], GROW.t[l, gi, row, :].partition_broadcast(128), [GROW.r], [Gt.r])

    rctr = [0]
    fence_t = sb("fence_t", [128, 1])

    def fence(res_list):
        G(lambda e: e.memset(fence_t.t[:], 0.0), [], [fence_t.r] + list(res_list))

    def resid_update(mi, s, j, pt, rows_aps=None, xres=None):
        if xres is None:
            xres = [XR.rs[mi]]
        if rows_aps is None:
            r0, nt = MT[mi]
            rows_aps = [(0, 128, XR.t[r0 + s * 128:r0 + (s + 1) * 128, j * 512:(j + 1) * 512])]
        t2 = t2ring[rctr[0] % len(t2ring)]; rctr[0] += 1
        V(lambda e, t2=t2: e.tensor_tensor(out=t2.t[:], in0=pt.t[:], in1=Gt.t[:, j * 512:(j + 1) * 512], op=ALU.mult),
          [pt.r, Gt.r], [t2.r])
        last = None
        for (p0, pn, ap) in rows_aps:
            last = P.op("gpsimd", lambda e, ap=ap, p0=p0, pn=pn, t2=t2: e.dma_start(out=ap, in_=t2.t[p0:p0 + pn, :], accum_op=ALU.add),
                        [t2.r], xres, is_dma=True)
        return last

    def proj_out_resid(srcT, nt, wdram, mi=None, rows_fn=None, xres=None):
        if xres is None:
            xres = [XR.rs[mi]]
        nsub = nt // 128
        Y = []
        for s_ in range(nsub):
            ap = BIG.t[:, 32 + 8 * s_:40 + 8 * s_, :].rearrange("p a b -> p (a b)").bitcast(F32)
            Y.append((ap, BIG.rs[32 + 8 * s_:40 + 8 * s_]))
        for j in range(D // 512):
            wt = wnext()
            dma("sync", wt.t[:], wdram.t.ap().rearrange("(kc p) c -> p kc c", p=128)[:, :, j * 512:(j + 1) * 512],
                [wdram.r], [wt.r])
            for s_ in range(nsub):
                pt = PSF[(s_ + j) % 2]
                for kc in range(KC):
                    mm(pt, pt.t[:], srcT.t[:, kc, s_ * 128:(s_ + 1) * 128], wt.t[:, kc, :], [srcT.r, wt.r], kc == 0, kc == KC - 1)
                yap, yrs = Y[s_]
                V(lambda e, pt=pt, yap=yap, j=j: e.tensor_tensor(out=yap[:, j * 512:(j + 1) * 512], in0=pt.t[:],
                                                                 in1=Gt.t[:, j * 512:(j + 1) * 512], op=ALU.mult),
                  [pt.r, Gt.r], yrs)
        for s_ in range(nsub):
            yap, yrs = Y[s_]
            if rows_fn is None:
                r0 = MT[mi][0]
                rows = [(0, 128, XR.t[r0 + s_ * 128:r0 + (s_ + 1) * 128, :])]
            else:
                rows = rows_fn(s_)
            for (p0, pn, ap) in rows:
                P.op("gpsimd", lambda e, ap=ap, p0=p0, pn=pn, yap=yap: e.dma_start(out=ap, in_=yap[p0:p0 + pn, :], accum_op=ALU.add),
                     yrs, xres, is_dma=True)

    def mlp_phase(l, final=False):
        finals = []
        if final:
            fg = sb("fing", [128, D])
            dma("sync", fg.t[:], fin.ap().partition_broadcast(128), [ext], [fg.r])
        tiles = [mi for mi in range(NMT) if not (final and mi == 0)]
        for tidx, mi in enumerate(tiles):
            r0, nt = MT[mi]
            if mi <= 1:
                load_G(l, 1, 1 if mi == 0 else 0)
            if tidx == 0:
                norm_to_hxT(l, mi, 2)
            nxt = tiles[tidx + 1] if tidx + 1 < len(tiles) else None
            for fb in range(DFF // 512):
                wt = wnext()
                dma("sync", wt.t[:], w1b[l].t.ap().rearrange("(kc p) f -> p kc f", p=128)[:, :, fb * 512:(fb + 1) * 512],
                    [w1b[l].r], [wt.r])
                for j in range(4):
                    fc = fb * 4 + j
                    pt = PSF[fc % 2]
                    for kc in range(KC):
                        mm(pt, pt.t[:, :nt], wt.t[:, kc, j * 128:(j + 1) * 128], hxT.t[:, kc, :nt], [wt.r, hxT.r], kc == 0, kc == KC - 1)
                    tq = tmp[fc % 2]
                    A(lambda e, pt=pt, tq=tq, nt=nt: e.activation(out=tq.t[:, :nt], in_=pt.t[:, :nt], func=AF.Relu), [pt.r], [tq.r])
                    V(lambda e, tq=tq, fc=fc, nt=nt: e.tensor_tensor(out=BIG.t[:, fc, :nt], in0=tq.t[:, :nt], in1=tq.t[:, :nt], op=ALU.mult),
                      [tq.r], [BIG.rs[fc]])
            nsub = nt // 128
            slot = 0
            hook_slots = {1: 0, 2: 1, 3: 2, 5: 3}
            bdone = []
            if nxt is not None:
                nsn = MT[nxt][1] // 128
                normA(nxt, 0)
                if nsn > 1:
                    normA(nxt, 1)
            for j in range(D // 512):
                for fg_ in range(FC // 16):
                    if nxt is not None and slot in hook_slots and hook_slots[slot] < nsn:
                        k = hook_slots[slot]
                        normB(l, nxt, k, 2)
                        bdone.append(k)
                        if k + 2 < nsn:
                            normA(nxt, k + 2)
                    slot += 1
                    wt = wnext()
                    dma("sync", wt.t[:], w2b[l].t.ap().rearrange("(fc p) d -> p fc d", p=128)[:, fg_ * 16:(fg_ + 1) * 16, j * 512:(j + 1) * 512],
                        [w2b[l].r], [wt.r])
                    for f in range(16):
                        fc = fg_ * 16 + f
                        for s in range(nsub):
                            pt = PSF[2 + s]
                            mm(pt, pt.t[:], BIG.t[:, fc, s * 128:(s + 1) * 128], wt.t[:, f, :], [BIG.rs[fc], wt.r], fc == 0, fc == FC - 1)
                for s in range(nsub):
                    pt = PSF[2 + s]
                    resid_update(mi, s, j, pt)
            if nxt is not None:
                for k in range(nsn):
                    if k not in bdone:
                        normB(l, nxt, k, 2)
                        if k + 2 < nsn:
                            normA(nxt, k + 2)
            if final:
                for s in range(nsub):
                    x_ = xt[s % 2]; xj = xn[s % 2]
                    dma("sync", x_.t[:], XR.t[r0 + s * 128:r0 + (s + 1) * 128, :], [XR.rs[mi]], [x_.r])
                    A(lambda e, x_=x_, xj=xj: e.activation(out=xj.t[:], in_=x_.t[:], func=AF.Square, accum_out=stat.t[:, 4:5]),
                      [x_.r], [xj.r, stat.r])
                    A(lambda e: e.activation(out=stat.t[:, 5:6], in_=stat.t[:, 4:5], func=AF.Sqrt, bias=epst.t[:, 0:1], scale=1.0 / D),
                      [stat.r, epst.r], [stat.r])
                    V(lambda e: e.reciprocal(out=stat.t[:, 6:7], in_=stat.t[:, 5:6]), [stat.r], [stat.r])
                    V(lambda e, x_=x_: e.scalar_tensor_tensor(out=x_.t[:], in0=x_.t[:], scalar=stat.t[:, 6:7], in1=fg.t[:],
                                                              op0=ALU.mult, op1=ALU.mult), [x_.r, stat.r, fg.r], [x_.r])
                    finals.append(dma("gpsimd", out[r0 - L + s * 128:r0 - L + (s + 1) * 128, :], x_.t[:], [x_.r], []))
        return finals


    if NO > 0:
        cch = sb("cch", [128, 4, 512], BF16); nsch = sb("nsch", [128, 4, 512], BF16)
        dma("gpsimd", cch.t[:], tb["t_cch"].ap().rearrange("(kc p) m -> p kc m", p=128), [ext], [cch.r])
        dma("gpsimd", nsch.t[:], tb["t_nsch"].ap().rearrange("(kc p) m -> p kc m", p=128), [ext], [nsch.r])
        c128 = sb("c128", [128, 128], BF16); s128 = sb("s128", [128, 128], BF16); ns128 = sb("ns128", [128, 128], BF16)
        dma("gpsimd", c128.t[:], tb["t_c128"][:, :], [ext], [c128.r])
        dma("gpsimd", s128.t[:], tb["t_s128"][:, :], [ext], [s128.r])
        dma("gpsimd", ns128.t[:], tb["t_ns128"][:, :], [ext], [ns128.r])
        twr = sb("twr", [128, N2]); twi = sb("twi", [128, N2])
        dma("sync", twr.t[:], tb["t_twr"][:, :], [ext], [twr.r])
        dma("sync", twi.t[:], tb["t_twi"][:, :], [ext], [twi.r])
        cs = sb("cs", [2 * N2, N2], BF16)
        dma("gpsimd", cs.t[:], tb["t_cs"][:, :], [ext], [cs.r])
        cl = sb("cl", [128, L // 128, L], BF16); sl = sb("sl", [128, L // 128, L], BF16)
        dma("gpsimd", cl.t[:], tb["t_cl"].ap().rearrange("(a p) k -> p a k", p=128), [ext], [cl.r])
        dma("gpsimd", sl.t[:], tb["t_sl"].ap().rearrange("(a p) k -> p a k", p=128), [ext], [sl.r])

    def zreg(b):
        ap = BIG.t[:, 8 * b:8 * b + 8, :].rearrange("p (r c) f -> p r (c f)", r=2)
        return ap, BIG.rs[8 * b:8 * b + 8]

    def evac(i, out_ap, in_ap, reads, writes):
        if i % 2 == 0:
            V(lambda e: e.tensor_copy(out=out_ap, in_=in_ap), reads, writes)
        else:
            A(lambda e: e.activation(out=out_ap, in_=in_ap, func=AF.Copy), reads, writes)

    def odd_mixer(l, i, last):
        KS = 128 // N2
        xlat = XR.rs[1:]
        for mi, (r0, nt) in enumerate(MT):
            if last and mi == 0:
                continue
            norm_to_hxT(l, mi, 1)
            for s in range(nt // 128):
                zap, zrs = zreg(s % 2)
                for ri in range(2):
                    tabl = cch if ri == 0 else nsch
                    for g in range(4):
                        pt = PSF[(ri * 4 + g) % 2]
                        for kc in range(4):
                            mm(pt, pt.t[:], hxT.t[:, 4 * g + kc, s * 128:(s + 1) * 128], tabl.t[:, kc, :], [hxT.r, tabl.r], kc == 0, kc == 3)
                        evac(g, zap[:, ri, g * 512:(g + 1) * 512], pt.t[:], [pt.r], zrs)
                dma("gpsimd", Z0.t[r0 + s * 128:r0 + (s + 1) * 128, :, :], zap, zrs, [Z0.rs[mi]])
        z0lat = Z0.t[L:L + N, :, :].rearrange("(n1 n2) r d -> n1 n2 r d", n2=N2)
        for n2 in range(N2):
            zin, zirs = zreg(n2 % 2)
            zo, zors = zreg(2 + n2 % 2)
            dma("sync", zin, z0lat[:, n2, :, :], Z0.rs[1:], zirs)
            for cb in range(4):
                sl_ = slice(cb * 512, (cb + 1) * 512)
                yr = PSF[0 + 2 * (cb % 2)]; yi = PSF[1 + 2 * (cb % 2)]
                mm(yr, yr.t[:], c128.t[:], zin[:, 0, sl_], [c128.r] + zirs, True, False)
                mm(yr, yr.t[:], s128.t[:], zin[:, 1, sl_], [s128.r] + zirs, False, True)
                mm(yi, yi.t[:], c128.t[:], zin[:, 1, sl_], [c128.r] + zirs, True, False)
                mm(yi, yi.t[:], ns128.t[:], zin[:, 0, sl_], [ns128.r] + zirs, False, True)
                t1 = tmp[0]; t2 = tmp[1]
                A(lambda e, yi=yi, n2=n2: e.activation(out=t1.t[:], in_=yi.t[:], func=AF.Copy, scale=twi.t[:, n2:n2 + 1]),
                  [yi.r, twi.r], [t1.r])
                A(lambda e, yi=yi, n2=n2: e.activation(out=t2.t[:], in_=yi.t[:], func=AF.Copy, scale=twr.t[:, n2:n2 + 1]),
                  [yi.r, twr.r], [t2.r])
                V(lambda e, yr=yr, n2=n2, sl_=sl_, zo=zo: e.scalar_tensor_tensor(
                    out=zo[:, 0, sl_], in0=yr.t[:], scalar=twr.t[:, n2:n2 + 1], in1=t1.t[:], op0=ALU.mult, op1=ALU.subtract),
                  [yr.r, twr.r, t1.r], zors)
                V(lambda e, yr=yr, n2=n2, sl_=sl_, zo=zo: e.scalar_tensor_tensor(
                    out=zo[:, 1, sl_], in0=yr.t[:], scalar=twi.t[:, n2:n2 + 1], in1=t2.t[:], op0=ALU.mult, op1=ALU.add),
                  [yr.r, twi.r, t2.r], zors)
            for ri in range(2):
                dma("gpsimd", Z1.t[ri, n2, :, :], zo[:, ri, :], zors, [Z1.rs[n2]])
        load_G(l, 0, 0)
        z1v = Z1.t.ap().rearrange("r n k d -> (r n) k d")
        xrl = XR.t[L:L + N, :].rearrange("(k2 k1) d -> k1 k2 d", k1=128)
        for m in range(128 // (4 * KS)):
            for s in range(4):
                for k1l in range(KS):
                    k1 = (4 * m + s) * KS + k1l
                    z3 = xn[k1l % 2]
                    dma("sync", z3.t[0:2 * N2, :], z1v[:, k1, :], Z1.rs, [z3.r])
                    for kc in range(KC):
                        pt = PSF[2 + kc // 4]
                        c0 = (kc % 4) * 128 + k1l * N2
                        mm(pt, pt.t[:, c0:c0 + N2], z3.t[0:2 * N2, kc * 128:(kc + 1) * 128], cs.t[:, :], [z3.r, cs.r], True, True)
                for b in range(4):
                    pt = PSF[2 + b]
                    evac(b, hxT.t[:, 4 * b:4 * b + 4, s * 128:(s + 1) * 128], pt.t[:].rearrange("p (c t) -> p c t", c=4), [pt.r], [hxT.r])

            def rows_fn(s, m=m):
                return [(k1l * N2, N2, xrl[(4 * m + s) * KS + k1l, :, :]) for k1l in range(KS)]
            proj_out_resid(hxT, 512, woob[i], None, rows_fn, xlat)
        if not last:
            load_G(l, 0, 1)
            zc = []
            for a in range(L // 128):
                zap, zrs = zreg(a)
                dma("sync", zap, Z0.t[a * 128:(a + 1) * 128, :, :], [Z0.rs[0]], zrs)
                zc.append((zap, zrs))
            for kc in range(KC):
                pt = PSF[kc % 2]
                n = 0
                for a in range(L // 128):
                    zap, zrs = zc[a]
                    for ri, tabl in ((0, cl), (1, sl)):
                        mm(pt, pt.t[:, :L], zap[:, ri, kc * 128:(kc + 1) * 128], tabl.t[:, a, :], zrs + [tabl.r], n == 0, n == 2 * (L // 128) - 1)
                        n += 1
                evac(kc, hxT.t[:, kc, 0:L], pt.t[:, :L], [pt.r], [hxT.r])
            proj_out_resid(hxT, L, woob[i], 0)

    wpool = sb("wpool", [128, 4, 2, 256], BF16)
    wrot = sb("wrot", [128, KC, 64], BF16)
    smallp = sb("smallp", [128, 16])
    qm = sb("qm", [128, 8]); km = sb("km", [128, 8]); nb = sb("nb", [128, 8])
    onesf = sb("onesf", [128, 128])
    G(lambda e: e.memset(onesf.t[:], 1.0), [], [onesf.r])

    def bv(c0, n):
        return BIG.t[:, c0:c0 + n, :], BIG.rs[c0:c0 + n]

    def bvf(c0, n):
        return BIG.t[:, c0:c0 + n, :].rearrange("p a b -> p (a b)").bitcast(F32), BIG.rs[c0:c0 + n]

    def even_mixer(l, i, last):
        wuq, wuq_r = bv(0, 12); wuq = wuq.rearrange("p a b -> p (a b)").rearrange("p (k c) -> p k c", k=4)
        wuqrot, wuqrot_r = bv(12, 4)
        wukv, wukv_r = bv(16, 8); wukv = wukv.rearrange("p a b -> p (a b)").rearrange("p (k c) -> p k c", k=2)
        wv, wv_r = bv(24, 4); wv = wv.rearrange("p a b -> p (a b)").rearrange("p (k c) -> p k c", k=2)
        dma("gpsimd", wuq, w_uq[i].rearrange("(k p) c -> p k c", p=128), [ext], wuq_r)
        dma("gpsimd", wukv, w_ukv[i].rearrange("(k p) c -> p k c", p=128), [ext], wukv_r)
        dma("gpsimd", wpool.t[:], w_pool[i].rearrange("g (k p) d -> p g k d", p=128), [ext], [wpool.r])
        dma("sync", smallp.t[:, 0:4], qnT[i, :, :], [ext], [smallp.r])
        dma("sync", smallp.t[:, 4:6], kvnT[i, :, :], [ext], [smallp.r])
        dma("sync", smallp.t[:, 6:14], pscT[i, :, :], [ext], [smallp.r])
        for h in range(H):
            V(lambda e, h=h: e.tensor_copy(out=wv[:, :, h * 128:(h + 1) * 128], in_=wukv[:, :, h * 256 + 128:h * 256 + 256]),
              wukv_r, wv_r)
            for a in range(2):
                src0 = h * 192 + 128 + a * 32
                V(lambda e, h=h, a=a, src0=src0: e.tensor_scalar(out=wuqrot[:, :, h * 64 + a * 32:h * 64 + a * 32 + 16],
                                                               in0=wuq[:, :, src0 + 16:src0 + 32], scalar1=-1.0, scalar2=None, op0=ALU.mult),
                  wuq_r, wuqrot_r)
                V(lambda e, h=h, a=a, src0=src0: e.tensor_copy(out=wuqrot[:, :, h * 64 + a * 32 + 16:h * 64 + a * 32 + 32],
                                                             in_=wuq[:, :, src0:src0 + 16]), wuq_r, wuqrot_r)
        wb0 = wnext()
        winv = winb[i].t.ap().rearrange("(kc p) c -> p kc c", p=128)
        dma("sync", wb0.t[:, :, 0:320], winv[:, :, 512:832], [winb[i].r], [wb0.r])
        for a in range(2):
            V(lambda e, a=a: e.tensor_scalar(out=wrot.t[:, :, a * 32:a * 32 + 16], in0=wb0.t[:, :, 256 + a * 32 + 16:256 + a * 32 + 32],
                                            scalar1=-1.0, scalar2=None, op0=ALU.mult), [wb0.r], [wrot.r])
            V(lambda e, a=a: e.tensor_copy(out=wrot.t[:, :, a * 32 + 16:a * 32 + 32], in_=wb0.t[:, :, 256 + a * 32:256 + a * 32 + 16]),
              [wb0.r], [wrot.r])
        G(lambda e: e.memset(qm.t[:], 0.0), [], [qm.r])
        G(lambda e: e.memset(km.t[:], 0.0), [], [km.r])
        qlat, qlat_r = bvf(28, 8); qlat = qlat.rearrange("p (k t) -> p k t", k=4)
        kvlat, kvlat_r = bvf(36, 4); kvlat = kvlat.rearrange("p (k t) -> p k t", k=2)
        qn_, qn_r = bv(40, 4)
        kvn, kvn_r = bv(44, 2)
        sqb, sqb_r = bv(46, 2)
        cst, cst_r = bvf(48, 2); snt, snt_r = bvf(50, 2)
        stg = [bv(52 + k, 1) for k in range(8)]
        sctr = [0]

        def stage():
            sctr[0] += 1
            ap, rs = stg[sctr[0] % 8]
            return ap[:, 0, :], rs

        def rms_lat(lat, lat_r, nch, ncol0, out, out_r, nt, dim):
            pss = PSF[2]
            for c in range(nch):
                A(lambda e, c=c: e.activation(out=sqb[:, 0, :nt], in_=lat[:, c, :nt], func=AF.Square), lat_r, sqb_r)
                mm(pss, pss.t[:, :nt], ones.t[:], sqb[:, 0, :nt], [ones.r] + sqb_r, c == 0, c == nch - 1)
            A(lambda e: e.activation(out=tmp[0].t[:, :nt], in_=pss.t[:, :nt], func=AF.Sqrt, bias=epst.t[:, 0:1], scale=1.0 / dim),
              [pss.r, epst.r], [tmp[0].r])
            V(lambda e: e.reciprocal(out=tmp[1].t[:, :nt], in_=tmp[0].t[:, :nt]), [tmp[0].r], [tmp[1].r])
            for c in range(nch):
                V(lambda e, c=c: e.scalar_tensor_tensor(out=out[:, c, :nt], in0=lat[:, c, :nt], scalar=smallp.t[:, ncol0 + c:ncol0 + c + 1],
                                                        in1=tmp[1].t[:, :nt], op0=ALU.mult, op1=ALU.mult),
                  lat_r + [smallp.r, tmp[1].r], out_r)

        def rope_out(pr, prot, nt, is_lat, dst_ap, dst_res):
            st, st_r = stage()
            if is_lat:
                V(lambda e: e.tensor_tensor(out=tmp[0].t[0:64, :nt], in0=prot.t[0:64, :nt], in1=snt[0:64, :nt], op=ALU.mult),
                  [prot.r] + snt_r, [tmp[0].r])
                V(lambda e: e.tensor_tensor(out=tmp[1].t[0:64, :nt], in0=pr.t[0:64, :nt], in1=cst[0:64, :nt], op=ALU.mult),
                  [pr.r] + cst_r, [tmp[1].r])
                V(lambda e: e.tensor_tensor(out=st[0:64, :nt], in0=tmp[0].t[0:64, :nt], in1=tmp[1].t[0:64, :nt], op=ALU.add),
                  [tmp[0].r, tmp[1].r], st_r)
            else:
                V(lambda e: e.tensor_copy(out=st[0:64, :nt], in_=pr.t[0:64, :nt]), [pr.r], st_r)
            dma("gpsimd", dst_ap, st[0:64, :nt], st_r + list(dst_res), [])
            return st, st_r

        def upd_max(mx, h, nope, nope_r, rp, rp_r, nt):
            pn = PSF[3]
            A(lambda e: e.activation(out=sqb[:, 0, :nt], in_=nope[:, :nt], func=AF.Square), nope_r, sqb_r)
            A(lambda e: e.activation(out=sqb[0:64, 1, :nt], in_=rp[0:64, :nt], func=AF.Square), rp_r, sqb_r)
            mm(pn, pn.t[:, :nt], ones.t[:], sqb[:, 0, :nt], [ones.r] + sqb_r, True, False)
            mm(pn, pn.t[:, :nt], ones.t[0:64, :], sqb[0:64, 1, :nt], [ones.r] + sqb_r, False, True)
            V(lambda e: e.reduce_max(out=stat.t[:, 7:8], in_=pn.t[:, :nt], axis=AX.X), [pn.r], [stat.r])
            V(lambda e, h=h: e.tensor_max(out=mx.t[:, h:h + 1], in0=mx.t[:, h:h + 1], in1=stat.t[:, 7:8]), [mx.r, stat.r], [mx.r])

        for mi, (r0, nt) in enumerate(MT):
            is_lat = mi > 0
            norm_to_hxT(l, mi, 1)
            if is_lat:
                dma("sync", cst[0:64, :nt], tb["t_cos"][:, r0 - L:r0 - L + nt], [ext], cst_r)
                dma("sync", snt[0:64, :nt], tb["t_sin"][:, r0 - L:r0 - L + nt], [ext], snt_r)
            wA = wnext()
            dma("sync", wA.t[:], winv[:, :, 0:512], [winb[i].r], [wA.r])
            for qc in range(4):
                pt = PSF[qc % 2]
                for kc in range(KC):
                    mm(pt, pt.t[:, :nt], wA.t[:, kc, qc * 128:(qc + 1) * 128], hxT.t[:, kc, :nt], [wA.r, hxT.r], kc == 0, kc == KC - 1)
                A(lambda e, pt=pt, qc=qc, nt=nt: e.activation(out=qlat[:, qc, :nt], in_=pt.t[:, :nt], func=AF.Copy), [pt.r], qlat_r)
            rms_lat(qlat, qlat_r, 4, 0, qn_, qn_r, nt, 512)
            wB = wnext()
            dma("sync", wB.t[:, :, 0:320], winv[:, :, 512:832], [winb[i].r], [wB.r])
            for c in range(2):
                pt = PSF[c % 2]
                for kc in range(KC):
                    mm(pt, pt.t[:, :nt], wB.t[:, kc, c * 128:(c + 1) * 128], hxT.t[:, kc, :nt], [wB.r, hxT.r], kc == 0, kc == KC - 1)
                A(lambda e, pt=pt, c=c, nt=nt: e.activation(out=kvlat[:, c, :nt], in_=pt.t[:, :nt], func=AF.Copy), [pt.r], kvlat_r)
            rms_lat(kvlat, kvlat_r, 2, 4, kvn, kvn_r, nt, 256)
            pr = PSF[0]; prot = PSF[1]
            for kc in range(KC):
                mm(pr, pr.t[0:64, :nt], wB.t[:, kc, 256:320], hxT.t[:, kc, :nt], [wB.r, hxT.r], kc == 0, kc == KC - 1)
            for kc in range(KC):
                mm(prot, prot.t[0:64, :nt], wrot.t[:, kc, :], hxT.t[:, kc, :nt], [wrot.r, hxT.r], kc == 0, kc == KC - 1)
            krs, krs_r = rope_out(pr, prot, nt, is_lat, KRd.t[:, r0:r0 + nt], [KRd.rs[mi]])
            for h in range(H):
                pt = PSF[0]
                for kc in range(2):
                    mm(pt, pt.t[:, :nt], wukv[:, kc, h * 256:h * 256 + 128], kvn[:, kc, :nt], wukv_r + kvn_r, kc == 0, kc == 1)
                st, st_r = stage()
                evac(h, st[:, :nt], pt.t[:, :nt], [pt.r], st_r)
                dma("gpsimd", KN.t[h, :, r0:r0 + nt], st[:, :nt], st_r + [KN.rs[mi]], [])
                upd_max(km, h, st, st_r, krs, krs_r, nt)
                pt = PSF[1]
                for kc in range(4):
                    mm(pt, pt.t[:, :nt], wuq[:, kc, h * 192:h * 192 + 128], qn_[:, kc, :nt], wuq_r + qn_r, kc == 0, kc == 3)
                sq_, sq_r = stage()
                evac(h + 1, sq_[:, :nt], pt.t[:, :nt], [pt.r], sq_r)
                dma("gpsimd", QN.t[h, :, r0:r0 + nt], sq_[:, :nt], sq_r + [QN.rs[mi]], [])
                pr = PSF[4]; prot = PSF[5]
                for kc in range(4):
                    mm(pr, pr.t[0:64, :nt], wuq[:, kc, h * 192 + 128:h * 192 + 192], qn_[:, kc, :nt], wuq_r + qn_r, kc == 0, kc == 3)
                for kc in range(4):
                    mm(prot, prot.t[0:64, :nt], wuqrot[:, kc, h * 64:(h + 1) * 64], qn_[:, kc, :nt], wuqrot_r + qn_r, kc == 0, kc == 3)
                qrs, qrs_r = rope_out(pr, prot, nt, is_lat, QRd.t[h, :, r0:r0 + nt], [QRd.rs[mi]])
                upd_max(qm, h, sq_, sq_r, qrs, qrs_r, nt)
            for s in range(nt // 128):
                for vb in range(2):
                    pt = PSF[vb]
                    for kc in range(2):
                        mm(pt, pt.t[:], kvn[:, kc, s * 128:(s + 1) * 128], wv[:, kc, vb * 512:(vb + 1) * 512], kvn_r + wv_r, kc == 0, kc == 1)
                    st, st_r = stage()
                    evac(vb, st, pt.t[:], [pt.r], st_r)
                    dma("gpsimd", VV.t[r0 + s * 128:r0 + (s + 1) * 128, vb * 512:(vb + 1) * 512], st, st_r + [VV.rs[mi]], [])
            for blk in range(2):
                wC = wnext()
                dma("sync", wC.t[:], winv[:, :, 832 + blk * 512:832 + (blk + 1) * 512], [winb[i].r], [wC.r])
                for c in range(4):
                    pt = PSF[c % 2]
                    for kc in range(KC):
                        mm(pt, pt.t[:, :nt], wC.t[:, kc, c * 128:(c + 1) * 128], hxT.t[:, kc, :nt], [wC.r, hxT.r], kc == 0, kc == KC - 1)
                    tq = tmp[c % 2]
                    evac(c, tq.t[:, :nt], pt.t[:, :nt], [pt.r], [tq.r])
                    uc = blk * 4 + c
                    dma("gpsimd", UU.t[uc * 128:(uc + 1) * 128, r0:r0 + nt], tq.t[:, :nt], [tq.r, UU.rs[mi]], [])
            fence([QN.rs[mi], QRd.rs[mi], KN.rs[mi], KRd.rs[mi], VV.rs[mi], UU.rs[mi]]) if (blk == 1 and c == 3) else None
        V(lambda e: e.tensor_tensor(out=nb.t[:], in0=qm.t[:], in1=km.t[:], op=ALU.mult), [qm.r, km.r], [nb.r])
        A(lambda e: e.activation(out=nb.t[:], in_=nb.t[:], func=AF.Sqrt), [nb.r], [nb.r])
        V(lambda e: e.tensor_scalar(out=nb.t[:], in0=nb.t[:], scalar1=-SCALE, scalar2=None, op0=ALU.mult), [nb.r], [nb.r])

        NKC = T // 128
        TC = (T + 511) // 512
        krt, krt_r = bv(0, TC); krt = krt.rearrange("p a b -> p (a b)")
        knt, knt_r = bv(TC, TC); knt = knt.rearrange("p a b -> p (a b)")
        vt, vt_r = bv(2 * TC, TC); vt = vt.rearrange("p a b -> p (a b)")[:, 0:NKC * 128].rearrange("p (k d) -> p k d", d=128)
        qb_ = [bv(3 * TC + k, 1) for k in range(4)]
        pts = [bv(3 * TC + 4 + k, 1) for k in range(3)]
        ots = [bv(3 * TC + 7 + k, 1) for k in range(2)]
        dma("sync", krt[0:64, 0:T], KRd.t[:, :], KRd.rs, krt_r)
        V(lambda e: e.memset(krt[64:128, :], 0.0), [], krt_r)
        for k in (1, 3):
            V(lambda e, k=k: e.memset(qb_[k][0][64:128, 0, :], 0.0), [], qb_[k][1])
        qblocks = [(r0, nt, (L // 128 if mi == 0 else NKC)) for mi, (r0, nt) in enumerate(MT) if not (last and mi == 0)]
        pc = [0]
        for h in range(H):
            dma("sync", knt[:, 0:T], KN.t[h, :, :], KN.rs, knt_r)
            dma("sync", vt, VV.t[:, h * 128:(h + 1) * 128].rearrange("(k p) d -> p k d", p=128), VV.rs, vt_r)
            steps = [(qi, kc) for qi, (r0, nt, nk) in enumerate(qblocks) for kc in range(nk)]

            def qbufs(qi):
                return qb_[(qi % 2) * 2], qb_[(qi % 2) * 2 + 1]

            def emit_S(idx, h=h):
                qi, kc = steps[idx]
                r0, nt, nk = qblocks[qi]
                mi = qi if not last else qi + 1
                (qa, qa_r), (qr_, qr_r) = qbufs(qi)
                if kc == 0:
                    dma("sync", qa[:, 0, :nt], QN.t[h, :, r0:r0 + nt], [QN.rs[mi]], qa_r)
                    dma("sync", qr_[0:64, 0, :nt], QRd.t[h, :, r0:r0 + nt], [QRd.rs[mi]], qr_r)
                st = SB3[idx % 3]
                mm(st, st.t[:, :nt], knt[:, kc * 128:(kc + 1) * 128], qa[:, 0, :nt], knt_r + qa_r, True, False)
                mm(st, st.t[:, :nt], krt[:, kc * 128:(kc + 1) * 128], qr_[:, 0, :nt], krt_r + qr_r, False, True)

            SB3 = [PSF[0], PSF[1], PSF[5]]
            emit_S(0)
            if len(steps) > 1:
                emit_S(1)
            for idx, (qi, kc) in enumerate(steps):
                r0, nt, nk = qblocks[qi]
                mi = qi if not last else qi + 1
                if idx + 2 < len(steps):
                    emit_S(idx + 2)
                st = SB3[idx % 3]
                po = PSF[2 + qi % 2]; pd = PSF[4]
                acc = accs[qi % 2]
                pT, pT_r = pts[pc[0] % 3]; pc[0] += 1
                A(lambda e, st=st, pT=pT, h=h, nt=nt: e.activation(out=pT[:, 0, :nt], in_=st.t[:, :nt], func=AF.Exp,
                                                                   bias=nb.t[:, h:h + 1], scale=SCALE), [st.r, nb.r], pT_r)
                mm(po, po.t[:, :nt], vt[:, kc, :], pT[:, 0, :nt], vt_r + pT_r, kc == 0, kc == nk - 1)
                if kc == 0:
                    V(lambda e, acc=acc, pT=pT, nt=nt: e.tensor_copy(out=acc.t[:, :nt], in_=pT[:, 0, :nt]), pT_r, [acc.r])
                else:
                    V(lambda e, acc=acc, pT=pT, nt=nt: e.tensor_tensor(out=acc.t[:, :nt], in0=acc.t[:, :nt], in1=pT[:, 0, :nt], op=ALU.add),
                      pT_r + [acc.r], [acc.r])
                if kc == nk - 1:
                    mm(pd, pd.t[:, :nt], onesf.t[:], acc.t[:, :nt], [onesf.r, acc.r], True, True)
                    rd = tmp[qi % 2]
                    V(lambda e, pd=pd, rd=rd, nt=nt: e.reciprocal(out=rd.t[:, :nt], in_=pd.t[:, :nt]), [pd.r], [rd.r])
                    ot, ot_r = ots[qi % 2]
                    V(lambda e, po=po, rd=rd, ot=ot, nt=nt: e.tensor_tensor(out=ot[:, 0, :nt], in0=po.t[:, :nt], in1=rd.t[:, :nt], op=ALU.mult),
                      [po.r, rd.r], ot_r)
                    dma("gpsimd", AT.t[h * 128:(h + 1) * 128, r0:r0 + nt], ot[:, 0, :nt], ot_r, [AT.rs[mi]])

        ub = [bvf(0 + 5 * k, 5) for k in range(3)]
        icv, icv_r = bvf(16, 2)
        plT, plT_r = bv(20, 2)
        atv = AT.t.ap().rearrange("(c p) t -> p c t", p=128)
        for mi, (r0, nt) in enumerate(MT):
            if last and mi == 0:
                continue
            seg0, seg1 = (0, L) if mi == 0 else (L, T)
            if mi <= 1:
                load_G(l, 0, 1 if mi == 0 else 0)
            dma("sync", hxT.t[:, 0:8, :nt], atv[:, :, r0:r0 + nt], [AT.rs[mi]], [hxT.r])
            W_ = nt + 16
            for g in range(4):
                U_, U_r = ub[0]; U_ = U_[:, 0:2 * W_].rearrange("p (c w) -> p c w", c=2)
                S1, S1_r = ub[1]; S1 = S1[:, 0:2 * W_].rearrange("p (c w) -> p c w", c=2)
                S2, S2_r = ub[2]; S2 = S2[:, 0:2 * W_].rearrange("p (c w) -> p c w", c=2)
                V(lambda e, U_=U_: e.memset(U_, 0.0), [], U_r)
                lo = max(seg0, r0 - 8); hi = min(seg1, r0 + nt + 8)
                ures = [UU.rs[k] for k in range(NMT) if MT[k][0] < hi and MT[k][0] + MT[k][1] > lo]
                for c in range(2):
                    dma("sync", U_[:, c, lo - (r0 - 8):hi - (r0 - 8)], UU.t[(2 * g + c) * 128:(2 * g + c + 1) * 128, lo:hi], ures, U_r)
                dma("sync", icv[:, :nt], tb["t_invc"][g, r0:r0 + nt].partition_broadcast(128), [ext], icv_r)
                V(lambda e, U_=U_, S1=S1, W_=W_: e.tensor_tensor(out=S1[:, :, 1:W_], in0=U_[:, :, 0:W_ - 1], in1=U_[:, :, 1:W_], op=ALU.add), U_r, S1_r)
                cur, cur_r = S1, S1_r
                oth, oth_r = S2, S2_r
                sh = 1
                for lev in range(g):
                    a = 2 * sh
                    V(lambda e, cur=cur, oth=oth, a=a, sh=sh, W_=W_: e.tensor_tensor(out=oth[:, :, a:W_ - a], in0=cur[:, :, a - sh:W_ - a - sh],
                                                                             in1=cur[:, :, a + sh:W_ - a + sh], op=ALU.add), cur_r, oth_r)
                    cur, cur_r, oth, oth_r = oth, oth_r, cur, cur_r
                    sh *= 2
                for c in range(2):
                    V(lambda e, cur=cur, c=c, nt=nt: e.tensor_tensor(out=tmp[c].t[:, :nt], in0=cur[:, c, 8:8 + nt], in1=icv[:, :nt], op=ALU.mult),
                      cur_r + icv_r, [tmp[c].r])
                    V(lambda e, c=c, U_=U_, nt=nt: e.tensor_tensor(out=plT[:, c, :nt], in0=tmp[c].t[:, :nt], in1=U_[:, c, 8:8 + nt], op=ALU.subtract),
                      [tmp[c].r] + U_r, plT_r)
                for dc in range(2):
                    pt = PSF[dc]
                    for kc in range(2):
                        mm(pt, pt.t[:, :nt], wpool.t[:, g, kc, dc * 128:(dc + 1) * 128], plT[:, kc, :nt], [wpool.r] + plT_r, kc == 0, kc == 1)
                    V(lambda e, pt=pt, g=g, dc=dc, nt=nt: e.tensor_scalar(out=hxT.t[:, 8 + 2 * g + dc, :nt], in0=pt.t[:, :nt],
                                                                  scalar1=smallp.t[:, 6 + 2 * g + dc:7 + 2 * g + dc], scalar2=None, op0=ALU.mult),
                      [pt.r, smallp.r], [hxT.r])
            proj_out_resid(hxT, nt, woeb[i], mi)


    finals = []
    for l in range(depth):
        last = l == depth - 1
        even = l % 2 == 0
        i = l // 2
        if even and cfg_mixers[0]:
            even_mixer(l, i, last)
        if (not even) and cfg_mixers[1]:
            odd_mixer(l, i, last)
        finals = mlp_phase(l, final=last)
    nops, nwait = P.finish(finals)
    print('sbuf bytes remaining', nc.sbuf_bytes_remaining)
    return nc, nops, nwait


def prep_core_inputs(cfg, inp, b, tables):
    KC = cfg.KC
    depth = cfg.depth

    def colT(v, n):
        v = np.asarray(v, np.float32)
        return np.ascontiguousarray(np.swapaxes(v.reshape(v.shape[:-1] + (n, 128)), -1, -2))
    m = {}
    m["x"] = np.ascontiguousarray(inp["x"][b], dtype=np.float32)
    m["ctx"] = np.ascontiguousarray(inp["ctx"][b], dtype=np.float32)
    c2 = np.stack([np.asarray(inp["c"][b], np.float32), np.asarray(inp["c_ctx"], np.float32)])
    m["c2T"] = np.ascontiguousarray(c2.reshape(2, KC, 128).transpose(2, 1, 0))
    m["w_mod"] = np.asarray(inp["w_mod"], np.float32)
    bm = np.asarray(inp["b_mod"], np.float32)
    D = cfg.D
    m["b_modT"] = colT(bm, 6 * KC)
    m["b_modg"] = np.ascontiguousarray(np.stack([bm[:, 2 * D:3 * D], bm[:, 5 * D:6 * D]], axis=1))
    m["norm1T"] = colT(inp["norm1"], KC)
    m["norm2T"] = colT(inp["norm2"], KC)
    m["w_in"] = np.asarray(inp["w_in"], np.float32)
    m["q_normT"] = colT(inp["q_norm"], 4)
    m["w_uq"] = np.asarray(inp["w_uq"], np.float32)
    m["kv_normT"] = colT(inp["kv_norm"], 2)
    m["w_ukv"] = np.asarray(inp["w_ukv"], np.float32)
    m["w_pool"] = np.asarray(inp["w_pool"], np.float32)
    m["pool_scaleT"] = colT(inp["pool_scale"], 8)
    m["w_out_even"] = np.asarray(inp["w_out_even"], np.float32)
    m["w_out_odd"] = np.asarray(inp["w_out_odd"], np.float32)
    m["w_mlp1"] = np.asarray(inp["w_mlp1"], np.float32)
    m["w_mlp2"] = np.asarray(inp["w_mlp2"], np.float32)
    m["final_norm"] = np.asarray(inp["final_norm"], np.float32)
    m.update(tables)
    return m


_CACHE = {}


def kernel(**inputs):
    cfg = Cfg()
    if "prog" not in _CACHE:
        _CACHE["prog"] = build(cfg)[0]
        _CACHE["tables"] = host_tables(cfg)
    nc = _CACHE["prog"]
    tables = _CACHE["tables"]
    inp = {k: np.asarray(v) for k, v in inputs.items()}
    B = inp["x"].shape[0]
    maps = [prep_core_inputs(cfg, inp, b, tables) for b in range(B)]
    in_maps = [maps[c % B] for c in range(8)]
    res = run_bass_kernel_spmd(nc, in_maps, core_ids=list(range(8)))
    out = np.stack([np.asarray(res.results[b]["out"], dtype=np.float32) for b in range(B)], axis=0)
    return out
```
